# Optimizing a Trainium2 kernel written in Bass

```python
import jax, jax.numpy as jnp
from jax import lax
import numpy as np

D_MODEL = 1024
BATCH = 8
SEQ = 4096
DEPTH = 2
DEC_BATCH = 32
DEC_SEQ = 1
PAST_LEN = 16384
PAGE_SIZE = 128

N_AB = (DEPTH + 1) // 2
N_CD = DEPTH // 2
POOL_WIDTH = D_MODEL // 2
POOL_WINDOWS = (2, 4, 8, 16)
POOL_GROUPS = len(POOL_WINDOWS)
POOL_GROUP_W = POOL_WIDTH // POOL_GROUPS
POOL_BUF = max(POOL_WINDOWS) - 1
SB_WIDTH = D_MODEL // 2
SB_HEAD_DIM = 64
SB_HEADS = SB_WIDTH // SB_HEAD_DIM
SB_BIAS_INIT = -8.0
Q_BLOCK = 128
GM_WIDTH = D_MODEL // 2
GM_GROUPS = 4
GM_GROUP_W = GM_WIDTH // GM_GROUPS
CHUNK = 128
SC_WIDTH = D_MODEL // 2
CONV_W = 3
AB_IN = POOL_WIDTH + 3 * SB_WIDTH
AB_OUT = POOL_WIDTH + SB_WIDTH
CD_IN = 2 * GM_WIDTH + 3 * SC_WIDTH
CD_OUT = GM_WIDTH + SC_WIDTH
D_FF = -(-(8 * D_MODEL) // (3 * 256)) * 256
RMS_EPS = 1e-6

kernel_name = "hybrid_pool_stickbreak_chunkmlp_shortconv_step"


def rmsnorm(x, g):
    xf = x.astype(jnp.float32)
    y = xf * lax.rsqrt(jnp.mean(xf * xf, axis=-1, keepdims=True) + RMS_EPS)
    return (y * g.astype(jnp.float32)).astype(x.dtype)


def swiglu(h, w_gate, w_up, w_down):
    return (jax.nn.silu(h @ w_gate) * (h @ w_up)) @ w_down


def pool_mix(u, prefix, pos0, w_pool, pool_scale):
    B, S, _ = u.shape
    P = prefix.shape[1]
    ext = jnp.concatenate([prefix.astype(u.dtype), u], axis=1)
    cs = jnp.pad(jnp.cumsum(ext.astype(jnp.float32), axis=1), ((0, 0), (1, 0), (0, 0)))
    end = cs[:, P + 1:]
    start = jnp.concatenate(
        [cs[:, P + 1 - w:P + 1 - w + S, g * POOL_GROUP_W:(g + 1) * POOL_GROUP_W]
         for g, w in enumerate(POOL_WINDOWS)], axis=-1)
    pos = pos0 + jnp.arange(S)
    cnt = jnp.minimum(pos[:, None] + 1, jnp.array(POOL_WINDOWS)[None, :]).astype(jnp.float32)
    mean = (end - start).reshape(B, S, POOL_GROUPS, POOL_GROUP_W) / cnt[None, :, :, None]
    d = (mean - u.reshape(B, S, POOL_GROUPS, POOL_GROUP_W).astype(jnp.float32)).astype(u.dtype)
    y = jnp.einsum('bsgc,gcd->bsgd', d, w_pool).reshape(B, S, POOL_WIDTH) * pool_scale
    return y, ext[:, -P:]


def sb_attention(q, k, v, q_pos, k_pos, sb_bias):
    B, Sq, H, Dh = q.shape
    blk = Q_BLOCK if Sq % Q_BLOCK == 0 else Sq
    nb = Sq // blk
    scale = Dh ** -0.5
    qb = q.reshape(B, nb, blk, H, Dh).swapaxes(0, 1)
    pb = q_pos.reshape(nb, blk)
    bias = sb_bias.astype(jnp.float32)[None, :, None, None]

    def block(args):
        qi, pi = args
        z = jnp.einsum('bqhd,bkhd->bhqk', qi, k).astype(jnp.float32) * scale + bias
        mask = k_pos[None, :] < pi[:, None]
        log_keep = jnp.where(mask, jax.nn.log_sigmoid(-z), 0.0)
        later = lax.cumsum(log_keep, axis=3, reverse=True) - log_keep
        w = jnp.where(mask, jnp.exp(jax.nn.log_sigmoid(z) + later), 0.0)
        return jnp.einsum('bhqk,bkhd->bqhd', w.astype(v.dtype), v)

    out = lax.map(block, (qb, pb))
    return out.swapaxes(0, 1).reshape(B, Sq, H, Dh)


def chunk_spatial_gate(u, v, w_s, b_s):
    B, S, _ = v.shape
    pad = (-S) % CHUNK
    nc = (S + pad) // CHUNK
    vp = jnp.pad(v, ((0, 0), (0, pad), (0, 0))).reshape(B, nc, CHUNK, GM_GROUPS, GM_GROUP_W)
    tri = jnp.tril(jnp.ones((CHUNK, CHUNK), dtype=bool))
    wm = jnp.where(tri[None], w_s, jnp.zeros_like(w_s))
    mixed = jnp.einsum('gts,bcsgd->bctgd', wm, vp) + b_s.T[None, None, :, :, None]
    mixed = mixed.reshape(B, nc * CHUNK, GM_WIDTH)[:, :S]
    return u * mixed


def ab_mixer(h, pool_prefix, k_past, v_past, pos0, w_in, sb_bias, w_pool, pool_scale, w_out):
    B, S, _ = h.shape
    proj = h @ w_in
    u, q, k, v = jnp.split(proj, [POOL_WIDTH, POOL_WIDTH + SB_WIDTH, POOL_WIDTH + 2 * SB_WIDTH], axis=-1)
    q = q.reshape(B, S, SB_HEADS, SB_HEAD_DIM)
    k = k.reshape(B, S, SB_HEADS, SB_HEAD_DIM)
    v = v.reshape(B, S, SB_HEADS, SB_HEAD_DIM)
    pool_out, pool_state = pool_mix(u, pool_prefix, pos0, w_pool, pool_scale)
    if k_past is None:
        k_all, v_all = k, v
    else:
        k_all = jnp.concatenate([k_past, k.astype(k_past.dtype)], axis=1)
        v_all = jnp.concatenate([v_past, v.astype(v_past.dtype)], axis=1)
    q_pos = pos0 + jnp.arange(S)
    k_pos = jnp.arange(k_all.shape[1])
    att = sb_attention(q, k_all, v_all, q_pos, k_pos, sb_bias).reshape(B, S, SB_WIDTH)
    y = jnp.concatenate([pool_out.astype(h.dtype), att.astype(h.dtype)], axis=-1) @ w_out
    return y, k, v, pool_state


def cd_mixer(h, conv_prefix, w_in, w_s, b_s, conv_w, w_out):
    B, S, _ = h.shape
    proj = h @ w_in
    uv, hh, bg, cg = jnp.split(proj, [2 * GM_WIDTH, 2 * GM_WIDTH + SC_WIDTH, 2 * GM_WIDTH + 2 * SC_WIDTH], axis=-1)
    u, v = jnp.split(jax.nn.gelu(uv), 2, axis=-1)
    gm = chunk_spatial_gate(u, v, w_s, b_s)
    z = cg * hh
    ext = jnp.concatenate([conv_prefix.astype(z.dtype), z], axis=1)
    conv = sum(ext[:, j:j + S] * conv_w[j] for j in range(CONV_W))
    sc = bg * conv
    y = jnp.concatenate([gm, sc], axis=-1) @ w_out
    return y, ext[:, -(CONV_W - 1):], v


def setup_inputs(seed: int = 0) -> dict:
    key = jax.random.key(seed)
    ks = jax.random.split(key, 24)
    f32 = jnp.float32
    n_pages = PAST_LEN // PAGE_SIZE
    n_used = DEC_BATCH * n_pages
    n_pool = n_used + (n_used + 3) // 4

    def nrm(k, shape, scale):
        return jax.random.normal(k, shape, f32) * scale

    return {
        'x_prompt': nrm(ks[0], (BATCH, SEQ, D_MODEL), 1.0),
        'x_sample': nrm(ks[1], (DEC_BATCH, DEC_SEQ, D_MODEL), 1.0),
        'cache_k': nrm(ks[2], (N_AB, n_pool, PAGE_SIZE, SB_HEADS, SB_HEAD_DIM), 1.0),
        'cache_v': nrm(ks[3], (N_AB, n_pool, PAGE_SIZE, SB_HEADS, SB_HEAD_DIM), 1.0),
        'state_pool': nrm(ks[4], (N_AB, DEC_BATCH, POOL_BUF, POOL_WIDTH), 1.0),
        'state_conv': nrm(ks[5], (N_CD, DEC_BATCH, CONV_W - 1, SC_WIDTH), 1.0),
        'page_table': jax.random.permutation(ks[6], n_pool)[:n_used].reshape(DEC_BATCH, n_pages).astype(jnp.int32),
        'norm_mix': 1.0 + nrm(ks[7], (DEPTH, D_MODEL), 0.05),
        'norm_ffn': 1.0 + nrm(ks[8], (DEPTH, D_MODEL), 0.05),
        'norm_final': 1.0 + nrm(ks[9], (D_MODEL,), 0.05),
        'ab_w_in': nrm(ks[10], (N_AB, D_MODEL, AB_IN), D_MODEL ** -0.5),
        'ab_sb_bias': SB_BIAS_INIT + nrm(ks[22], (N_AB, SB_HEADS), 0.1),
        'ab_w_pool': nrm(ks[11], (N_AB, POOL_GROUPS, POOL_GROUP_W, POOL_GROUP_W), POOL_GROUP_W ** -0.5),
        'ab_pool_scale': 1.0 + nrm(ks[12], (N_AB, POOL_WIDTH), 0.1),
        'ab_w_out': nrm(ks[13], (N_AB, AB_OUT, D_MODEL), AB_OUT ** -0.5),
        'cd_w_in': nrm(ks[14], (N_CD, D_MODEL, CD_IN), D_MODEL ** -0.5),
        'cd_w_s': nrm(ks[15], (N_CD, GM_GROUPS, CHUNK, CHUNK), CHUNK ** -0.5),
        'cd_b_s': 1.0 + nrm(ks[16], (N_CD, GM_GROUPS, CHUNK), 0.1),
        'cd_conv_w': nrm(ks[17], (N_CD, CONV_W, SC_WIDTH), CONV_W ** -0.5),
        'cd_w_out': nrm(ks[18], (N_CD, CD_OUT, D_MODEL), CD_OUT ** -0.5),
        'ffn_w_gate': nrm(ks[19], (DEPTH, D_MODEL, D_FF), D_MODEL ** -0.5),
        'ffn_w_up': nrm(ks[20], (DEPTH, D_MODEL, D_FF), D_MODEL ** -0.5),
        'ffn_w_down': nrm(ks[21], (DEPTH, D_FF, D_MODEL), D_FF ** -0.5),
    }


def reference(x_prompt, x_sample, cache_k, cache_v, state_pool, state_conv, page_table,
              norm_mix, norm_ffn, norm_final,
              ab_w_in, ab_sb_bias, ab_w_pool, ab_pool_scale, ab_w_out,
              cd_w_in, cd_w_s, cd_b_s, cd_conv_w, cd_w_out,
              ffn_w_gate, ffn_w_up, ffn_w_down):
    B = x_prompt.shape[0]
    DB = x_sample.shape[0]
    past = page_table.shape[1] * cache_k.shape[2]
    yp, ys = x_prompt, x_sample
    kp_l, vp_l, ks_l, vs_l = [], [], [], []
    poolp_l, pools_l, convp_l, convs_l, chv_l = [], [], [], [], []
    for l in range(DEPTH):
        i = l // 2
        hp = rmsnorm(yp, norm_mix[l])
        hs = rmsnorm(ys, norm_mix[l])
        if l % 2 == 0:
            mp, kp, vp, poolp = ab_mixer(hp, jnp.zeros((B, POOL_BUF, POOL_WIDTH), hp.dtype), None, None, 0,
                                         ab_w_in[i], ab_sb_bias[i], ab_w_pool[i], ab_pool_scale[i], ab_w_out[i])
            k_past = cache_k[i][page_table].reshape(DB, past, SB_HEADS, SB_HEAD_DIM)
            v_past = cache_v[i][page_table].reshape(DB, past, SB_HEADS, SB_HEAD_DIM)
            ms, ks, vs, pools = ab_mixer(hs, state_pool[i], k_past, v_past, past,
                                         ab_w_in[i], ab_sb_bias[i], ab_w_pool[i], ab_pool_scale[i], ab_w_out[i])
            kp_l.append(kp); vp_l.append(vp); ks_l.append(ks); vs_l.append(vs)
            poolp_l.append(poolp); pools_l.append(pools)
        else:
            mp, convp, _ = cd_mixer(hp, jnp.zeros((B, CONV_W - 1, SC_WIDTH), hp.dtype),
                                    cd_w_in[i], cd_w_s[i], cd_b_s[i], cd_conv_w[i], cd_w_out[i])
            ms, convs, chv = cd_mixer(hs, state_conv[i],
                                      cd_w_in[i], cd_w_s[i], cd_b_s[i], cd_conv_w[i], cd_w_out[i])
            convp_l.append(convp); convs_l.append(convs); chv_l.append(chv)
        yp = yp + mp
        ys = ys + ms
        yp = yp + swiglu(rmsnorm(yp, norm_ffn[l]), ffn_w_gate[l], ffn_w_up[l], ffn_w_down[l])
        ys = ys + swiglu(rmsnorm(ys, norm_ffn[l]), ffn_w_gate[l], ffn_w_up[l], ffn_w_down[l])
    yp = rmsnorm(yp, norm_final)
    ys = rmsnorm(ys, norm_final)
    return (yp, ys, jnp.stack(kp_l), jnp.stack(vp_l), jnp.stack(ks_l), jnp.stack(vs_l),
            jnp.stack(poolp_l), jnp.stack(pools_l), jnp.stack(convp_l), jnp.stack(convs_l), jnp.stack(chv_l))
```

```python
import numpy as np
from contextlib import ExitStack
import concourse.bass as bass
import concourse.mybir as mybir
from concourse.bass_utils import run_bass_kernel_spmd

F32 = mybir.dt.float32
BF16 = mybir.dt.bfloat16
I32 = mybir.dt.int32
AF = mybir.ActivationFunctionType
ALU = mybir.AluOpType
AX = mybir.AxisListType

CENG = ("pe", "act", "dve", "pool")
ALLENG = ("pe", "act", "dve", "pool", "sp")


class Buf:
    __slots__ = ("name", "w", "r")

    def __init__(self, name=""):
        self.name = name
        self.w = None
        self.r = {}


class DSem:
    __slots__ = ("sem", "cnt", "name", "unit")

    def __init__(self, sem, name, unit=16):
        self.sem = sem
        self.cnt = 0
        self.name = name
        self.unit = unit


class Op:
    __slots__ = ("eng", "fn", "waits", "need", "sig", "ds", "isdma")

    def __init__(self, eng, fn, isdma=False, ds=None):
        self.eng = eng
        self.fn = fn
        self.waits = []
        self.need = False
        self.sig = 0
        self.ds = ds
        self.isdma = isdma


class Prog:
    def __init__(self, nc):
        self.nc = nc
        self.stack = ExitStack()
        self.ops = {e: [] for e in ALLENG}
        self.esem = {}
        self.dsems = []
        self.nops = 0

    def sb(self, name, shape, dtype):
        return self.stack.enter_context(self.nc.sbuf_tensor(name, list(shape), dtype))

    def ps(self, name, shape, dtype=F32):
        return self.stack.enter_context(self.nc.psum_tensor(name, list(shape), dtype))

    def dsem(self, name, unit=16):
        s = self.stack.enter_context(self.nc.semaphore(name))
        d = DSem(s, name, unit)
        self.dsems.append(d)
        return d

    def _deps(self, op, reads, writes):
        def add(tok, kind):
            if tok is None:
                return
            if tok[0] == "c":
                prod = tok[1]
                if (not op.isdma) and prod.eng == op.eng and kind != "raw":
                    return
                prod.need = True
                op.waits.append(("c", prod))
            else:
                ds = tok[1]
                op.waits.append(("d", ds, ds.unit * ds.cnt))
        for b in reads:
            add(b.w, "raw")
        for b in writes:
            add(b.w, "waw")
            for t in b.r.values():
                add(t, "war")
        tok = ("d", op.ds) if op.isdma else ("c", op)
        for b in writes:
            b.w = tok
            b.r = {}
        rkey = id(op.ds) if op.isdma else op.eng
        for b in reads:
            b.r[rkey] = tok

    def op(self, eng, fn, reads=(), writes=()):
        o = Op(eng, fn)
        self._deps(o, reads, writes)
        self.ops[eng].append(o)
        self.nops += 1
        return o

    def dma(self, q, ds, fn, reads=(), writes=()):
        o = Op(q, fn, isdma=True, ds=ds)
        self._deps(o, reads, writes)
        ds.cnt += 1
        self.ops[q].append(o)
        self.nops += 1
        return o

    def wait_all_dma(self, q="sp"):
        o = Op(q, None)
        for d in self.dsems:
            if d.cnt:
                o.waits.append(("d", d, d.unit * d.cnt))
        self.ops[q].append(o)

    def emit(self):
        nc = self.nc
        for e in CENG:
            self.esem[e] = self.stack.enter_context(nc.semaphore("es_" + e))
        for e in ALLENG:
            n = 0
            for o in self.ops[e]:
                if o.need and not o.isdma:
                    n += 1
                    o.sig = n
        ops = self.ops
        esem = self.esem

        def stream(ename, eng):
            waited = {}
            for o in ops[ename]:
                for w in o.waits:
                    if w[0] == "c":
                        prod = w[1]
                        key = prod.eng
                        val = prod.sig
                        sem = esem[prod.eng]
                    else:
                        key = w[1]
                        val = w[2]
                        sem = w[1].sem
                    if waited.get(key, 0) >= val:
                        continue
                    waited[key] = val
                    eng.wait_ge(sem, val)
                if o.fn is None:
                    continue
                ins = o.fn(eng)
                if o.isdma:
                    ins.then_inc(o.ds.sem, o.ds.unit)
                elif o.need:
                    ins.then_inc(esem[ename], 1)

        with nc.Block() as block:
            @block.sync
            def _(e):
                stream("sp", e)

            @block.scalar
            def _(e):
                stream("act", e)

            @block.tensor
            def _(e):
                stream("pe", e)

            @block.vector
            def _(e):
                stream("dve", e)

            @block.gpsimd
            def _(e):
                stream("pool", e)
        self.stack.close()


class TPool:
    def __init__(self, items):
        self.free = list(items)

    def get(self):
        assert self.free, "tile pool exhausted"
        return self.free.pop(0)

    def put(self, it):
        self.free.append(it)


D = 1024
DFF = 2816
KFF = DFF // 128
EPS = 1e-6


def default_cfg():
    return dict(NCORES=8, S=4096, DB=32, NPG=128, PPC=640, NSPLIT=2)


def build(cfg):
    NCORES = cfg["NCORES"]; S = cfg["S"]; DB = cfg["DB"]; NPG = cfg["NPG"]; PPC = cfg["PPC"]
    NT = S // 512
    OWN = DB // NCORES
    assert OWN == 4 and DB <= 32
    G = 8
    assert PPC % G == 0
    NGRP = PPC // G

    nc = bass.Bass("TRN2", target_bir_lowering=False)

    def din(name, shape, dt=F32):
        return nc.dram_tensor(name, list(shape), dt, kind="ExternalInput").ap()

    def dout(name, shape, dt=F32):
        return nc.dram_tensor(name, list(shape), dt, kind="ExternalOutput").ap()

    def dint(name, shape, dt=F32):
        return nc.dram_tensor(name, list(shape), dt, kind="Internal").ap()

    xp = din("xp", [S, D]); xs = din("xs", [DB, D])
    NSPLIT = cfg.get("NSPLIT", 1)
    PPS = PPC // NSPLIT
    assert PPC % NSPLIT == 0 and PPS % G == 0
    cks = [din("ck%d" % i, [PPS * 128, 512]) for i in range(NSPLIT)]
    cvs = [din("cv%d" % i, [PPS * 128, 512]) for i in range(NSPLIT)]
    spst = din("spst", [OWN, 15, 512]); scst = din("scst", [OWN, 2, 512])
    ptT = din("ptT", [NPG, DB], I32); pbase = din("pbase", [128, 1])
    norm_mix = din("norm_mix", [2, D]); norm_ffn = din("norm_ffn", [2, D]); norm_final = din("norm_final", [1, D])
    w_in0 = din("w_in0", [D, 2048]); sb_bias = din("sb_bias", [1, 8]); w_pool = din("w_pool", [512, 128])
    pool_scale = din("pool_scale", [1, 512]); w_out0 = din("w_out0", [D, D])
    w_in1 = din("w_in1", [D, 2560]); w_s = din("w_s", [512, 128]); b_s = din("b_s", [4, 128])
    conv_w = din("conv_w", [3, 512]); w_out1 = din("w_out1", [D, D])
    w_gate = [din("w_gate%d" % l, [D, DFF]) for l in range(2)]
    w_up = [din("w_up%d" % l, [D, DFF]) for l in range(2)]
    w_down = [din("w_down%d" % l, [DFF, D]) for l in range(2)]

    yp = dout("yp", [S, D]); ys = dout("ys", [OWN, D])
    kp = dout("kp", [S, 512]); vp = dout("vp", [S, 512])
    ksm = dout("ksm", [OWN, 512]); vsm = dout("vsm", [OWN, 512])
    poolp = dout("poolp", [15, 512]); pools = dout("pools", [OWN, 15, 512])
    convp = dout("convp", [2, 512]); convs = dout("convs", [OWN, 2, 512])
    chv = dout("chv", [OWN, 512])
    DBG = cfg.get("dbg", False)
    if DBG:
        dbg_wmt = dout("dbg_wmt", [128, 512], BF16)
        dbg_mix = dout("dbg_mix", [128, 4096], BF16)
        dbg_mix0 = dout("dbg_mix0", [128, 4096], BF16)
        dbg_t32 = dout("dbg_t32", [128, 512], F32)
        dbg_t32b = dout("dbg_t32b", [128, 512], F32)
        dbg_wmt0 = dout("dbg_wmt0", [128, 512], BF16)

    WS = {}
    wsrc = {"in0": (w_in0, 8, 2048), "out0": (w_out0, 8, D), "in1": (w_in1, 8, 2560), "out1": (w_out1, 8, D)}
    for l in range(2):
        wsrc["gate%d" % l] = (w_gate[l], 8, DFF)
        wsrc["up%d" % l] = (w_up[l], 8, DFF)
        wsrc["down%d" % l] = (w_down[l], KFF, D)
    for k, (src, kc, m) in wsrc.items():
        WS[k] = dint("wbf_" + k, [128, kc, m], BF16)
    rec_o_loc = dint("rec_o_loc", [PPC * 8, 64]); rec_t_loc = dint("rec_t_loc", [PPC, 8])
    CPG = cfg.get("CPG", min(128, PPC))
    assert PPC % CPG == 0
    NCH = PPC // CPG
    rec_o_all = dint("rec_o_all", [NCH * NCORES * CPG * 8, 64]); rec_t_all = dint("rec_t_all", [NCORES * PPC, 8])

    p = Prog(nc)
    cfg["_sbuf0"] = nc.sbuf_bytes_remaining
    psum = TPool([(p.ps("bank%d" % i, [128, 512]), Buf("bank%d" % i)) for i in range(8)])
    P32W = 544
    NP32 = cfg.get("NP32", 12)
    NP16 = cfg.get("NP16", 20)
    p32 = TPool([(p.sb("p32_%d" % i, [128, P32W], F32), Buf("p32_%d" % i)) for i in range(NP32)])
    p16 = TPool([(p.sb("p16_%d" % i, [128, 512], BF16), Buf("p16_%d" % i)) for i in range(NP16)])
    X = p.sb("X", [128, 8, 512], F32); Xb = [Buf("X%d" % c) for c in range(8)]
    H = p.sb("H", [128, 8, 512], BF16); Hb = [Buf("H%d" % c) for c in range(8)]
    QT = p.sb("QT", [128, 4, 512], BF16); QTb = [Buf() for _ in range(4)]
    MIX = p.sb("MIX", [128, 8, 512], BF16); MIXb = [Buf() for _ in range(8)]
    GF = None; GFb = None
    ARENA = p.sb("ARENA", [128, 16384], F32)
    AR16 = ARENA[:].bitcast(BF16)
    KT = AR16[:, 0:16384].rearrange("p (j n) -> p j n", j=4)
    VB = AR16[:, 16384:32768].rearrange("p (b f) -> p b f", f=512)
    KTb = [[Buf() for _ in range(NT)] for _ in range(4)]
    VBb = [Buf() for _ in range(32)]
    phase = Buf("phase")
    NSLOT = 3
    WR = [(p.sb("wr%d" % i, [128, 4096], BF16), Buf("wr%d" % i)) for i in range(NSLOT)]
    wr_i = [0]
    Xs = p.sb("Xs", [128, 8, 32], F32); Xsb = [Buf() for _ in range(8)]
    Hs = p.sb("Hs", [128, 8, 32], BF16); Hsb = [Buf() for _ in range(8)]
    MIXs = p.sb("MIXs", [128, 8, 4], BF16); MIXsb = [Buf() for _ in range(8)]
    GFs = p.sb("GFs", [128, 22, 4], BF16); GFsb = [Buf() for _ in range(22)]
    TOKS = p.sb("TOKS", [32, 2560], F32); TOKSb = Buf()
    QSb16 = p.sb("QSb16", [DB, 512], BF16); QSbb = Buf()
    ident = p.sb("ident", [128, 128], F32); identb = p.sb("identb", [128, 128], BF16)
    trineg = p.sb("trineg", [128, 128], BF16); onesneg = p.sb("onesneg", [128, 128], BF16)
    onesmean = p.sb("onesmean", [128, 128], BF16)
    tripos = p.sb("tripos", [128, 128], F32)
    trimask = p.sb("trimask", [128, 128], F32)
    ones32 = p.sb("ones32", [128, 128], F32)
    negm = p.sb("negm", [128, 4, 512], BF16)
    cb = Buf("const")
    GMIX = p.sb("GMIX", [128, 2, 8], F32); GFFN = p.sb("GFFN", [128, 2, 8], F32); GFIN = p.sb("GFIN", [128, 8], F32)
    BIAS = p.sb("BIAS", [128, 8], F32); PSC = p.sb("PSC", [128, 4], F32); CW = p.sb("CW", [128, 3, 4], F32)
    WPb = p.sb("WPb", [128, 4, 128], BF16); WMT = p.sb("WMT", [128, 4, 128], BF16)
    BSH = p.sb("BSH", [2, 4, 128], BF16); BSL = p.sb("BSL", [2, 4, 128], BF16)
    ones2 = p.sb("ones2", [2, 128], BF16)
    RC0 = p.sb("RC0", [128, 16], F32)
    HALO = p.sb("HALO", [128, 4, 16], F32); HALOb = Buf()
    ZHALO = p.sb("ZHALO", [128, 4, 2], F32); ZHALOb = Buf()
    PTT = p.sb("PTT", [128, DB], I32); PTF = p.sb("PTF", [128, DB], F32); PBASE = p.sb("PBASE", [128, 1], F32)
    OH = p.sb("OH", [DB, PPC], BF16); OHb = Buf()
    IOTAP = p.sb("IOTAP", [128, PPC], F32)
    ESEL = p.sb("ESEL", [128, DB, DB], BF16)
    BIASG = p.sb("BIASG", [128, G, 8], F32)
    BD128 = p.sb("BD128", [128, 512], F32); SEL4 = p.sb("SEL4", [128, 4], F32)

    d_const = p.dsem("d_const"); d_misc = p.dsem("d_misc"); d_toks = p.dsem("d_toks")
    _bds = {}

    def dsof(buf):
        if id(buf) not in _bds:
            _bds[id(buf)] = p.dsem("db%d" % len(_bds))
        return _bds[id(buf)]
    d_wr = [p.dsem("d_wr%d" % i) for i in range(NSLOT)]
    d_cc = p.dsem("d_cc", unit=1)
    d_pc = p.dsem("d_pc")

    def MM(out, lhsT, rhs, start, stop, reads, writes):
        return p.op("pe", lambda e: e.matmul(out, lhsT=lhsT, rhs=rhs, start=start, stop=stop), reads, writes)

    def TR(out, in_, reads, writes):
        k = in_.shape[0]
        return p.op("pe", lambda e: e.transpose(out, in_, ident[:k, :k]), list(reads) + [cb], writes)

    def ACT(out, in_, func, reads, writes, bias=None, scale=None):
        kw = {}
        if bias is not None:
            kw["bias"] = bias
        if scale is not None:
            kw["scale"] = scale
        return p.op("act", lambda e: e.activation(out=out, in_=in_, func=func, **kw), reads, writes)

    def TT(eng, out, in0, in1, op, reads, writes):
        return p.op(eng, lambda e: e.tensor_tensor(out=out, in0=in0, in1=in1, op=op), reads, writes)

    def TS(eng, out, in0, s1, s2, op0, op1, reads, writes):
        if s2 is None:
            return p.op(eng, lambda e: e.tensor_scalar(out=out, in0=in0, scalar1=s1, scalar2=None, op0=op0), reads, writes)
        return p.op(eng, lambda e: e.tensor_scalar(out=out, in0=in0, scalar1=s1, scalar2=s2, op0=op0, op1=op1), reads, writes)

    def STT(out, in0, scalar, in1, op0, op1, reads, writes):
        return p.op("dve", lambda e: e.scalar_tensor_tensor(out=out, in0=in0, scalar=scalar, in1=in1, op0=op0, op1=op1),
                    reads, writes)

    def CP(eng, out, in_, reads, writes):
        if eng == "act":
            return ACT(out, in_, AF.Copy, reads, writes)
        return p.op(eng, lambda e: e.tensor_copy(out=out, in_=in_), reads, writes)

    def MS(eng, ap, val, writes):
        return p.op(eng, lambda e: e.memset(ap, val), (), writes)

    def DMA(q, ds, out, in_, reads, writes, slow=False):
        if slow:
            return p.dma(q, ds, lambda e: e.dma_start(out=out, in_=in_, allow_slow_non_contiguous=True), reads, writes)
        return p.dma(q, ds, lambda e: e.dma_start(out=out, in_=in_), reads, writes)

    ev_i = [0]

    def evac_eng():
        ev_i[0] += 1
        return "act" if ev_i[0] % 2 else "dve"

    UPTO = cfg.get("upto", 99)
    print("sbuf bytes remaining", cfg["_sbuf0"], "->", nc.sbuf_bytes_remaining)

    class _Done(Exception):
        pass

    def fin():
        p.wait_all_dma("sp")
        p.emit()
        return nc

    MS("pool", ident[:], 1.0, [cb])
    p.op("pool", lambda e: e.affine_select(out=ident[:], in_=ident[:], pattern=[[-1, 128]], compare_op=ALU.is_equal,
                                           fill=0.0, base=0, channel_multiplier=1), [cb], [cb])
    CP("pool", identb[:], ident[:], [cb], [cb])
    MS("pool", onesneg[:], -1.0, [cb])
    MS("pool", ones32[:], 1.0, [cb])
    MS("pool", onesmean[:], 1.0 / 1024.0, [cb])
    MS("pool", ones2[:], 1.0, [cb])
    p.op("pool", lambda e: e.affine_select(out=trineg[:], in_=onesneg[:], pattern=[[-1, 128]], compare_op=ALU.is_ge,
                                           fill=0.0, base=0, channel_multiplier=1), [cb], [cb])
    p.op("pool", lambda e: e.affine_select(out=tripos[:], in_=ones32[:], pattern=[[-1, 128]], compare_op=ALU.is_gt,
                                           fill=0.0, base=0, channel_multiplier=1), [cb], [cb])
    p.op("pool", lambda e: e.affine_select(out=trimask[:], in_=ones32[:], pattern=[[1, 128]], compare_op=ALU.is_ge,
                                           fill=0.0, base=0, channel_multiplier=-1), [cb], [cb])
    big = p32.get()
    MS("pool", big[0][:, 0:512], -30000.0, [big[1]])
    for r in range(4):
        p.op("pool", lambda e, r=r: e.affine_select(out=negm[:, r, :], in_=big[0][:, 0:512], pattern=[[-1, 512]],
                                                    compare_op=ALU.is_ge, fill=0.0, base=r * 128, channel_multiplier=1),
             [big[1]], [cb])
    p32.put(big)
    p.op("pool", lambda e: e.iota(RC0[:], pattern=[[1, 16]], base=1, channel_multiplier=0,
                                  allow_small_or_imprecise_dtypes=True), (), [cb])
    p.op("dve", lambda e: e.reciprocal(out=RC0[:], in_=RC0[:]), [cb], [cb])
    p.op("pool", lambda e: e.iota(IOTAP[:], pattern=[[1, PPC]], base=0, channel_multiplier=0,
                                  allow_small_or_imprecise_dtypes=True), (), [cb])
    MS("pool", ESEL[:], 1.0, [cb])
    p.op("pool", lambda e: e.affine_select(out=ESEL[:], in_=ESEL[:], pattern=[[1, DB], [-1, DB]], compare_op=ALU.is_equal,
                                           fill=0.0, base=0, channel_multiplier=0), [cb], [cb])
    MS("pool", BD128[:], 1.0, [cb])
    MS("pool", SEL4[:], 0.0, [cb])
    for q in range(4):
        p.op("pool", lambda e, q=q: e.affine_select(out=BD128[q * 32:(q + 1) * 32, :].rearrange("p (h d) -> p h d", h=8),
                                                    in_=BD128[q * 32:(q + 1) * 32, :].rearrange("p (h d) -> p h d", h=8),
                                                    pattern=[[-1, 8], [0, 64]], compare_op=ALU.is_equal, fill=0.0, base=0,
                                                    channel_multiplier=1), [cb], [cb])
        MS("pool", SEL4[q * 32:(q + 1) * 32, q:q + 1], 1.0, [cb])
    MS("pool", HALO[:], 0.0, [HALOb])
    MS("pool", ZHALO[:], 0.0, [ZHALOb])
    for l in range(2):
        DMA("sp", d_const, GMIX[:, l, :], norm_mix[l].rearrange("(c q) -> q c", q=128), (), [cb], slow=True)
        DMA("sp", d_const, GFFN[:, l, :], norm_ffn[l].rearrange("(c q) -> q c", q=128), (), [cb], slow=True)
    DMA("sp", d_const, GFIN[:], norm_final[0].rearrange("(c q) -> q c", q=128), (), [cb], slow=True)
    DMA("sp", d_const, BIAS[:], sb_bias.to_broadcast([128, 8]), (), [cb])
    DMA("sp", d_const, PSC[:], pool_scale[0].rearrange("(c q) -> q c", q=128), (), [cb], slow=True)
    for j in range(3):
        DMA("sp", d_const, CW[:, j, :], conv_w[j].rearrange("(c q) -> q c", q=128), (), [cb], slow=True)
    DMA("sp", d_const, PTT[:NPG, :], ptT, (), [cb])
    DMA("sp", d_const, PBASE[:], pbase, (), [cb])
    t32 = p32.get()
    DMA("sp", dsof(t32[1]), t32[0][:, 0:512].rearrange("p (g d) -> p g d", g=4), w_pool.rearrange("(g c) d -> c g d", g=4), (), [t32[1]])
    CP("dve", WPb[:], t32[0][:, 0:512].rearrange("p (g d) -> p g d", g=4), [t32[1]], [cb])
    p32.put(t32)
    t32 = p32.get()
    DMA("sp", dsof(t32[1]), t32[0][:, 0:512].rearrange("p (g d) -> p g d", g=4), w_s.rearrange("(g t) s -> t g s", g=4), (), [t32[1]])
    tb_ = psum.get()
    for g in range(4):
        TR(tb_[0][:, g * 128:(g + 1) * 128], t32[0][:, g * 128:(g + 1) * 128], [t32[1]], [tb_[1]])
    t32b = p32.get()
    CP("dve", t32b[0][:, 0:512], tb_[0][:, :], [tb_[1]], [t32b[1]])
    psum.put(tb_)
    for g in range(4):
        TT("dve", WMT[:, g, :], t32b[0][:, g * 128:(g + 1) * 128], trimask[:], ALU.mult, [t32b[1], cb], [cb])
    if DBG:
        DMA("sp", d_misc, dbg_wmt0, WMT[:].rearrange("p g t -> p (g t)"), [cb], ())
        DMA("sp", d_misc, dbg_t32, t32[0][:, 0:512], [t32[1]], ())
        DMA("sp", d_misc, dbg_t32b, t32b[0][:, 0:512], [t32b[1]], ())
    p32.put(t32); p32.put(t32b)
    t32 = p32.get(); t32b = p32.get()
    DMA("sp", dsof(t32[1]), t32[0][0:1, 0:512], b_s.rearrange("(o g) t -> o (g t)", o=1), (), [t32[1]])
    CP("dve", BSH[0:1, :, :].rearrange("p g t -> p (g t)"), t32[0][0:1, 0:512], [t32[1]], [cb])
    CP("dve", t32b[0][0:1, 0:512], BSH[0:1, :, :].rearrange("p g t -> p (g t)"), [cb], [t32b[1]])
    TT("dve", t32b[0][0:1, 0:512], t32[0][0:1, 0:512], t32b[0][0:1, 0:512], ALU.subtract, [t32[1], t32b[1]], [t32b[1]])
    CP("dve", BSL[0:1, :, :].rearrange("p g t -> p (g t)"), t32b[0][0:1, 0:512], [t32b[1]], [cb])
    p32.put(t32); p32.put(t32b)
    for g in range(G):
        CP("pool", BIASG[:, g, :], BIAS[:], [cb], [cb])

    wsb = {k: [Buf("ws_%s_%d" % (k, c)) for c in range(wsrc[k][1])] for k in WS}
    for k, (src, kc, m) in wsrc.items():
        for c in range(kc):
            DMA("pool", d_pc, WS[k][:, c, :], src[c * 128:(c + 1) * 128, :], (), [wsb[k][c]])

    def wload(k, c0, mc):
        kc = wsrc[k][1]
        i = wr_i[0] % NSLOT
        wr_i[0] += 1
        slot, sbuf = WR[i]
        view = slot[:, 0:kc * mc].rearrange("p (k m) -> p k m", k=kc)
        DMA("sp", d_wr[i], view, WS[k][:, :, c0:c0 + mc], wsb[k], [sbuf])
        return view, sbuf

    def dense_fm(k, c0, nm, rhs, rhs_bufs, N, epi, mc=512):
        kc = wsrc[k][1]
        mper = mc // 128
        m = 0
        while m < nm:
            nmm = min(mper, nm - m)
            view, sbuf = wload(k, c0 + m * 128, nmm * 128)
            for mi in range(nmm):
                bank = psum.get()
                for c in range(kc):
                    MM(bank[0][:, 0:N], view[:, c, mi * 128:(mi + 1) * 128], rhs(c), c == 0, c == kc - 1,
                       [sbuf, rhs_bufs[c]], [bank[1]])
                epi(m + mi, bank)
                psum.put(bank)
            m += nmm

    def rmsnorm(Xt, Xbufs, gcol, Ht, Hbufs, N, out32=None):
        ms = psum.get()
        for c in range(8):
            sq = p16.get()
            ACT(sq[0][:, 0:N], Xt[:, c, 0:N], AF.Square, [Xbufs[c]], [sq[1]])
            MM(ms[0][:, 0:N], onesmean[:], sq[0][:, 0:N], c == 0, c == 7, [sq[1], cb], [ms[1]])
            p16.put(sq)
        rstd = p32.get()
        ACT(rstd[0][:, 0:N], ms[0][:, 0:N], AF.Ln, [ms[1]], [rstd[1]], bias=EPS)
        psum.put(ms)
        ACT(rstd[0][:, 0:N], rstd[0][:, 0:N], AF.Exp, [rstd[1]], [rstd[1]], scale=-0.5)
        if out32 is not None:
            return rstd
        for c in range(8):
            STT(Ht[:, c, 0:N], Xt[:, c, 0:N], gcol(c), rstd[0][:, 0:N], ALU.mult, ALU.mult,
                [Xbufs[c], rstd[1], cb], [Hbufs[c]])
        p32.put(rstd)
        return None

    def ffn(l, Xt, Xbufs, Ht, Hbufs, N, GFt, GFbufs, nhalf):
        rmsnorm(Xt, Xbufs, lambda c: GFFN[:, l, c:c + 1], Ht, Hbufs, N)
        per = KFF // nhalf
        for hf in range(nhalf):
            m0 = hf * per
            mm = 0
            gfl = None
            if GFt is None:
                gfl = [p16.get() for _ in range(per)]
            while mm < per:
                nmm = min(4, per - mm)
                gview, gsb = wload("gate%d" % l, (m0 + mm) * 128, nmm * 128)
                uview, usb = wload("up%d" % l, (m0 + mm) * 128, nmm * 128)
                for mi in range(nmm):
                    bg_ = psum.get(); bu_ = psum.get()
                    for c in range(8):
                        MM(bg_[0][:, 0:N], gview[:, c, mi * 128:(mi + 1) * 128], Ht[:, c, 0:N], c == 0, c == 7,
                           [gsb, Hbufs[c]], [bg_[1]])
                    for c in range(8):
                        MM(bu_[0][:, 0:N], uview[:, c, mi * 128:(mi + 1) * 128], Ht[:, c, 0:N], c == 0, c == 7,
                           [usb, Hbufs[c]], [bu_[1]])
                    st = p32.get()
                    ACT(st[0][:, 0:N], bg_[0][:, 0:N], AF.Silu, [bg_[1]], [st[1]])
                    psum.put(bg_)
                    if gfl is not None:
                        TT("dve", gfl[mm + mi][0][:, 0:N], st[0][:, 0:N], bu_[0][:, 0:N], ALU.mult, [st[1], bu_[1]],
                           [gfl[mm + mi][1]])
                    else:
                        TT("dve", GFt[:, mm + mi, 0:N], st[0][:, 0:N], bu_[0][:, 0:N], ALU.mult, [st[1], bu_[1]],
                           [GFbufs[mm + mi]])
                    psum.put(bu_)
                    p32.put(st)
                mm += nmm
            dstep = 2 if per <= 16 else 1
            for mo in range(0, 8, dstep):
                i = wr_i[0] % NSLOT
                wr_i[0] += 1
                slot, sbuf = WR[i]
                view = slot[:, 0:per * 128 * dstep].rearrange("p (k m) -> p k m", k=per)
                DMA("sp", d_wr[i], view, WS["down%d" % l][:, m0:m0 + per, mo * 128:(mo + dstep) * 128], wsb["down%d" % l], [sbuf])
                for mi in range(dstep):
                    bank = psum.get()
                    for c in range(per):
                        if gfl is not None:
                            MM(bank[0][:, 0:N], view[:, c, mi * 128:(mi + 1) * 128], gfl[c][0][:, 0:N], c == 0, c == per - 1,
                               [sbuf, gfl[c][1]], [bank[1]])
                        else:
                            MM(bank[0][:, 0:N], view[:, c, mi * 128:(mi + 1) * 128], GFt[:, c, 0:N], c == 0, c == per - 1,
                               [sbuf, GFbufs[c]], [bank[1]])
                    TT("dve", Xt[:, mo + mi, 0:N], bank[0][:, 0:N], Xt[:, mo + mi, 0:N], ALU.add,
                       [bank[1], Xbufs[mo + mi]], [Xbufs[mo + mi]])
                    psum.put(bank)
            if gfl is not None:
                for it in gfl:
                    p16.put(it)

    def outproj(k, MIXt, MIXbufs, Xt, Xbufs, N):
        def epi(m, bank):
            TT("dve", Xt[:, m, 0:N], bank[0][:, 0:N], Xt[:, m, 0:N], ALU.add, [bank[1], Xbufs[m]], [Xbufs[m]])
        dense_fm(k, 0, 8, lambda c: MIXt[:, c, 0:N], MIXbufs, N, epi)

    if UPTO == 1:
        return fin()
    for hf in range(2):
        xt = p32.get()
        DMA("sp", dsof(xt[1]), xt[0][0:DB, 0:512], xs[:, hf * 512:(hf + 1) * 512], (), [xt[1]])
        bank = psum.get()
        for c4 in range(4):
            TR(bank[0][:, c4 * 32:c4 * 32 + DB], xt[0][0:DB, c4 * 128:(c4 + 1) * 128], [xt[1]], [bank[1]])
        p32.put(xt)
        for c4 in range(4):
            CP("dve", Xs[:, hf * 4 + c4, 0:DB], bank[0][:, c4 * 32:c4 * 32 + DB], [bank[1]], [Xsb[hf * 4 + c4]])
        psum.put(bank)
    rmsnorm(Xs, Xsb, lambda c: GMIX[:, 0, c:c + 1], Hs, Hsb, DB)
    for pc in range(4):
        view, sbuf = wload("in0", pc * 512, 512)
        bank = psum.get()
        for c in range(8):
            MM(bank[0][0:DB, :], Hs[:, c, 0:DB], view[:, c, :], c == 0, c == 7, [sbuf, Hsb[c]], [bank[1]])
        CP("dve", TOKS[0:DB, pc * 512:(pc + 1) * 512], bank[0][0:DB, :], [bank[1]], [TOKSb])
        psum.put(bank)
    DMA("pool", d_toks, ksm, TOKS[0:OWN, 1024:1536], [TOKSb], ())
    DMA("pool", d_toks, vsm, TOKS[0:OWN, 1536:2048], [TOKSb], ())
    DMA("pool", d_toks, pools[:, 14, :], TOKS[0:OWN, 0:512], [TOKSb], ())
    DMA("pool", d_misc, pools[:, 0:14, :], spst[:, 1:15, :], (), ())
    TS("dve", QSb16[0:DB, :], TOKS[0:DB, 512:1024], 0.125, None, ALU.mult, None, [TOKSb], [QSbb])
    stt_ = p32.get()
    DMA("sp", dsof(stt_[1]), stt_[0][0:OWN * 15, 0:512], spst.rearrange("b j c -> (b j) c"), (), [stt_[1]])
    ST = p32.get()
    bank = psum.get()
    for g in range(4):
        TR(bank[0][:, g * 64:g * 64 + OWN * 15], stt_[0][0:OWN * 15, g * 128:(g + 1) * 128], [stt_[1]], [bank[1]])
    p32.put(stt_)
    CP("dve", ST[0][:, 0:256], bank[0][:, 0:256], [bank[1]], [ST[1]])
    psum.put(bank)
    US = p32.get()
    view, sbuf = wload("in0", 0, 512)
    bank = psum.get()
    for g in range(4):
        for c in range(8):
            MM(bank[0][:, g * 4:g * 4 + OWN], view[:, c, g * 128:(g + 1) * 128], Hs[:, c, 0:OWN], c == 0, c == 7,
               [sbuf, Hsb[c]], [bank[1]])
    CP("dve", US[0][:, 0:16], bank[0][:, 0:16], [bank[1]], [US[1]])
    psum.put(bank)
    DS = p16.get()
    for g in range(4):
        w = 2 << g
        ssum = p32.get()
        p.op("dve", lambda e, g=g, w=w, ssum=ssum: e.tensor_reduce(
            out=ssum[0][:, 0:OWN], in_=ST[0][:, g * 64:g * 64 + OWN * 15].rearrange("p (b j) -> p b j", j=15)[:, :, 16 - w:15],
            axis=AX.X, op=ALU.add), [ST[1]], [ssum[1]])
        TT("dve", ssum[0][:, 0:OWN], ssum[0][:, 0:OWN], US[0][:, g * 4:g * 4 + OWN], ALU.add, [ssum[1], US[1]], [ssum[1]])
        STT(DS[0][:, g * 4:g * 4 + OWN], ssum[0][:, 0:OWN], 1.0 / w, US[0][:, g * 4:g * 4 + OWN], ALU.mult, ALU.subtract,
            [ssum[1], US[1]], [DS[1]])
        p32.put(ssum)
    bank = psum.get()
    for g in range(4):
        MM(bank[0][:, g * 4:g * 4 + OWN], WPb[:, g, :], DS[0][:, g * 4:g * 4 + OWN], True, True, [DS[1], cb], [bank[1]])
    for g in range(4):
        TS("dve", MIXs[:, g, 0:OWN], bank[0][:, g * 4:g * 4 + OWN], PSC[:, g:g + 1], None, ALU.mult, None,
           [bank[1], cb], [MIXsb[g]])
    psum.put(bank)
    p16.put(DS); p32.put(ST); p32.put(US)

    if UPTO == 2:
        return fin()
    CP("dve", PTF[:NPG, :], PTT[:NPG, :], [cb], [cb])
    TS("dve", PTF[:NPG, :], PTF[:NPG, :], PBASE[:NPG, 0:1], None, ALU.subtract, None, [cb], [cb])
    CH = 128
    for c0 in range(0, PPC, CH):
        cw = min(CH, PPC - c0)
        cmpb = Buf()
        CMP = AR16[:, 0:DB * cw].rearrange("p (b q) -> p b q", b=DB)
        p.op("dve", lambda e, CMP=CMP, c0=c0, cw=cw: e.tensor_tensor(
            out=CMP[:NPG], in0=IOTAP[:NPG, c0:c0 + cw].unsqueeze(1).to_broadcast([NPG, DB, cw]),
            in1=PTF[:NPG, :].unsqueeze(2).to_broadcast([NPG, DB, cw]), op=ALU.is_equal), [cb, phase], [cmpb])
        bank = psum.get()
        for b in range(DB):
            MM(bank[0][0:DB, 0:cw], ESEL[:NPG, b, :], CMP[:NPG, b, :], b == 0, b == DB - 1, [cmpb, cb, phase], [bank[1]])
        CP("dve", OH[:, c0:c0 + cw], bank[0][0:DB, 0:cw], [bank[1]], [OHb])
        psum.put(bank)
        p.op("pool", lambda e: e.memset(AR16[0:1, 0:2], 0.0), [cmpb, phase], [cmpb])

    p.op("pool", lambda e: e.memset(AR16[0:1, 0:2], 0.0), [], [phase])
    NKS = 3
    VBOFF = NKS * G * 512
    kslots = [(ARENA[:, s * G * 512:(s + 1) * G * 512].rearrange("p (g f) -> p g f", g=G), Buf()) for s in range(NKS)]
    vbslots = [(AR16[:, 2 * VBOFF + s * G * 512:2 * VBOFF + (s + 1) * G * 512].rearrange("p (g f) -> p g f", g=G), Buf()) for s in range(2)]
    OHR = [p.sb("OHR%d" % s, [DB, G * 128], BF16) for s in range(2)]
    ohrslots = [(OHR[s][:, :].rearrange("p (g t) -> p g t", g=G), Buf()) for s in range(2)]
    def sampA(gi):
        s = gi % 2
        kt, kb_ = kslots[gi % NKS]; vbt, vbb_ = vbslots[s]; oht, ohb_ = ohrslots[s]
        pg0 = gi * G
        ck = cks[pg0 // PPS]; cv = cvs[pg0 // PPS]; pl0 = pg0 % PPS
        DMA("sp", dsof(kb_), kt, ck[pl0 * 128:(pl0 + G) * 128, :].rearrange("(g t) f -> t g f", g=G), [phase], [kb_])
        DMA("pool", dsof(vbb_), vbt, cv[pl0 * 128:(pl0 + G) * 128, :].rearrange("(g t) f -> t g f", g=G), [phase], [vbb_])
        CP("pool", oht, OH[:, pg0:pg0 + G].unsqueeze(2).to_broadcast([DB, G, 128]), [OHb, phase], [ohb_])
        zt = p32.get()
        for g in range(G):
            qb = psum.get()
            MM(qb[0][:, :], oht[:, g, :], QSb16[0:DB, :], True, True, [ohb_, QSbb, phase], [qb[1]])
            pr = p32.get()
            TT("dve", pr[0][:, 0:512], kt[:, g, :], qb[0][:, :], ALU.mult, [kb_, qb[1], phase], [pr[1]])
            psum.put(qb)
            p.op("dve", lambda e, pr=pr, zt=zt, g=g: e.tensor_reduce(
                out=zt[0][:, g * 8:(g + 1) * 8], in_=pr[0][:, 0:512].rearrange("p (h d) -> p h d", h=8), axis=AX.X,
                op=ALU.add), [pr[1]], [zt[1]])
            p32.put(pr)
        return (zt, vbt, vbb_, pg0)

    def sampB(ctx):
        zt, vbt, vbb_, pg0 = ctx
        NC8 = G * 8
        TT("dve", zt[0][:, 0:NC8], zt[0][:, 0:NC8], BIASG[:].rearrange("p g h -> p (g h)"), ALU.add, [zt[1], cb], [zt[1]])
        et = p32.get()
        ACT(et[0][:, 0:NC8], zt[0][:, 0:NC8], AF.Exp, [zt[1]], [et[1]])
        spb = p16.get()
        ACT(spb[0][:, 0:NC8], et[0][:, 0:NC8], AF.Ln, [et[1]], [spb[1]], bias=1.0)
        p32.put(et)
        lat = psum.get()
        MM(lat[0][:, 0:NC8], trineg[:], spb[0][:, 0:NC8], True, True, [spb[1], cb], [lat[1]])
        tot = psum.get()
        MM(tot[0][0:1, 0:NC8], onesneg[:, 0:1], spb[0][:, 0:NC8], True, True, [spb[1], cb], [tot[1]])
        p16.put(spb)
        TT("dve", zt[0][:, 0:NC8], lat[0][:, 0:NC8], zt[0][:, 0:NC8], ALU.add, [lat[1], zt[1]], [zt[1]])
        psum.put(lat)
        wt = p16.get()
        ACT(wt[0][:, 0:NC8], zt[0][:, 0:NC8], AF.Exp, [zt[1]], [wt[1]])
        p32.put(zt)
        tots = p32.get()
        CP("act", tots[0][0:1, 0:NC8], tot[0][0:1, 0:NC8], [tot[1]], [tots[1]])
        psum.put(tot)
        DMA("pool", dsof(tots[1]), rec_t_loc[pg0:pg0 + G, :].rearrange("(o g) h -> o (g h)", o=1), tots[0][0:1, 0:NC8], [tots[1]], [phase] if False else ())
        p32.put(tots)
        for sg in range(G // 4):
            ot = psum.get()
            for q in range(4):
                pg = sg * 4 + q
                p.op("pe", lambda e, ot=ot, q=q, pg=pg, vbt=vbt, wt=wt: e.matmul(
                    ot[0][q * 32:q * 32 + 8, :], lhsT=wt[0][:, pg * 8:(pg + 1) * 8], rhs=vbt[:, pg, :], start=True, stop=True,
                    tile_position=(0, q * 32)), [vbb_, wt[1], phase], [ot[1]])
            msk = p32.get()
            TT("dve", msk[0][:, 0:512], ot[0][:, :], BD128[:], ALU.mult, [ot[1], cb], [msk[1]])
            psum.put(ot)
            o2 = psum.get()
            MM(o2[0][0:4, :], SEL4[:], msk[0][:, 0:512], True, True, [msk[1], cb], [o2[1]])
            p32.put(msk)
            orow = p32.get()
            CP("act", orow[0][0:4, 0:512], o2[0][0:4, :], [o2[1]], [orow[1]])
            psum.put(o2)
            DMA("pool", dsof(orow[1]), rec_o_loc.rearrange("(q h) d -> q (h d)", h=8)[pg0 + sg * 4:pg0 + sg * 4 + 4, :],
                orow[0][0:4, 0:512], [orow[1]], ())
            p32.put(orow)
        p16.put(wt)
    ctx_ = sampA(0)
    for gi in range(NGRP):
        nxt_ = sampA(gi + 1) if gi + 1 < NGRP else None
        sampB(ctx_)
        ctx_ = nxt_
    recb = Buf("rec")

    def allgather(groups, src, dst, first, rb, wb):
        p.dma("pool", d_cc, lambda e: e.collective_compute("AllGather", ALU.bypass, replica_groups=groups, ins=[src],
                                                           outs=[dst]), rb, wb)
        if first:
            for _it in p32.free:
                _d = dsof(_it[1])
                p.ops["pool"][-1].waits.append(("d", _d, _d.unit * _d.cnt))

    CR = CPG * 8
    stage2 = []
    if NCORES == 1:
        DMA("pool", d_misc, rec_o_all, rec_o_loc, [], [recb])
        for _it in p32.free:
            _d = dsof(_it[1])
            p.ops["pool"][-1].waits.append(("d", _d, _d.unit * _d.cnt))
        DMA("pool", d_misc, rec_t_all, rec_t_loc, [], [recb])
    elif NCORES == 8:
        g1 = [[0, 1, 2, 3], [4, 5, 6, 7]]; g2 = [[0, 4], [1, 5], [2, 6], [3, 7]]
        rec_t_half = dint("rec_t_half", [4 * PPC, 8])
        hb = Buf()
        allgather(g1, rec_t_loc, rec_t_half, True, [], [hb])
        stage2.append((g2, rec_t_half, rec_t_all, hb))
        for k in range(NCH):
            half = dint("rec_o_half%d" % k, [4 * CR, 64])
            hb = Buf()
            allgather(g1, rec_o_loc[k * CR:(k + 1) * CR, :], half, False, [], [hb])
            stage2.append((g2, half, rec_o_all[k * NCORES * CR:(k + 1) * NCORES * CR, :], hb))
    else:
        rg = [list(range(NCORES))]
        allgather(rg, rec_t_loc, rec_t_all, True, [], [recb])
        for k in range(NCH):
            allgather(rg, rec_o_loc[k * CR:(k + 1) * CR, :], rec_o_all[k * NCORES * CR:(k + 1) * NCORES * CR, :], False, [], [recb])

    def issue_stage2():
        for (g, src, dst, hb) in stage2:
            allgather(g, src, dst, False, [hb], [recb])
        stage2.clear()

    PGF = p.sb("PGF", [128, DB], F32); RNK = p.sb("RNK", [128, DB], F32); CHK = p.sb("CHK", [128, DB], F32)
    IDXO = p.sb("IDXO", [128, DB], I32)
    ib = Buf("idx")
    CP("dve", PGF[:NPG, :], PTT[:NPG, :], [cb], [ib])
    MS("dve", RNK[:NPG, :], 0.0, [ib])
    MS("dve", CHK[:NPG, :], 0.0, [ib])
    for r in range(1, NCORES):
        STT(RNK[:NPG, :], PGF[:NPG, :], float(r * PPC), RNK[:NPG, :], ALU.is_ge, ALU.add, [ib], [ib])
    STT(PGF[:NPG, :], RNK[:NPG, :], float(-PPC), PGF[:NPG, :], ALU.mult, ALU.add, [ib], [ib])
    for k in range(1, NCH):
        STT(CHK[:NPG, :], PGF[:NPG, :], float(k * CPG), CHK[:NPG, :], ALU.is_ge, ALU.add, [ib], [ib])
    STT(PGF[:NPG, :], CHK[:NPG, :], float((NCORES - 1) * CPG), PGF[:NPG, :], ALU.mult, ALU.add, [ib], [ib])
    STT(PGF[:NPG, :], RNK[:NPG, :], float(CPG), PGF[:NPG, :], ALU.mult, ALU.add, [ib], [ib])
    CP("dve", IDXO[:NPG, :], PGF[:NPG, :], [ib], [ib])
    p.op("pool", lambda e: e.memset(AR16[0:1, 0:2], 0.0), [], [phase])

    if UPTO == 3:
        return fin()
    def prompt_tile(t):
        N = 512
        r0 = t * 512
        for hf in range(2):
            banks = [psum.get() for _ in range(4)]
            for tb in range(4):
                xt = p32.get()
                DMA("sp", dsof(xt[1]), xt[0][:, 0:512], xp[r0 + tb * 128:r0 + (tb + 1) * 128, hf * 512:(hf + 1) * 512], (), [xt[1]])
                for c4 in range(4):
                    TR(banks[c4][0][:, tb * 128:(tb + 1) * 128], xt[0][:, c4 * 128:(c4 + 1) * 128], [xt[1]], [banks[c4][1]])
                p32.put(xt)
            for c4 in range(4):
                CP(evac_eng(), X[:, hf * 4 + c4, :], banks[c4][0][:, :], [banks[c4][1]], [Xb[hf * 4 + c4]])
                psum.put(banks[c4])
        if UPTO == 10:
            raise _Done()
        rmsnorm(X, Xb, lambda c: GMIX[:, 0, c:c + 1], H, Hb, N)
        if UPTO == 11:
            raise _Done()
        U = [p32.get() for _ in range(4)]

        def epi_u(m, bank):
            CP("dve", U[m][0][:, 0:16], HALO[:, m, :], [HALOb], [U[m][1]])
            CP(evac_eng(), U[m][0][:, 16:528], bank[0][:, :], [bank[1]], [U[m][1]])
        dense_fm("in0", 0, 4, lambda c: H[:, c, :], Hb, N, epi_u)
        if UPTO == 111:
            raise _Done()

        def epi_q(m, bank):
            ACT(QT[:, m, :], bank[0][:, :], AF.Copy, [bank[1]], [QTb[m]], scale=0.125)
        dense_fm("in0", 512, 4, lambda c: H[:, c, :], Hb, N, epi_q)
        if UPTO == 112:
            raise _Done()
        view, sbuf = wload("in0", 1024, 512)
        for m in range(4):
            bank = psum.get()
            for c in range(8):
                MM(bank[0][:, :], view[:, c, m * 128:(m + 1) * 128], H[:, c, :], c == 0, c == 7, [sbuf, Hb[c]], [bank[1]])
            CP(evac_eng(), KT[:, m, r0:r0 + 512], bank[0][:, :], [bank[1], phase], [KTb[m][t]])
            psum.put(bank)
        if UPTO == 113:
            raise _Done()
        for tb in range(4):
            bank = psum.get()
            for c in range(8):
                MM(bank[0][:, :], H[:, c, tb * 128:(tb + 1) * 128], view[:, c, :], c == 0, c == 7, [sbuf, Hb[c]], [bank[1]])
            stg = p32.get()
            CP(evac_eng(), stg[0][:, 0:512], bank[0][:, :], [bank[1]], [stg[1]])
            psum.put(bank)
            DMA("pool", dsof(stg[1]), kp[r0 + tb * 128:r0 + (tb + 1) * 128, :], stg[0][:, 0:512], [stg[1]], ())
            p32.put(stg)
        if UPTO == 114:
            raise _Done()
        view, sbuf = wload("in0", 1536, 512)
        for tb in range(4):
            bank = psum.get()
            for c in range(8):
                MM(bank[0][:, :], H[:, c, tb * 128:(tb + 1) * 128], view[:, c, :], c == 0, c == 7, [sbuf, Hb[c]], [bank[1]])
            stg = p32.get()
            CP("act", stg[0][:, 0:512], bank[0][:, :], [bank[1]], [stg[1]])
            psum.put(bank)
            CP("pool", VB[:, t * 4 + tb, :], stg[0][:, 0:512], [stg[1], phase], [VBb[t * 4 + tb]])
            DMA("pool", dsof(stg[1]), vp[r0 + tb * 128:r0 + (tb + 1) * 128, :], stg[0][:, 0:512], [stg[1]], ())
            p32.put(stg)
        if UPTO == 12:
            raise _Done()
        for g in range(4):
            w = 2 << g
            ub = U[g]
            cur = ub
            tmps = []
            for k in range(g):
                sh = 1 << k
                lo = 1 + (2 << k) - 1
                nxt = p32.get()
                tmps.append(nxt)
                TT("pool", nxt[0][:, lo:528], cur[0][:, lo:528], cur[0][:, lo - sh:528 - sh], ALU.add, [cur[1]], [nxt[1]])
                cur = nxt
            sh = 1 << g
            ssum = p32.get()
            TT("pool", ssum[0][:, 0:512], cur[0][:, 16:528], cur[0][:, 16 - sh:528 - sh], ALU.add, [cur[1]], [ssum[1]])
            for tm_ in tmps:
                p32.put(tm_)
            dd = p16.get()
            STT(dd[0][:, :], ssum[0][:, 0:512], 1.0 / w, ub[0][:, 16:528], ALU.mult, ALU.subtract, [ssum[1], ub[1]], [dd[1]])
            if t == 0:
                fx = p32.get()
                TT("dve", fx[0][:, 0:w - 1], ssum[0][:, 0:w - 1], RC0[:, 0:w - 1], ALU.mult, [ssum[1], cb], [fx[1]])
                TT("dve", dd[0][:, 0:w - 1], fx[0][:, 0:w - 1], ub[0][:, 16:16 + w - 1], ALU.subtract, [fx[1], ub[1]], [dd[1]])
                p32.put(fx)
            p32.put(ssum)
            CP("pool", HALO[:, g, 1:16], ub[0][:, 513:528], [ub[1]], [HALOb])
            if t == NT - 1:
                DMA("pool", dsof(ub[1]), poolp[:, g * 128:(g + 1) * 128].rearrange("j c -> c j"), ub[0][:, 513:528], [ub[1]], (), slow=True)
            bank = psum.get()
            MM(bank[0][:, :], WPb[:, g, :], dd[0][:, :], True, True, [dd[1], cb], [bank[1]])
            p16.put(dd)
            TS("dve", MIX[:, g, :], bank[0][:, :], PSC[:, g:g + 1], None, ALU.mult, None, [bank[1], cb], [MIXb[g]])
            psum.put(bank)
            p32.put(ub)
        if UPTO == 13:
            raise _Done()
        nkb = 4 * t + 4
        for hg in range(2):
            av = [psum.get(), psum.get()]
            lacc = [p32.get() for _ in range(4)]
            laccb = [p16.get() for _ in range(4)]
            for kbi in range(nkb):
                kb = nkb - 1 - kbi
                diag = kb >= 4 * t
                r = kb - 4 * t
                kt_ = kb // 4
                spbs = []
                for hh in range(4):
                    h = hg * 4 + hh
                    j = h // 2
                    po = (h % 2) * 64
                    kT = KT[po:po + 64, j, kb * 128:(kb + 1) * 128]
                    qT = QT[po:po + 64, j, :]
                    kq_reads = [KTb[j][kt_], QTb[j], phase]
                    zb = psum.get()
                    MM(zb[0][:, :], kT, qT, True, not diag, kq_reads, [zb[1]])
                    if diag:
                        MM(zb[0][:, :], identb[:], negm[:, r, :], False, True, [cb], [zb[1]])
                    et = p32.get()
                    ACT(et[0][:, 0:512], zb[0][:, :], AF.Exp, [zb[1], cb], [et[1]], bias=BIAS[:, h:h + 1])
                    psum.put(zb)
                    spb = p16.get()
                    ACT(spb[0][:, :], et[0][:, 0:512], AF.Ln, [et[1]], [spb[1]], bias=1.0)
                    p32.put(et)
                    spbs.append(spb)
                for hh in range(4):
                    h = hg * 4 + hh
                    j = h // 2
                    po = (h % 2) * 64
                    kT = KT[po:po + 64, j, kb * 128:(kb + 1) * 128]
                    qT = QT[po:po + 64, j, :]
                    kq_reads = [KTb[j][kt_], QTb[j], phase]
                    spb = spbs[hh]
                    b2 = psum.get()
                    MM(b2[0][:, :], kT, qT, True, False, kq_reads, [b2[1]])
                    if diag:
                        MM(b2[0][:, :], identb[:], negm[:, r, :], False, False, [cb], [b2[1]])
                    MM(b2[0][:, :], trineg[:], spb[0][:, :], False, kbi == 0, [spb[1], cb], [b2[1]])
                    if kbi > 0:
                        MM(b2[0][:, :], onesneg[:], laccb[hh][0][:, :], False, True, [laccb[hh][1], cb], [b2[1]])
                    wT = p16.get()
                    ACT(wT[0][:, :], b2[0][:, :], AF.Exp, [b2[1], cb], [wT[1]], bias=BIAS[:, h:h + 1])
                    psum.put(b2)
                    MM(av[hh // 2][0][(hh % 2) * 64:(hh % 2) * 64 + 64, :], VB[:, kb, h * 64:(h + 1) * 64], wT[0][:, :],
                       kbi == 0, kbi == nkb - 1, [VBb[kb], wT[1], phase], [av[hh // 2][1]])
                    p16.put(wT)
                    if kbi < nkb - 1:
                        if kbi == 0:
                            CP("dve", lacc[hh][0][:, 0:512], spb[0][:, :], [spb[1]], [lacc[hh][1]])
                        else:
                            TT("dve", lacc[hh][0][:, 0:512], lacc[hh][0][:, 0:512], spb[0][:, :], ALU.add,
                               [lacc[hh][1], spb[1]], [lacc[hh][1]])
                        CP("dve", laccb[hh][0][:, :], lacc[hh][0][:, 0:512], [lacc[hh][1]], [laccb[hh][1]])
                    p16.put(spb)
            for k2 in range(2):
                CP("dve", MIX[:, 4 + hg * 2 + k2, :], av[k2][0][:, :], [av[k2][1]], [MIXb[4 + hg * 2 + k2]])
                psum.put(av[k2])
            for it in lacc:
                p32.put(it)
            for it in laccb:
                p16.put(it)
        if UPTO == 14:
            raise _Done()
        if DBG and t == NT - 1:
            DMA("sp", d_misc, dbg_mix0, MIX[:].rearrange("p c n -> p (c n)"), MIXb, ())
        outproj("out0", MIX, MIXb, X, Xb, N)
        if UPTO == 15:
            raise _Done()
        ffn(0, X, Xb, H, Hb, N, GF, GFb, 2)
        if UPTO == 16:
            raise _Done()
        rmsnorm(X, Xb, lambda c: GMIX[:, 1, c:c + 1], H, Hb, N)
        UG = [p32.get() for _ in range(4)]

        def epi_ug(m, bank):
            ACT(UG[m][0][:, 0:512], bank[0][:, :], AF.Gelu, [bank[1]], [UG[m][1]])
        dense_fm("in1", 0, 4, lambda c: H[:, c, :], Hb, N, epi_ug)
        VT = [p16.get() for _ in range(4)]
        view, sbuf = wload("in1", 512, 512)
        for tb in range(4):
            bank = psum.get()
            for c in range(8):
                MM(bank[0][:, :], H[:, c, tb * 128:(tb + 1) * 128], view[:, c, :], c == 0, c == 7, [sbuf, Hb[c]], [bank[1]])
            ACT(VT[tb][0][:, :], bank[0][:, :], AF.Gelu, [bank[1]], [VT[tb][1]])
            psum.put(bank)
        for g in range(4):
            bank = psum.get()
            for tb in range(4):
                MM(bank[0][:, tb * 128:(tb + 1) * 128], VT[tb][0][:, g * 128:(g + 1) * 128], WMT[:, g, :], tb == 0, False,
                   [VT[tb][1], cb], [bank[1]])
            for tb in range(4):
                MM(bank[0][:, tb * 128:(tb + 1) * 128], ones2[0:1, :], BSH[0:1, g, :], False, False, [cb], [bank[1]])
                MM(bank[0][:, tb * 128:(tb + 1) * 128], ones2[0:1, :], BSL[0:1, g, :], False, True, [cb], [bank[1]])
            TT("dve", MIX[:, g, :], bank[0][:, :], UG[g][0][:, 0:512], ALU.mult, [bank[1], UG[g][1]], [MIXb[g]])
            psum.put(bank)
        for it in VT:
            p16.put(it)
        for it in UG:
            p32.put(it)
        ZZ = [p32.get() for _ in range(4)]

        def epi_hh(m, bank):
            CP(evac_eng(), ZZ[m][0][:, 2:514], bank[0][:, :], [bank[1]], [ZZ[m][1]])
        dense_fm("in1", 1024, 4, lambda c: H[:, c, :], Hb, N, epi_hh)
        BG = [p32.get() for _ in range(4)]

        def epi_bg(m, bank):
            CP(evac_eng(), BG[m][0][:, 0:512], bank[0][:, :], [bank[1]], [BG[m][1]])
        dense_fm("in1", 1536, 4, lambda c: H[:, c, :], Hb, N, epi_bg)

        def epi_cg(m, bank):
            TT("dve", ZZ[m][0][:, 2:514], bank[0][:, :], ZZ[m][0][:, 2:514], ALU.mult, [bank[1], ZZ[m][1]], [ZZ[m][1]])
            CP("pool", ZZ[m][0][:, 0:2], ZHALO[:, m, :], [ZHALOb], [ZZ[m][1]])
            cv_ = p32.get()
            TS("pool", cv_[0][:, 0:512], ZZ[m][0][:, 0:512], CW[:, 0, m:m + 1], None, ALU.mult, None, [ZZ[m][1], cb], [cv_[1]])
            STT(cv_[0][:, 0:512], ZZ[m][0][:, 1:513], CW[:, 1, m:m + 1], cv_[0][:, 0:512], ALU.mult, ALU.add,
                [ZZ[m][1], cv_[1], cb], [cv_[1]])
            STT(cv_[0][:, 0:512], ZZ[m][0][:, 2:514], CW[:, 2, m:m + 1], cv_[0][:, 0:512], ALU.mult, ALU.add,
                [ZZ[m][1], cv_[1], cb], [cv_[1]])
            TT("pool", MIX[:, 4 + m, :], cv_[0][:, 0:512], BG[m][0][:, 0:512], ALU.mult, [cv_[1], BG[m][1]], [MIXb[4 + m]])
            p32.put(cv_)
            CP("pool", ZHALO[:, m, :], ZZ[m][0][:, 512:514], [ZZ[m][1]], [ZHALOb])
            if t == NT - 1:
                DMA("pool", dsof(ZZ[m][1]), convp[:, m * 128:(m + 1) * 128].rearrange("j c -> c j"), ZZ[m][0][:, 512:514], [ZZ[m][1]], (), slow=True)
        dense_fm("in1", 2048, 4, lambda c: H[:, c, :], Hb, N, epi_cg)
        for it in ZZ:
            p32.put(it)
        for it in BG:
            p32.put(it)
        if UPTO == 17:
            raise _Done()
        if DBG and t == NT - 1:
            DMA("sp", d_misc, dbg_mix, MIX[:].rearrange("p c n -> p (c n)"), MIXb, ())
            DMA("sp", d_misc, dbg_wmt, WMT[:].rearrange("p g t -> p (g t)"), [cb], ())
        outproj("out1", MIX, MIXb, X, Xb, N)
        ffn(1, X, Xb, H, Hb, N, GF, GFb, 2)
        if UPTO == 18:
            raise _Done()
        rstd = rmsnorm(X, Xb, None, None, None, N, out32=True)
        for hf in range(2):
            banks = [psum.get() for _ in range(4)]
            for c4 in range(4):
                c = hf * 4 + c4
                yn = p32.get()
                STT(yn[0][:, 0:512], X[:, c, :], GFIN[:, c:c + 1], rstd[0][:, 0:512], ALU.mult, ALU.mult,
                    [Xb[c], rstd[1], cb], [yn[1]])
                for tb in range(4):
                    TR(banks[tb][0][:, c4 * 128:(c4 + 1) * 128], yn[0][:, tb * 128:(tb + 1) * 128], [yn[1]], [banks[tb][1]])
                p32.put(yn)
            for tb in range(4):
                stg = p32.get()
                CP(evac_eng(), stg[0][:, 0:512], banks[tb][0][:, :], [banks[tb][1]], [stg[1]])
                psum.put(banks[tb])
                DMA("pool", dsof(stg[1]), yp[r0 + tb * 128:r0 + (tb + 1) * 128, hf * 512:(hf + 1) * 512], stg[0][:, 0:512], [stg[1]], ())
                p32.put(stg)
        p32.put(rstd)

    try:
        for t in range(NT):
            prompt_tile(t)
            if t == 0:
                issue_stage2()
    except _Done:
        return fin()
    if UPTO == 30:
        return fin()

    N = OWN
    for b in range(OWN):
        R = p32.get(); Tt = p32.get()
        p.dma("pool", dsof(R[1]), lambda e, R=R, b=b: e.indirect_dma_start(
            out=R[0][:NPG, 0:512], out_offset=None, in_=rec_o_all.rearrange("(q h) d -> q (h d)", h=8),
            in_offset=bass.IndirectOffsetOnAxis(ap=IDXO[:NPG, b:b + 1], axis=0)), [recb, cb, ib], [R[1]])
        p.dma("pool", dsof(Tt[1]), lambda e, Tt=Tt, b=b: e.indirect_dma_start(
            out=Tt[0][:NPG, 0:8], out_offset=None, in_=rec_t_all,
            in_offset=bass.IndirectOffsetOnAxis(ap=PTT[:NPG, b:b + 1], axis=0)), [recb, cb], [Tt[1]])
        sfx = psum.get()
        MM(sfx[0][:NPG, 0:8], tripos[:NPG, :NPG], Tt[0][:NPG, 0:8], True, True, [Tt[1], cb], [sfx[1]])
        cc_ = p32.get()
        ACT(cc_[0][:NPG, 0:8], sfx[0][:NPG, 0:8], AF.Exp, [sfx[1]], [cc_[1]])
        psum.put(sfx)
        TT("dve", R[0][:NPG, 0:512].rearrange("p (h d) -> p h d", h=8), R[0][:NPG, 0:512].rearrange("p (h d) -> p h d", h=8),
           cc_[0][:NPG, 0:8].unsqueeze(2).to_broadcast([NPG, 8, 64]), ALU.mult, [R[1], cc_[1]], [R[1]])
        p32.put(cc_); p32.put(Tt)
        att = psum.get()
        for j in range(4):
            MM(att[0][:, j:j + 1], R[0][:NPG, j * 128:(j + 1) * 128], ones32[:NPG, 0:1], True, True, [R[1], cb], [att[1]])
        for j in range(4):
            CP("dve", MIXs[:, 4 + j, b:b + 1], att[0][:, j:j + 1], [att[1]], [MIXsb[4 + j]])
        psum.put(att)
        p32.put(R)
    outproj("out0", MIXs, MIXsb, Xs, Xsb, N)
    ffn(0, Xs, Xsb, Hs, Hsb, N, GFs, GFsb, 1)
    rmsnorm(Xs, Xsb, lambda c: GMIX[:, 1, c:c + 1], Hs, Hsb, N)
    for pc in range(5):
        view, sbuf = wload("in1", pc * 512, 512)
        bank = psum.get()
        for c in range(8):
            MM(bank[0][0:N, :], Hs[:, c, 0:N], view[:, c, :], c == 0, c == 7, [sbuf, Hsb[c]], [bank[1]])
        if pc < 2:
            ACT(TOKS[0:N, pc * 512:(pc + 1) * 512], bank[0][0:N, :], AF.Gelu, [bank[1]], [TOKSb])
        else:
            CP("dve", TOKS[0:N, pc * 512:(pc + 1) * 512], bank[0][0:N, :], [bank[1]], [TOKSb])
        psum.put(bank)
    DMA("pool", d_toks, chv, TOKS[0:N, 512:1024], [TOKSb], ())
    TT("dve", TOKS[0:N, 1024:1536], TOKS[0:N, 1024:1536], TOKS[0:N, 2048:2560], ALU.mult, [TOKSb], [TOKSb])
    DMA("pool", d_toks, convs[:, 1, :], TOKS[0:N, 1024:1536], [TOKSb], ())
    DMA("pool", d_misc, convs[:, 0, :], scst[:, 1, :], (), ())
    FM = p32.get()
    for sec in range(5):
        view, sbuf = wload("in1", sec * 512, 512)
        bank = psum.get()
        for g in range(4):
            for c in range(8):
                MM(bank[0][:, g * 4:g * 4 + N], view[:, c, g * 128:(g + 1) * 128], Hs[:, c, 0:N], c == 0, c == 7,
                   [sbuf, Hsb[c]], [bank[1]])
        if sec < 2:
            ACT(FM[0][:, sec * 16:sec * 16 + 16], bank[0][:, 0:16], AF.Gelu, [bank[1]], [FM[1]])
        else:
            CP("dve", FM[0][:, sec * 16:sec * 16 + 16], bank[0][:, 0:16], [bank[1]], [FM[1]])
        psum.put(bank)
    WS0 = p32.get()
    for g in range(4):
        DMA("sp", dsof(WS0[1]), WS0[0][:, g:g + 1], w_s[g * 128:g * 128 + 1, 0:1].to_broadcast([128, 1]), (), [WS0[1]])
        DMA("sp", dsof(WS0[1]), WS0[0][:, 4 + g:5 + g], b_s[g:g + 1, 0:1].to_broadcast([128, 1]), (), [WS0[1]])
    for g in range(4):
        tmpg = p32.get()
        TS("dve", tmpg[0][:, 0:N], FM[0][:, 16 + g * 4:16 + g * 4 + N], WS0[0][:, g:g + 1], WS0[0][:, 4 + g:5 + g], ALU.mult, ALU.add,
           [FM[1], WS0[1]], [tmpg[1]])
        TT("dve", MIXs[:, g, 0:N], tmpg[0][:, 0:N], FM[0][:, g * 4:g * 4 + N], ALU.mult, [tmpg[1], FM[1]], [MIXsb[g]])
        p32.put(tmpg)
    p32.put(WS0)
    cst = p32.get()
    DMA("sp", dsof(cst[1]), cst[0][0:OWN * 2, 0:512], scst.rearrange("b j c -> (b j) c"), (), [cst[1]])
    bank = psum.get()
    for g in range(4):
        TR(bank[0][:, g * 8:g * 8 + OWN * 2], cst[0][0:OWN * 2, g * 128:(g + 1) * 128], [cst[1]], [bank[1]])
    CST = p32.get()
    CP("dve", CST[0][:, 0:32], bank[0][:, 0:32], [bank[1]], [CST[1]])
    psum.put(bank); p32.put(cst)
    for g in range(4):
        zf = p32.get(); cvv = p32.get()
        TT("dve", zf[0][:, 0:N], FM[0][:, 32 + g * 4:32 + g * 4 + N], FM[0][:, 64 + g * 4:64 + g * 4 + N], ALU.mult, [FM[1]], [zf[1]])
        st3 = CST[0][:, g * 8:g * 8 + 8].rearrange("p (b j) -> p b j", j=2)
        TS("dve", cvv[0][:, 0:N], st3[:, :, 0], CW[:, 0, g:g + 1], None, ALU.mult, None, [CST[1], cb], [cvv[1]])
        STT(cvv[0][:, 0:N], st3[:, :, 1], CW[:, 1, g:g + 1], cvv[0][:, 0:N], ALU.mult, ALU.add, [CST[1], cvv[1], cb], [cvv[1]])
        STT(cvv[0][:, 0:N], zf[0][:, 0:N], CW[:, 2, g:g + 1], cvv[0][:, 0:N], ALU.mult, ALU.add, [zf[1], cvv[1], cb], [cvv[1]])
        TT("dve", MIXs[:, 4 + g, 0:N], cvv[0][:, 0:N], FM[0][:, 48 + g * 4:48 + g * 4 + N], ALU.mult, [cvv[1], FM[1]], [MIXsb[4 + g]])
        p32.put(zf); p32.put(cvv)
    p32.put(CST); p32.put(FM)
    outproj("out1", MIXs, MIXsb, Xs, Xsb, N)
    ffn(1, Xs, Xsb, Hs, Hsb, N, GFs, GFsb, 1)
    rstd = rmsnorm(Xs, Xsb, None, None, None, N, out32=True)
    for hf in range(2):
        bank = psum.get()
        for c4 in range(4):
            c = hf * 4 + c4
            yn = p32.get()
            STT(yn[0][:, 0:N], Xs[:, c, 0:N], GFIN[:, c:c + 1], rstd[0][:, 0:N], ALU.mult, ALU.mult, [Xsb[c], rstd[1], cb], [yn[1]])
            TR(bank[0][0:N, c4 * 128:(c4 + 1) * 128], yn[0][:, 0:N], [yn[1]], [bank[1]])
            p32.put(yn)
        stg = p32.get()
        CP("dve", stg[0][0:N, 0:512], bank[0][0:N, :], [bank[1]], [stg[1]])
        psum.put(bank)
        DMA("pool", dsof(stg[1]), ys[:, hf * 512:(hf + 1) * 512], stg[0][0:N, 0:512], [stg[1]], ())
        p32.put(stg)
    p32.put(rstd)
    p.wait_all_dma("sp")
    p.emit()
    return nc


def make_in_maps(cfg, inputs):
    NCORES = cfg["NCORES"]; S = cfg["S"]; DB = cfg["DB"]; NPG = cfg["NPG"]; PPC = cfg["PPC"]
    OWN = DB // NCORES
    f = lambda a: np.ascontiguousarray(np.asarray(a))
    x_prompt = f(inputs["x_prompt"]); x_sample = f(inputs["x_sample"])[:, 0, :]
    cache_k = np.asarray(inputs["cache_k"])[0].reshape(-1, 128 * 512)
    cache_v = np.asarray(inputs["cache_v"])[0].reshape(-1, 128 * 512)
    state_pool = f(inputs["state_pool"])[0]; state_conv = f(inputs["state_conv"])[0]
    pt = f(inputs["page_table"]).astype(np.int32)
    common = dict(
        norm_mix=f(inputs["norm_mix"]), norm_ffn=f(inputs["norm_ffn"]), norm_final=f(inputs["norm_final"]).reshape(1, D),
        w_in0=f(inputs["ab_w_in"])[0], sb_bias=f(inputs["ab_sb_bias"]).reshape(1, 8), w_pool=f(inputs["ab_w_pool"])[0].reshape(512, 128),
        pool_scale=f(inputs["ab_pool_scale"]).reshape(1, 512), w_out0=f(inputs["ab_w_out"])[0], w_in1=f(inputs["cd_w_in"])[0],
        w_s=f(inputs["cd_w_s"])[0].reshape(512, 128), b_s=f(inputs["cd_b_s"])[0], conv_w=f(inputs["cd_conv_w"])[0],
        w_out1=f(inputs["cd_w_out"])[0])
    for l in range(2):
        common["w_gate%d" % l] = f(inputs["ffn_w_gate"])[l]
        common["w_up%d" % l] = f(inputs["ffn_w_up"])[l]
        common["w_down%d" % l] = f(inputs["ffn_w_down"])[l]
    maps = []
    for c in range(NCORES):
        order = np.roll(np.arange(DB), -c * OWN)
        m = dict(common)
        m["xp"] = x_prompt[c]
        m["xs"] = f(x_sample[order])
        NSPLIT = cfg.get("NSPLIT", 1); PPS = PPC // NSPLIT
        for i in range(NSPLIT):
            m["ck%d" % i] = f(cache_k[c * PPC + i * PPS:c * PPC + (i + 1) * PPS]).reshape(PPS * 128, 512)
            m["cv%d" % i] = f(cache_v[c * PPC + i * PPS:c * PPC + (i + 1) * PPS]).reshape(PPS * 128, 512)
        m["spst"] = f(state_pool[order[:OWN]]); m["scst"] = f(state_conv[order[:OWN]])
        m["ptT"] = f(pt[order].T)
        m["pbase"] = np.full((128, 1), float(c * PPC), np.float32)
        maps.append(m)
    return maps


def gather_outputs(cfg, res):
    NCORES = cfg["NCORES"]; S = cfg["S"]; DB = cfg["DB"]
    OWN = DB // NCORES
    R = res
    cat = lambda k: np.stack([r[k] for r in R])
    y_prompt = cat("yp")
    y_sample = np.concatenate([r["ys"] for r in R], 0).reshape(DB, 1, D)
    k_prompt = cat("kp").reshape(1, NCORES, S, 8, 64); v_prompt = cat("vp").reshape(1, NCORES, S, 8, 64)
    k_sample = np.concatenate([r["ksm"] for r in R], 0).reshape(1, DB, 1, 8, 64)
    v_sample = np.concatenate([r["vsm"] for r in R], 0).reshape(1, DB, 1, 8, 64)
    pool_prompt = cat("poolp")[None]; pool_sample = np.concatenate([r["pools"] for r in R], 0)[None]
    conv_prompt = cat("convp")[None]; conv_sample = np.concatenate([r["convs"] for r in R], 0)[None]
    chunk_v = np.concatenate([r["chv"] for r in R], 0).reshape(1, DB, 1, 512)
    return tuple(np.ascontiguousarray(a.astype(np.float32)) for a in
                 (y_prompt, y_sample, k_prompt, v_prompt, k_sample, v_sample, pool_prompt, pool_sample, conv_prompt,
                  conv_sample, chunk_v))


def kernel(**inputs):
    cfg = default_cfg()
    nc = build(cfg)
    maps = make_in_maps(cfg, inputs)
    res = run_bass_kernel_spmd(nc, maps, core_ids=list(range(cfg["NCORES"])))
    return gather_outputs(cfg, res.results)
```

```python
import numpy as np
from contextlib import ExitStack
import concourse.bass as bass
import concourse.mybir as mybir
from concourse.bass_utils import run_bass_kernel_spmd

F32 = mybir.dt.float32
BF16 = mybir.dt.bfloat16
I32 = mybir.dt.int32
AF = mybir.ActivationFunctionType
ALU = mybir.AluOpType
AX = mybir.AxisListType

CENG = ("pe", "act", "dve", "pool")
ALLENG = ("pe", "act", "dve", "pool", "sp")


class Buf:
    __slots__ = ("name", "w", "r")

    def __init__(self, name=""):
        self.name = name
        self.w = None
        self.r = {}


class DSem:
    __slots__ = ("sem", "cnt", "name", "unit")

    def __init__(self, sem, name, unit=16):
        self.sem = sem
        self.cnt = 0
        self.name = name
        self.unit = unit


class Op:
    __slots__ = ("eng", "fn", "waits", "need", "sig", "ds", "isdma")

    def __init__(self, eng, fn, isdma=False, ds=None):
        self.eng = eng
        self.fn = fn
        self.waits = []
        self.need = False
        self.sig = 0
        self.ds = ds
        self.isdma = isdma


class Prog:
    def __init__(self, nc):
        self.nc = nc
        self.stack = ExitStack()
        self.ops = {e: [] for e in ALLENG}
        self.esem = {}
        self.dsems = []
        self.nops = 0

    def sb(self, name, shape, dtype):
        return self.stack.enter_context(self.nc.sbuf_tensor(name, list(shape), dtype))

    def ps(self, name, shape, dtype=F32):
        return self.stack.enter_context(self.nc.psum_tensor(name, list(shape), dtype))

    def dsem(self, name, unit=16):
        s = self.stack.enter_context(self.nc.semaphore(name))
        d = DSem(s, name, unit)
        self.dsems.append(d)
        return d

    def _deps(self, op, reads, writes):
        def add(tok, kind):
            if tok is None:
                return
            if tok[0] == "c":
                prod = tok[1]
                if (not op.isdma) and prod.eng == op.eng and kind != "raw":
                    return
                prod.need = True
                op.waits.append(("c", prod))
            else:
                ds = tok[1]
                op.waits.append(("d", ds, ds.unit * ds.cnt))
        for b in reads:
            add(b.w, "raw")
        for b in writes:
            add(b.w, "waw")
            for t in b.r.values():
                add(t, "war")
        tok = ("d", op.ds) if op.isdma else ("c", op)
        for b in writes:
            b.w = tok
            b.r = {}
        rkey = id(op.ds) if op.isdma else op.eng
        for b in reads:
            b.r[rkey] = tok

    def op(self, eng, fn, reads=(), writes=()):
        o = Op(eng, fn)
        self._deps(o, reads, writes)
        self.ops[eng].append(o)
        self.nops += 1
        return o

    def dma(self, q, ds, fn, reads=(), writes=()):
        o = Op(q, fn, isdma=True, ds=ds)
        self._deps(o, reads, writes)
        ds.cnt += 1
        self.ops[q].append(o)
        self.nops += 1
        return o

    def wait_all_dma(self, q="sp"):
        o = Op(q, None)
        for d in self.dsems:
            if d.cnt:
                o.waits.append(("d", d, d.unit * d.cnt))
        self.ops[q].append(o)

    def emit(self):
        nc = self.nc
        for e in CENG:
            self.esem[e] = self.stack.enter_context(nc.semaphore("es_" + e))
        for e in ALLENG:
            n = 0
            for o in self.ops[e]:
                if o.need and not o.isdma:
                    n += 1
                    o.sig = n
        ops = self.ops
        esem = self.esem

        def stream(ename, eng):
            waited = {}
            for o in ops[ename]:
                for w in o.waits:
                    if w[0] == "c":
                        prod = w[1]
                        key = prod.eng
                        val = prod.sig
                        sem = esem[prod.eng]
                    else:
                        key = w[1]
                        val = w[2]
                        sem = w[1].sem
                    if waited.get(key, 0) >= val:
                        continue
                    waited[key] = val
                    eng.wait_ge(sem, val)
                if o.fn is None:
                    continue
                ins = o.fn(eng)
                if o.isdma:
                    ins.then_inc(o.ds.sem, o.ds.unit)
                elif o.need:
                    ins.then_inc(esem[ename], 1)

        with nc.Block() as block:
            @block.sync
            def _(e):
                stream("sp", e)

            @block.scalar
            def _(e):
                stream("act", e)

            @block.tensor
            def _(e):
                stream("pe", e)

            @block.vector
            def _(e):
                stream("dve", e)

            @block.gpsimd
            def _(e):
                stream("pool", e)
        self.stack.close()


class TPool:
    def __init__(self, items):
        self.free = list(items)

    def get(self):
        assert self.free, "tile pool exhausted"
        return self.free.pop(0)

    def put(self, it):
        self.free.append(it)


D = 1024
DFF = 2816
KFF = DFF // 128
EPS = 1e-6


def default_cfg():
    return dict(NCORES=8, S=4096, DB=32, NPG=128, PPC=640, NSPLIT=2)


def build(cfg):
    NCORES = cfg["NCORES"]; S = cfg["S"]; DB = cfg["DB"]; NPG = cfg["NPG"]; PPC = cfg["PPC"]
    NT = S // 512
    OWN = DB // NCORES
    assert OWN == 4 and DB <= 32
    G = 8
    assert PPC % G == 0
    NGRP = PPC // G

    nc = bass.Bass("TRN2", target_bir_lowering=False)

    def din(name, shape, dt=F32):
        return nc.dram_tensor(name, list(shape), dt, kind="ExternalInput").ap()

    def dout(name, shape, dt=F32):
        return nc.dram_tensor(name, list(shape), dt, kind="ExternalOutput").ap()

    def dint(name, shape, dt=F32):
        return nc.dram_tensor(name, list(shape), dt, kind="Internal").ap()

    xp = din("xp", [S, D]); xs = din("xs", [DB, D])
    NSPLIT = cfg.get("NSPLIT", 1)
    PPS = PPC // NSPLIT
    assert PPC % NSPLIT == 0 and PPS % G == 0
    cks = [din("ck%d" % i, [PPS * 128, 512]) for i in range(NSPLIT)]
    cvs = [din("cv%d" % i, [PPS * 128, 512]) for i in range(NSPLIT)]
    spst = din("spst", [OWN, 15, 512]); scst = din("scst", [OWN, 2, 512])
    ptT = din("ptT", [NPG, DB], I32); pbase = din("pbase", [128, 1])
    norm_mix = din("norm_mix", [2, D]); norm_ffn = din("norm_ffn", [2, D]); norm_final = din("norm_final", [1, D])
    w_in0 = din("w_in0", [D, 2048]); sb_bias = din("sb_bias", [1, 8]); w_pool = din("w_pool", [512, 128])
    pool_scale = din("pool_scale", [1, 512]); w_out0 = din("w_out0", [D, D])
    w_in1 = din("w_in1", [D, 2560]); w_s = din("w_s", [512, 128]); b_s = din("b_s", [4, 128])
    conv_w = din("conv_w", [3, 512]); w_out1 = din("w_out1", [D, D])
    w_gate = [din("w_gate%d" % l, [D, DFF]) for l in range(2)]
    w_up = [din("w_up%d" % l, [D, DFF]) for l in range(2)]
    w_down = [din("w_down%d" % l, [DFF, D]) for l in range(2)]

    yp = dout("yp", [S, D]); ys = dout("ys", [OWN, D])
    kp = dout("kp", [S, 512]); vp = dout("vp", [S, 512])
    ksm = dout("ksm", [OWN, 512]); vsm = dout("vsm", [OWN, 512])
    poolp = dout("poolp", [15, 512]); pools = dout("pools", [OWN, 15, 512])
    convp = dout("convp", [2, 512]); convs = dout("convs", [OWN, 2, 512])
    chv = dout("chv", [OWN, 512])
    DBG = cfg.get("dbg", False)
    if DBG:
        dbg_wmt = dout("dbg_wmt", [128, 512], BF16)
        dbg_mix = dout("dbg_mix", [128, 4096], BF16)
        dbg_mix0 = dout("dbg_mix0", [128, 4096], BF16)
        dbg_t32 = dout("dbg_t32", [128, 512], F32)
        dbg_t32b = dout("dbg_t32b", [128, 512], F32)
        dbg_wmt0 = dout("dbg_wmt0", [128, 512], BF16)

    WS = {}
    wsrc = {"in0": (w_in0, 8, 2048), "out0": (w_out0, 8, D), "in1": (w_in1, 8, 2560), "out1": (w_out1, 8, D)}
    for l in range(2):
        wsrc["gate%d" % l] = (w_gate[l], 8, DFF)
        wsrc["up%d" % l] = (w_up[l], 8, DFF)
        wsrc["down%d" % l] = (w_down[l], KFF, D)
    for k, (src, kc, m) in wsrc.items():
        WS[k] = dint("wbf_" + k, [128, kc, m], BF16)
    rec_o_loc = dint("rec_o_loc", [PPC * 8, 64]); rec_t_loc = dint("rec_t_loc", [PPC, 8])
    CPG = cfg.get("CPG", min(128, PPC))
    assert PPC % CPG == 0
    NCH = PPC // CPG
    rec_o_all = dint("rec_o_all", [NCH * NCORES * CPG * 8, 64]); rec_t_all = dint("rec_t_all", [NCORES * PPC, 8])

    p = Prog(nc)
    cfg["_sbuf0"] = nc.sbuf_bytes_remaining
    psum = TPool([(p.ps("bank%d" % i, [128, 512]), Buf("bank%d" % i)) for i in range(8)])
    P32W = 544
    NP32 = cfg.get("NP32", 12)
    NP16 = cfg.get("NP16", 20)
    p32 = TPool([(p.sb("p32_%d" % i, [128, P32W], F32), Buf("p32_%d" % i)) for i in range(NP32)])
    p16 = TPool([(p.sb("p16_%d" % i, [128, 512], BF16), Buf("p16_%d" % i)) for i in range(NP16)])
    X = p.sb("X", [128, 8, 512], F32); Xb = [Buf("X%d" % c) for c in range(8)]
    H = p.sb("H", [128, 8, 512], BF16); Hb = [Buf("H%d" % c) for c in range(8)]
    QT = p.sb("QT", [128, 4, 512], BF16); QTb = [Buf() for _ in range(4)]
    MIX = p.sb("MIX", [128, 8, 512], BF16); MIXb = [Buf() for _ in range(8)]
    GF = None; GFb = None
    ARENA = p.sb("ARENA", [128, 16384], F32)
    AR16 = ARENA[:].bitcast(BF16)
    KT = AR16[:, 0:16384].rearrange("p (j n) -> p j n", j=4)
    VB = AR16[:, 16384:32768].rearrange("p (b f) -> p b f", f=512)
    KTb = [[Buf() for _ in range(NT)] for _ in range(4)]
    VBb = [Buf() for _ in range(32)]
    phase = Buf("phase")
    NSLOT = 3
    WR = [(p.sb("wr%d" % i, [128, 4096], BF16), Buf("wr%d" % i)) for i in range(NSLOT)]
    wr_i = [0]
    Xs = p.sb("Xs", [128, 8, 32], F32); Xsb = [Buf() for _ in range(8)]
    Hs = p.sb("Hs", [128, 8, 32], BF16); Hsb = [Buf() for _ in range(8)]
    MIXs = p.sb("MIXs", [128, 8, 4], BF16); MIXsb = [Buf() for _ in range(8)]
    GFs = p.sb("GFs", [128, 22, 4], BF16); GFsb = [Buf() for _ in range(22)]
    TOKS = p.sb("TOKS", [32, 2560], F32); TOKSb = Buf()
    QSb16 = p.sb("QSb16", [DB, 512], BF16); QSbb = Buf()
    ident = p.sb("ident", [128, 128], F32); identb = p.sb("identb", [128, 128], BF16)
    trineg = p.sb("trineg", [128, 128], BF16); onesneg = p.sb("onesneg", [128, 128], BF16)
    onesmean = p.sb("onesmean", [128, 128], BF16)
    tripos = p.sb("tripos", [128, 128], F32)
    trimask = p.sb("trimask", [128, 128], F32)
    ones32 = p.sb("ones32", [128, 128], F32)
    negm = p.sb("negm", [128, 4, 512], BF16)
    cb = Buf("const")
    GMIX = p.sb("GMIX", [128, 2, 8], F32); GFFN = p.sb("GFFN", [128, 2, 8], F32); GFIN = p.sb("GFIN", [128, 8], F32)
    BIAS = p.sb("BIAS", [128, 8], F32); PSC = p.sb("PSC", [128, 4], F32); CW = p.sb("CW", [128, 3, 4], F32)
    WPb = p.sb("WPb", [128, 4, 128], BF16); WMT = p.sb("WMT", [128, 4, 128], BF16)
    BSH = p.sb("BSH", [2, 4, 128], BF16); BSL = p.sb("BSL", [2, 4, 128], BF16)
    ones2 = p.sb("ones2", [2, 128], BF16)
    RC0 = p.sb("RC0", [128, 16], F32)
    HALO = p.sb("HALO", [128, 4, 16], F32); HALOb = Buf()
    ZHALO = p.sb("ZHALO", [128, 4, 2], F32); ZHALOb = Buf()
    PTT = p.sb("PTT", [128, DB], I32); PTF = p.sb("PTF", [128, DB], F32); PBASE = p.sb("PBASE", [128, 1], F32)
    OH = p.sb("OH", [DB, PPC], BF16); OHb = Buf()
    IOTAP = p.sb("IOTAP", [128, PPC], F32)
    ESEL = p.sb("ESEL", [128, DB, DB], BF16)
    BIASG = p.sb("BIASG", [128, G, 8], F32)
    BD128 = p.sb("BD128", [128, 512], F32); SEL4 = p.sb("SEL4", [128, 4], F32)

    d_const = p.dsem("d_const"); d_misc = p.dsem("d_misc"); d_toks = p.dsem("d_toks")
    _bds = {}

    def dsof(buf):
        if id(buf) not in _bds:
            _bds[id(buf)] = p.dsem("db%d" % len(_bds))
        return _bds[id(buf)]
    d_wr = [p.dsem("d_wr%d" % i) for i in range(NSLOT)]
    d_cc = p.dsem("d_cc", unit=1)
    d_pc = p.dsem("d_pc")

    def MM(out, lhsT, rhs, start, stop, reads, writes):
        return p.op("pe", lambda e: e.matmul(out, lhsT=lhsT, rhs=rhs, start=start, stop=stop), reads, writes)

    def TR(out, in_, reads, writes):
        k = in_.shape[0]
        return p.op("pe", lambda e: e.transpose(out, in_, ident[:k, :k]), list(reads) + [cb], writes)

    def ACT(out, in_, func, reads, writes, bias=None, scale=None):
        kw = {}
        if bias is not None:
            kw["bias"] = bias
        if scale is not None:
            kw["scale"] = scale
        return p.op("act", lambda e: e.activation(out=out, in_=in_, func=func, **kw), reads, writes)

    def TT(eng, out, in0, in1, op, reads, writes):
        return p.op(eng, lambda e: e.tensor_tensor(out=out, in0=in0, in1=in1, op=op), reads, writes)

    def TS(eng, out, in0, s1, s2, op0, op1, reads, writes):
        if s2 is None:
            return p.op(eng, lambda e: e.tensor_scalar(out=out, in0=in0, scalar1=s1, scalar2=None, op0=op0), reads, writes)
        return p.op(eng, lambda e: e.tensor_scalar(out=out, in0=in0, scalar1=s1, scalar2=s2, op0=op0, op1=op1), reads, writes)

    def STT(out, in0, scalar, in1, op0, op1, reads, writes):
        return p.op("dve", lambda e: e.scalar_tensor_tensor(out=out, in0=in0, scalar=scalar, in1=in1, op0=op0, op1=op1),
                    reads, writes)

    def CP(eng, out, in_, reads, writes):
        if eng == "act":
            return ACT(out, in_, AF.Copy, reads, writes)
        return p.op(eng, lambda e: e.tensor_copy(out=out, in_=in_), reads, writes)

    def MS(eng, ap, val, writes):
        return p.op(eng, lambda e: e.memset(ap, val), (), writes)

    def DMA(q, ds, out, in_, reads, writes, slow=False):
        if slow:
            return p.dma(q, ds, lambda e: e.dma_start(out=out, in_=in_, allow_slow_non_contiguous=True), reads, writes)
        return p.dma(q, ds, lambda e: e.dma_start(out=out, in_=in_), reads, writes)

    ev_i = [0]

    def evac_eng():
        ev_i[0] += 1
        return "act" if ev_i[0] % 2 else "dve"

    UPTO = cfg.get("upto", 99)
    print("sbuf bytes remaining", cfg["_sbuf0"], "->", nc.sbuf_bytes_remaining)

    class _Done(Exception):
        pass

    def fin():
        p.wait_all_dma("sp")
        p.emit()
        return nc

    MS("pool", ident[:], 1.0, [cb])
    p.op("pool", lambda e: e.affine_select(out=ident[:], in_=ident[:], pattern=[[-1, 128]], compare_op=ALU.is_equal,
                                           fill=0.0, base=0, channel_multiplier=1), [cb], [cb])
    CP("pool", identb[:], ident[:], [cb], [cb])
    MS("pool", onesneg[:], -1.0, [cb])
    MS("pool", ones32[:], 1.0, [cb])
    MS("pool", onesmean[:], 1.0 / 1024.0, [cb])
    MS("pool", ones2[:], 1.0, [cb])
    p.op("pool", lambda e: e.affine_select(out=trineg[:], in_=onesneg[:], pattern=[[-1, 128]], compare_op=ALU.is_ge,
                                           fill=0.0, base=0, channel_multiplier=1), [cb], [cb])
    p.op("pool", lambda e: e.affine_select(out=tripos[:], in_=ones32[:], pattern=[[-1, 128]], compare_op=ALU.is_gt,
                                           fill=0.0, base=0, channel_multiplier=1), [cb], [cb])
    p.op("pool", lambda e: e.affine_select(out=trimask[:], in_=ones32[:], pattern=[[1, 128]], compare_op=ALU.is_ge,
                                           fill=0.0, base=0, channel_multiplier=-1), [cb], [cb])
    big = p32.get()
    MS("pool", big[0][:, 0:512], -30000.0, [big[1]])
    for r in range(4):
        p.op("pool", lambda e, r=r: e.affine_select(out=negm[:, r, :], in_=big[0][:, 0:512], pattern=[[-1, 512]],
                                                    compare_op=ALU.is_ge, fill=0.0, base=r * 128, channel_multiplier=1),
             [big[1]], [cb])
    p32.put(big)
    p.op("pool", lambda e: e.iota(RC0[:], pattern=[[1, 16]], base=1, channel_multiplier=0,
                                  allow_small_or_imprecise_dtypes=True), (), [cb])
    p.op("dve", lambda e: e.reciprocal(out=RC0[:], in_=RC0[:]), [cb], [cb])
    p.op("pool", lambda e: e.iota(IOTAP[:], pattern=[[1, PPC]], base=0, channel_multiplier=0,
                                  allow_small_or_imprecise_dtypes=True), (), [cb])
    MS("pool", ESEL[:], 1.0, [cb])
    p.op("pool", lambda e: e.affine_select(out=ESEL[:], in_=ESEL[:], pattern=[[1, DB], [-1, DB]], compare_op=ALU.is_equal,
                                           fill=0.0, base=0, channel_multiplier=0), [cb], [cb])
    MS("pool", BD128[:], 1.0, [cb])
    MS("pool", SEL4[:], 0.0, [cb])
    for q in range(4):
        p.op("pool", lambda e, q=q: e.affine_select(out=BD128[q * 32:(q + 1) * 32, :].rearrange("p (h d) -> p h d", h=8),
                                                    in_=BD128[q * 32:(q + 1) * 32, :].rearrange("p (h d) -> p h d", h=8),
                                                    pattern=[[-1, 8], [0, 64]], compare_op=ALU.is_equal, fill=0.0, base=0,
                                                    channel_multiplier=1), [cb], [cb])
        MS("pool", SEL4[q * 32:(q + 1) * 32, q:q + 1], 1.0, [cb])
    MS("pool", HALO[:], 0.0, [HALOb])
    MS("pool", ZHALO[:], 0.0, [ZHALOb])
    for l in range(2):
        DMA("sp", d_const, GMIX[:, l, :], norm_mix[l].rearrange("(c q) -> q c", q=128), (), [cb], slow=True)
        DMA("sp", d_const, GFFN[:, l, :], norm_ffn[l].rearrange("(c q) -> q c", q=128), (), [cb], slow=True)
    DMA("sp", d_const, GFIN[:], norm_final[0].rearrange("(c q) -> q c", q=128), (), [cb], slow=True)
    DMA("sp", d_const, BIAS[:], sb_bias.to_broadcast([128, 8]), (), [cb])
    DMA("sp", d_const, PSC[:], pool_scale[0].rearrange("(c q) -> q c", q=128), (), [cb], slow=True)
    for j in range(3):
        DMA("sp", d_const, CW[:, j, :], conv_w[j].rearrange("(c q) -> q c", q=128), (), [cb], slow=True)
    DMA("sp", d_const, PTT[:NPG, :], ptT, (), [cb])
    DMA("sp", d_const, PBASE[:], pbase, (), [cb])
    t32 = p32.get()
    DMA("sp", dsof(t32[1]), t32[0][:, 0:512].rearrange("p (g d) -> p g d", g=4), w_pool.rearrange("(g c) d -> c g d", g=4), (), [t32[1]])
    CP("dve", WPb[:], t32[0][:, 0:512].rearrange("p (g d) -> p g d", g=4), [t32[1]], [cb])
    p32.put(t32)
    t32 = p32.get()
    DMA("sp", dsof(t32[1]), t32[0][:, 0:512].rearrange("p (g d) -> p g d", g=4), w_s.rearrange("(g t) s -> t g s", g=4), (), [t32[1]])
    tb_ = psum.get()
    for g in range(4):
        TR(tb_[0][:, g * 128:(g + 1) * 128], t32[0][:, g * 128:(g + 1) * 128], [t32[1]], [tb_[1]])
    t32b = p32.get()
    CP("dve", t32b[0][:, 0:512], tb_[0][:, :], [tb_[1]], [t32b[1]])
    psum.put(tb_)
    for g in range(4):
        TT("dve", WMT[:, g, :], t32b[0][:, g * 128:(g + 1) * 128], trimask[:], ALU.mult, [t32b[1], cb], [cb])
    if DBG:
        DMA("sp", d_misc, dbg_wmt0, WMT[:].rearrange("p g t -> p (g t)"), [cb], ())
        DMA("sp", d_misc, dbg_t32, t32[0][:, 0:512], [t32[1]], ())
        DMA("sp", d_misc, dbg_t32b, t32b[0][:, 0:512], [t32b[1]], ())
    p32.put(t32); p32.put(t32b)
    t32 = p32.get(); t32b = p32.get()
    DMA("sp", dsof(t32[1]), t32[0][0:1, 0:512], b_s.rearrange("(o g) t -> o (g t)", o=1), (), [t32[1]])
    CP("dve", BSH[0:1, :, :].rearrange("p g t -> p (g t)"), t32[0][0:1, 0:512], [t32[1]], [cb])
    CP("dve", t32b[0][0:1, 0:512], BSH[0:1, :, :].rearrange("p g t -> p (g t)"), [cb], [t32b[1]])
    TT("dve", t32b[0][0:1, 0:512], t32[0][0:1, 0:512], t32b[0][0:1, 0:512], ALU.subtract, [t32[1], t32b[1]], [t32b[1]])
    CP("dve", BSL[0:1, :, :].rearrange("p g t -> p (g t)"), t32b[0][0:1, 0:512], [t32b[1]], [cb])
    p32.put(t32); p32.put(t32b)
    for g in range(G):
        CP("pool", BIASG[:, g, :], BIAS[:], [cb], [cb])

    wsb = {k: [Buf("ws_%s_%d" % (k, c)) for c in range(wsrc[k][1])] for k in WS}
    for k, (src, kc, m) in wsrc.items():
        for c in range(kc):
            DMA("pool", d_pc, WS[k][:, c, :], src[c * 128:(c + 1) * 128, :], (), [wsb[k][c]])

    def wload(k, c0, mc):
        kc = wsrc[k][1]
        i = wr_i[0] % NSLOT
        wr_i[0] += 1
        slot, sbuf = WR[i]
        view = slot[:, 0:kc * mc].rearrange("p (k m) -> p k m", k=kc)
        DMA("sp", d_wr[i], view, WS[k][:, :, c0:c0 + mc], wsb[k], [sbuf])
        return view, sbuf

    def dense_fm(k, c0, nm, rhs, rhs_bufs, N, epi, mc=512):
        kc = wsrc[k][1]
        mper = mc // 128
        m = 0
        while m < nm:
            nmm = min(mper, nm - m)
            view, sbuf = wload(k, c0 + m * 128, nmm * 128)
            for mi in range(nmm):
                bank = psum.get()
                for c in range(kc):
                    MM(bank[0][:, 0:N], view[:, c, mi * 128:(mi + 1) * 128], rhs(c), c == 0, c == kc - 1,
                       [sbuf, rhs_bufs[c]], [bank[1]])
                epi(m + mi, bank)
                psum.put(bank)
            m += nmm

    def rmsnorm(Xt, Xbufs, gcol, Ht, Hbufs, N, out32=None):
        ms = psum.get()
        for c in range(8):
            sq = p16.get()
            ACT(sq[0][:, 0:N], Xt[:, c, 0:N], AF.Square, [Xbufs[c]], [sq[1]])
            MM(ms[0][:, 0:N], onesmean[:], sq[0][:, 0:N], c == 0, c == 7, [sq[1], cb], [ms[1]])
            p16.put(sq)
        rstd = p32.get()
        ACT(rstd[0][:, 0:N], ms[0][:, 0:N], AF.Ln, [ms[1]], [rstd[1]], bias=EPS)
        psum.put(ms)
        ACT(rstd[0][:, 0:N], rstd[0][:, 0:N], AF.Exp, [rstd[1]], [rstd[1]], scale=-0.5)
        if out32 is not None:
            return rstd
        for c in range(8):
            STT(Ht[:, c, 0:N], Xt[:, c, 0:N], gcol(c), rstd[0][:, 0:N], ALU.mult, ALU.mult,
                [Xbufs[c], rstd[1], cb], [Hbufs[c]])
        p32.put(rstd)
        return None

    def ffn(l, Xt, Xbufs, Ht, Hbufs, N, GFt, GFbufs, nhalf):
        rmsnorm(Xt, Xbufs, lambda c: GFFN[:, l, c:c + 1], Ht, Hbufs, N)
        per = KFF // nhalf
        for hf in range(nhalf):
            m0 = hf * per
            mm = 0
            gfl = None
            if GFt is None:
                gfl = [p16.get() for _ in range(per)]
            while mm < per:
                nmm = min(4, per - mm)
                gview, gsb = wload("gate%d" % l, (m0 + mm) * 128, nmm * 128)
                uview, usb = wload("up%d" % l, (m0 + mm) * 128, nmm * 128)
                for mi in range(nmm):
                    bg_ = psum.get(); bu_ = psum.get()
                    for c in range(8):
                        MM(bg_[0][:, 0:N], gview[:, c, mi * 128:(mi + 1) * 128], Ht[:, c, 0:N], c == 0, c == 7,
                           [gsb, Hbufs[c]], [bg_[1]])
                    for c in range(8):
                        MM(bu_[0][:, 0:N], uview[:, c, mi * 128:(mi + 1) * 128], Ht[:, c, 0:N], c == 0, c == 7,
                           [usb, Hbufs[c]], [bu_[1]])
                    st = p32.get()
                    ACT(st[0][:, 0:N], bg_[0][:, 0:N], AF.Silu, [bg_[1]], [st[1]])
                    psum.put(bg_)
                    if gfl is not None:
                        TT("dve", gfl[mm + mi][0][:, 0:N], st[0][:, 0:N], bu_[0][:, 0:N], ALU.mult, [st[1], bu_[1]],
                           [gfl[mm + mi][1]])
                    else:
                        TT("dve", GFt[:, mm + mi, 0:N], st[0][:, 0:N], bu_[0][:, 0:N], ALU.mult, [st[1], bu_[1]],
                           [GFbufs[mm + mi]])
                    psum.put(bu_)
                    p32.put(st)
                mm += nmm
            dstep = 2 if per <= 16 else 1
            for mo in range(0, 8, dstep):
                i = wr_i[0] % NSLOT
                wr_i[0] += 1
                slot, sbuf = WR[i]
                view = slot[:, 0:per * 128 * dstep].rearrange("p (k m) -> p k m", k=per)
                DMA("sp", d_wr[i], view, WS["down%d" % l][:, m0:m0 + per, mo * 128:(mo + dstep) * 128], wsb["down%d" % l], [sbuf])
                for mi in range(dstep):
                    bank = psum.get()
                    for c in range(per):
                        if gfl is not None:
                            MM(bank[0][:, 0:N], view[:, c, mi * 128:(mi + 1) * 128], gfl[c][0][:, 0:N], c == 0, c == per - 1,
                               [sbuf, gfl[c][1]], [bank[1]])
                        else:
                            MM(bank[0][:, 0:N], view[:, c, mi * 128:(mi + 1) * 128], GFt[:, c, 0:N], c == 0, c == per - 1,
                               [sbuf, GFbufs[c]], [bank[1]])
                    TT("dve", Xt[:, mo + mi, 0:N], bank[0][:, 0:N], Xt[:, mo + mi, 0:N], ALU.add,
                       [bank[1], Xbufs[mo + mi]], [Xbufs[mo + mi]])
                    psum.put(bank)
            if gfl is not None:
                for it in gfl:
                    p16.put(it)

    def outproj(k, MIXt, MIXbufs, Xt, Xbufs, N):
        def epi(m, bank):
            TT("dve", Xt[:, m, 0:N], bank[0][:, 0:N], Xt[:, m, 0:N], ALU.add, [bank[1], Xbufs[m]], [Xbufs[m]])
        dense_fm(k, 0, 8, lambda c: MIXt[:, c, 0:N], MIXbufs, N, epi)

    if UPTO == 1:
        return fin()
    for hf in range(2):
        xt = p32.get()
        DMA("sp", dsof(xt[1]), xt[0][0:DB, 0:512], xs[:, hf * 512:(hf + 1) * 512], (), [xt[1]])
        bank = psum.get()
        for c4 in range(4):
            TR(bank[0][:, c4 * 32:c4 * 32 + DB], xt[0][0:DB, c4 * 128:(c4 + 1) * 128], [xt[1]], [bank[1]])
        p32.put(xt)
        for c4 in range(4):
            CP("dve", Xs[:, hf * 4 + c4, 0:DB], bank[0][:, c4 * 32:c4 * 32 + DB], [bank[1]], [Xsb[hf * 4 + c4]])
        psum.put(bank)
    rmsnorm(Xs, Xsb, lambda c: GMIX[:, 0, c:c + 1], Hs, Hsb, DB)
    for pc in range(4):
        view, sbuf = wload("in0", pc * 512, 512)
        bank = psum.get()
        for c in range(8):
            MM(bank[0][0:DB, :], Hs[:, c, 0:DB], view[:, c, :], c == 0, c == 7, [sbuf, Hsb[c]], [bank[1]])
        CP("dve", TOKS[0:DB, pc * 512:(pc + 1) * 512], bank[0][0:DB, :], [bank[1]], [TOKSb])
        psum.put(bank)
    DMA("pool", d_toks, ksm, TOKS[0:OWN, 1024:1536], [TOKSb], ())
    DMA("pool", d_toks, vsm, TOKS[0:OWN, 1536:2048], [TOKSb], ())
    DMA("pool", d_toks, pools[:, 14, :], TOKS[0:OWN, 0:512], [TOKSb], ())
    DMA("pool", d_misc, pools[:, 0:14, :], spst[:, 1:15, :], (), ())
    TS("dve", QSb16[0:DB, :], TOKS[0:DB, 512:1024], 0.125, None, ALU.mult, None, [TOKSb], [QSbb])
    stt_ = p32.get()
    DMA("sp", dsof(stt_[1]), stt_[0][0:OWN * 15, 0:512], spst.rearrange("b j c -> (b j) c"), (), [stt_[1]])
    ST = p32.get()
    bank = psum.get()
    for g in range(4):
        TR(bank[0][:, g * 64:g * 64 + OWN * 15], stt_[0][0:OWN * 15, g * 128:(g + 1) * 128], [stt_[1]], [bank[1]])
    p32.put(stt_)
    CP("dve", ST[0][:, 0:256], bank[0][:, 0:256], [bank[1]], [ST[1]])
    psum.put(bank)
    US = p32.get()
    view, sbuf = wload("in0", 0, 512)
    bank = psum.get()
    for g in range(4):
        for c in range(8):
            MM(bank[0][:, g * 4:g * 4 + OWN], view[:, c, g * 128:(g + 1) * 128], Hs[:, c, 0:OWN], c == 0, c == 7,
               [sbuf, Hsb[c]], [bank[1]])
    CP("dve", US[0][:, 0:16], bank[0][:, 0:16], [bank[1]], [US[1]])
    psum.put(bank)
    DS = p16.get()
    for g in range(4):
        w = 2 << g
        ssum = p32.get()
        p.op("dve", lambda e, g=g, w=w, ssum=ssum: e.tensor_reduce(
            out=ssum[0][:, 0:OWN], in_=ST[0][:, g * 64:g * 64 + OWN * 15].rearrange("p (b j) -> p b j", j=15)[:, :, 16 - w:15],
            axis=AX.X, op=ALU.add), [ST[1]], [ssum[1]])
        TT("dve", ssum[0][:, 0:OWN], ssum[0][:, 0:OWN], US[0][:, g * 4:g * 4 + OWN], ALU.add, [ssum[1], US[1]], [ssum[1]])
        STT(DS[0][:, g * 4:g * 4 + OWN], ssum[0][:, 0:OWN], 1.0 / w, US[0][:, g * 4:g * 4 + OWN], ALU.mult, ALU.subtract,
            [ssum[1], US[1]], [DS[1]])
        p32.put(ssum)
    bank = psum.get()
    for g in range(4):
        MM(bank[0][:, g * 4:g * 4 + OWN], WPb[:, g, :], DS[0][:, g * 4:g * 4 + OWN], True, True, [DS[1], cb], [bank[1]])
    for g in range(4):
        TS("dve", MIXs[:, g, 0:OWN], bank[0][:, g * 4:g * 4 + OWN], PSC[:, g:g + 1], None, ALU.mult, None,
           [bank[1], cb], [MIXsb[g]])
    psum.put(bank)
    p16.put(DS); p32.put(ST); p32.put(US)

    if UPTO == 2:
        return fin()
    CP("dve", PTF[:NPG, :], PTT[:NPG, :], [cb], [cb])
    TS("dve", PTF[:NPG, :], PTF[:NPG, :], PBASE[:NPG, 0:1], None, ALU.subtract, None, [cb], [cb])
    CH = 128
    for c0 in range(0, PPC, CH):
        cw = min(CH, PPC - c0)
        cmpb = Buf()
        CMP = AR16[:, 0:DB * cw].rearrange("p (b q) -> p b q", b=DB)
        p.op("dve", lambda e, CMP=CMP, c0=c0, cw=cw: e.tensor_tensor(
            out=CMP[:NPG], in0=IOTAP[:NPG, c0:c0 + cw].unsqueeze(1).to_broadcast([NPG, DB, cw]),
            in1=PTF[:NPG, :].unsqueeze(2).to_broadcast([NPG, DB, cw]), op=ALU.is_equal), [cb, phase], [cmpb])
        bank = psum.get()
        for b in range(DB):
            MM(bank[0][0:DB, 0:cw], ESEL[:NPG, b, :], CMP[:NPG, b, :], b == 0, b == DB - 1, [cmpb, cb, phase], [bank[1]])
        CP("dve", OH[:, c0:c0 + cw], bank[0][0:DB, 0:cw], [bank[1]], [OHb])
        psum.put(bank)
        p.op("pool", lambda e: e.memset(AR16[0:1, 0:2], 0.0), [cmpb, phase], [cmpb])

    p.op("pool", lambda e: e.memset(AR16[0:1, 0:2], 0.0), [], [phase])
    NKS = 3
    VBOFF = NKS * G * 512
    kslots = [(ARENA[:, s * G * 512:(s + 1) * G * 512].rearrange("p (g f) -> p g f", g=G), Buf()) for s in range(NKS)]
    vbslots = [(AR16[:, 2 * VBOFF + s * G * 512:2 * VBOFF + (s + 1) * G * 512].rearrange("p (g f) -> p g f", g=G), Buf()) for s in range(2)]
    OHR = [p.sb("OHR%d" % s, [DB, G * 128], BF16) for s in range(2)]
    ohrslots = [(OHR[s][:, :].rearrange("p (g t) -> p g t", g=G), Buf()) for s in range(2)]
    def sampA(gi):
        s = gi % 2
        kt, kb_ = kslots[gi % NKS]; vbt, vbb_ = vbslots[s]; oht, ohb_ = ohrslots[s]
        pg0 = gi * G
        ck = cks[pg0 // PPS]; cv = cvs[pg0 // PPS]; pl0 = pg0 % PPS
        DMA("sp", dsof(kb_), kt, ck[pl0 * 128:(pl0 + G) * 128, :].rearrange("(g t) f -> t g f", g=G), [phase], [kb_])
        DMA("pool", dsof(vbb_), vbt, cv[pl0 * 128:(pl0 + G) * 128, :].rearrange("(g t) f -> t g f", g=G), [phase], [vbb_])
        CP("pool", oht, OH[:, pg0:pg0 + G].unsqueeze(2).to_broadcast([DB, G, 128]), [OHb, phase], [ohb_])
        zt = p32.get()
        for g in range(G):
            qb = psum.get()
            MM(qb[0][:, :], oht[:, g, :], QSb16[0:DB, :], True, True, [ohb_, QSbb, phase], [qb[1]])
            pr = p32.get()
            TT("dve", pr[0][:, 0:512], kt[:, g, :], qb[0][:, :], ALU.mult, [kb_, qb[1], phase], [pr[1]])
            psum.put(qb)
            p.op("dve", lambda e, pr=pr, zt=zt, g=g: e.tensor_reduce(
                out=zt[0][:, g * 8:(g + 1) * 8], in_=pr[0][:, 0:512].rearrange("p (h d) -> p h d", h=8), axis=AX.X,
                op=ALU.add), [pr[1]], [zt[1]])
            p32.put(pr)
        return (zt, vbt, vbb_, pg0)

    def sampB(ctx):
        zt, vbt, vbb_, pg0 = ctx
        NC8 = G * 8
        TT("dve", zt[0][:, 0:NC8], zt[0][:, 0:NC8], BIASG[:].rearrange("p g h -> p (g h)"), ALU.add, [zt[1], cb], [zt[1]])
        et = p32.get()
        ACT(et[0][:, 0:NC8], zt[0][:, 0:NC8], AF.Exp, [zt[1]], [et[1]])
        spb = p16.get()
        ACT(spb[0][:, 0:NC8], et[0][:, 0:NC8], AF.Ln, [et[1]], [spb[1]], bias=1.0)
        p32.put(et)
        lat = psum.get()
        MM(lat[0][:, 0:NC8], trineg[:], spb[0][:, 0:NC8], True, True, [spb[1], cb], [lat[1]])
        tot = psum.get()
        MM(tot[0][0:1, 0:NC8], onesneg[:, 0:1], spb[0][:, 0:NC8], True, True, [spb[1], cb], [tot[1]])
        p16.put(spb)
        TT("dve", zt[0][:, 0:NC8], lat[0][:, 0:NC8], zt[0][:, 0:NC8], ALU.add, [lat[1], zt[1]], [zt[1]])
        psum.put(lat)
        wt = p16.get()
        ACT(wt[0][:, 0:NC8], zt[0][:, 0:NC8], AF.Exp, [zt[1]], [wt[1]])
        p32.put(zt)
        tots = p32.get()
        CP("act", tots[0][0:1, 0:NC8], tot[0][0:1, 0:NC8], [tot[1]], [tots[1]])
        psum.put(tot)
        DMA("pool", dsof(tots[1]), rec_t_loc[pg0:pg0 + G, :].rearrange("(o g) h -> o (g h)", o=1), tots[0][0:1, 0:NC8], [tots[1]], [phase] if False else ())
        p32.put(tots)
        for sg in range(G // 4):
            ot = psum.get()
            for q in range(4):
                pg = sg * 4 + q
                p.op("pe", lambda e, ot=ot, q=q, pg=pg, vbt=vbt, wt=wt: e.matmul(
                    ot[0][q * 32:q * 32 + 8, :], lhsT=wt[0][:, pg * 8:(pg + 1) * 8], rhs=vbt[:, pg, :], start=True, stop=True,
                    tile_position=(0, q * 32)), [vbb_, wt[1], phase], [ot[1]])
            msk = p32.get()
            TT("dve", msk[0][:, 0:512], ot[0][:, :], BD128[:], ALU.mult, [ot[1], cb], [msk[1]])
            psum.put(ot)
            o2 = psum.get()
            MM(o2[0][0:4, :], SEL4[:], msk[0][:, 0:512], True, True, [msk[1], cb], [o2[1]])
            p32.put(msk)
            orow = p32.get()
            CP("act", orow[0][0:4, 0:512], o2[0][0:4, :], [o2[1]], [orow[1]])
            psum.put(o2)
            DMA("pool", dsof(orow[1]), rec_o_loc.rearrange("(q h) d -> q (h d)", h=8)[pg0 + sg * 4:pg0 + sg * 4 + 4, :],
                orow[0][0:4, 0:512], [orow[1]], ())
            p32.put(orow)
        p16.put(wt)
    ctx_ = sampA(0)
    for gi in range(NGRP):
        nxt_ = sampA(gi + 1) if gi + 1 < NGRP else None
        sampB(ctx_)
        ctx_ = nxt_
    recb = Buf("rec")

    def allgather(groups, src, dst, first, rb, wb):
        p.dma("pool", d_cc, lambda e: e.collective_compute("AllGather", ALU.bypass, replica_groups=groups, ins=[src],
                                                           outs=[dst]), rb, wb)
        if first:
            for _it in p32.free:
                _d = dsof(_it[1])
                p.ops["pool"][-1].waits.append(("d", _d, _d.unit * _d.cnt))

    CR = CPG * 8
    stage2 = []
    if NCORES == 1:
        DMA("pool", d_misc, rec_o_all, rec_o_loc, [], [recb])
        for _it in p32.free:
            _d = dsof(_it[1])
            p.ops["pool"][-1].waits.append(("d", _d, _d.unit * _d.cnt))
        DMA("pool", d_misc, rec_t_all, rec_t_loc, [], [recb])
    elif NCORES == 8:
        g1 = [[0, 1, 2, 3], [4, 5, 6, 7]]; g2 = [[0, 4], [1, 5], [2, 6], [3, 7]]
        rec_t_half = dint("rec_t_half", [4 * PPC, 8])
        hb = Buf()
        allgather(g1, rec_t_loc, rec_t_half, True, [], [hb])
        stage2.append((g2, rec_t_half, rec_t_all, hb))
        for k in range(NCH):
            half = dint("rec_o_half%d" % k, [4 * CR, 64])
            hb = Buf()
            allgather(g1, rec_o_loc[k * CR:(k + 1) * CR, :], half, False, [], [hb])
            stage2.append((g2, half, rec_o_all[k * NCORES * CR:(k + 1) * NCORES * CR, :], hb))
    else:
        rg = [list(range(NCORES))]
        allgather(rg, rec_t_loc, rec_t_all, True, [], [recb])
        for k in range(NCH):
            allgather(rg, rec_o_loc[k * CR:(k + 1) * CR, :], rec_o_all[k * NCORES * CR:(k + 1) * NCORES * CR, :], False, [], [recb])

    def issue_stage2():
        for (g, src, dst, hb) in stage2:
            allgather(g, src, dst, False, [hb], [recb])
        stage2.clear()

    PGF = p.sb("PGF", [128, DB], F32); RNK = p.sb("RNK", [128, DB], F32); CHK = p.sb("CHK", [128, DB], F32)
    IDXO = p.sb("IDXO", [128, DB], I32)
    ib = Buf("idx")
    CP("dve", PGF[:NPG, :], PTT[:NPG, :], [cb], [ib])
    MS("dve", RNK[:NPG, :], 0.0, [ib])
    MS("dve", CHK[:NPG, :], 0.0, [ib])
    for r in range(1, NCORES):
        STT(RNK[:NPG, :], PGF[:NPG, :], float(r * PPC), RNK[:NPG, :], ALU.is_ge, ALU.add, [ib], [ib])
    STT(PGF[:NPG, :], RNK[:NPG, :], float(-PPC), PGF[:NPG, :], ALU.mult, ALU.add, [ib], [ib])
    for k in range(1, NCH):
        STT(CHK[:NPG, :], PGF[:NPG, :], float(k * CPG), CHK[:NPG, :], ALU.is_ge, ALU.add, [ib], [ib])
    STT(PGF[:NPG, :], CHK[:NPG, :], float((NCORES - 1) * CPG), PGF[:NPG, :], ALU.mult, ALU.add, [ib], [ib])
    STT(PGF[:NPG, :], RNK[:NPG, :], float(CPG), PGF[:NPG, :], ALU.mult, ALU.add, [ib], [ib])
    CP("dve", IDXO[:NPG, :], PGF[:NPG, :], [ib], [ib])
    p.op("pool", lambda e: e.memset(AR16[0:1, 0:2], 0.0), [], [phase])

    if UPTO == 3:
        return fin()
    def prompt_tile(t):
        N = 512
        r0 = t * 512
        for hf in range(2):
            banks = [psum.get() for _ in range(4)]
            for tb in range(4):
                xt = p32.get()
                DMA("sp", dsof(xt[1]), xt[0][:, 0:512], xp[r0 + tb * 128:r0 + (tb + 1) * 128, hf * 512:(hf + 1) * 512], (), [xt[1]])
                for c4 in range(4):
                    TR(banks[c4][0][:, tb * 128:(tb + 1) * 128], xt[0][:, c4 * 128:(c4 + 1) * 128], [xt[1]], [banks[c4][1]])
                p32.put(xt)
            for c4 in range(4):
                CP(evac_eng(), X[:, hf * 4 + c4, :], banks[c4][0][:, :], [banks[c4][1]], [Xb[hf * 4 + c4]])
                psum.put(banks[c4])
        if UPTO == 10:
            raise _Done()
        rmsnorm(X, Xb, lambda c: GMIX[:, 0, c:c + 1], H, Hb, N)
        if UPTO == 11:
            raise _Done()
        U = [p32.get() for _ in range(4)]

        def epi_u(m, bank):
            CP("dve", U[m][0][:, 0:16], HALO[:, m, :], [HALOb], [U[m][1]])
            CP(evac_eng(), U[m][0][:, 16:528], bank[0][:, :], [bank[1]], [U[m][1]])
        dense_fm("in0", 0, 4, lambda c: H[:, c, :], Hb, N, epi_u)
        if UPTO == 111:
            raise _Done()

        def epi_q(m, bank):
            ACT(QT[:, m, :], bank[0][:, :], AF.Copy, [bank[1]], [QTb[m]], scale=0.125)
        dense_fm("in0", 512, 4, lambda c: H[:, c, :], Hb, N, epi_q)
        if UPTO == 112:
            raise _Done()
        view, sbuf = wload("in0", 1024, 512)
        for m in range(4):
            bank = psum.get()
            for c in range(8):
                MM(bank[0][:, :], view[:, c, m * 128:(m + 1) * 128], H[:, c, :], c == 0, c == 7, [sbuf, Hb[c]], [bank[1]])
            CP(evac_eng(), KT[:, m, r0:r0 + 512], bank[0][:, :], [bank[1], phase], [KTb[m][t]])
            psum.put(bank)
        if UPTO == 113:
            raise _Done()
        for tb in range(4):
            bank = psum.get()
            for c in range(8):
                MM(bank[0][:, :], H[:, c, tb * 128:(tb + 1) * 128], view[:, c, :], c == 0, c == 7, [sbuf, Hb[c]], [bank[1]])
            stg = p32.get()
            CP(evac_eng(), stg[0][:, 0:512], bank[0][:, :], [bank[1]], [stg[1]])
            psum.put(bank)
            DMA("pool", dsof(stg[1]), kp[r0 + tb * 128:r0 + (tb + 1) * 128, :], stg[0][:, 0:512], [stg[1]], ())
            p32.put(stg)
        if UPTO == 114:
            raise _Done()
        view, sbuf = wload("in0", 1536, 512)
        for tb in range(4):
            bank = psum.get()
            for c in range(8):
                MM(bank[0][:, :], H[:, c, tb * 128:(tb + 1) * 128], view[:, c, :], c == 0, c == 7, [sbuf, Hb[c]], [bank[1]])
            stg = p32.get()
            CP("act", stg[0][:, 0:512], bank[0][:, :], [bank[1]], [stg[1]])
            psum.put(bank)
            CP("pool", VB[:, t * 4 + tb, :], stg[0][:, 0:512], [stg[1], phase], [VBb[t * 4 + tb]])
            DMA("pool", dsof(stg[1]), vp[r0 + tb * 128:r0 + (tb + 1) * 128, :], stg[0][:, 0:512], [stg[1]], ())
            p32.put(stg)
        if UPTO == 12:
            raise _Done()
        for g in range(4):
            w = 2 << g
            ub = U[g]
            cur = ub
            tmps = []
            for k in range(g):
                sh = 1 << k
                lo = 1 + (2 << k) - 1
                nxt = p32.get()
                tmps.append(nxt)
                TT("pool", nxt[0][:, lo:528], cur[0][:, lo:528], cur[0][:, lo - sh:528 - sh], ALU.add, [cur[1]], [nxt[1]])
                cur = nxt
            sh = 1 << g
            ssum = p32.get()
            TT("pool", ssum[0][:, 0:512], cur[0][:, 16:528], cur[0][:, 16 - sh:528 - sh], ALU.add, [cur[1]], [ssum[1]])
            for tm_ in tmps:
                p32.put(tm_)
            dd = p16.get()
            STT(dd[0][:, :], ssum[0][:, 0:512], 1.0 / w, ub[0][:, 16:528], ALU.mult, ALU.subtract, [ssum[1], ub[1]], [dd[1]])
            if t == 0:
                fx = p32.get()
                TT("dve", fx[0][:, 0:w - 1], ssum[0][:, 0:w - 1], RC0[:, 0:w - 1], ALU.mult, [ssum[1], cb], [fx[1]])
                TT("dve", dd[0][:, 0:w - 1], fx[0][:, 0:w - 1], ub[0][:, 16:16 + w - 1], ALU.subtract, [fx[1], ub[1]], [dd[1]])
                p32.put(fx)
            p32.put(ssum)
            CP("pool", HALO[:, g, 1:16], ub[0][:, 513:528], [ub[1]], [HALOb])
            if t == NT - 1:
                DMA("pool", dsof(ub[1]), poolp[:, g * 128:(g + 1) * 128].rearrange("j c -> c j"), ub[0][:, 513:528], [ub[1]], (), slow=True)
            bank = psum.get()
            MM(bank[0][:, :], WPb[:, g, :], dd[0][:, :], True, True, [dd[1], cb], [bank[1]])
            p16.put(dd)
            TS("dve", MIX[:, g, :], bank[0][:, :], PSC[:, g:g + 1], None, ALU.mult, None, [bank[1], cb], [MIXb[g]])
            psum.put(bank)
            p32.put(ub)
        if UPTO == 13:
            raise _Done()
        nkb = 4 * t + 4
        for hg in range(2):
            av = [psum.get(), psum.get()]
            lacc = [p32.get() for _ in range(4)]
            laccb = [p16.get() for _ in range(4)]
            for kbi in range(nkb):
                kb = nkb - 1 - kbi
                diag = kb >= 4 * t
                r = kb - 4 * t
                kt_ = kb // 4
                spbs = []
                for hh in range(4):
                    h = hg * 4 + hh
                    j = h // 2
                    po = (h % 2) * 64
                    kT = KT[po:po + 64, j, kb * 128:(kb + 1) * 128]
                    qT = QT[po:po + 64, j, :]
                    kq_reads = [KTb[j][kt_], QTb[j], phase]
                    zb = psum.get()
                    MM(zb[0][:, :], kT, qT, True, not diag, kq_reads, [zb[1]])
                    if diag:
                        MM(zb[0][:, :], identb[:], negm[:, r, :], False, True, [cb], [zb[1]])
                    et = p32.get()
                    ACT(et[0][:, 0:512], zb[0][:, :], AF.Exp, [zb[1], cb], [et[1]], bias=BIAS[:, h:h + 1])
                    spb = p16.get()
                    ACT(spb[0][:, :], et[0][:, 0:512], AF.Ln, [et[1]], [spb[1]], bias=1.0)
                    p32.put(et)
                    spbs.append((spb, zb))
                for hh in range(4):
                    h = hg * 4 + hh
                    j = h // 2
                    po = (h % 2) * 64
                    kT = KT[po:po + 64, j, kb * 128:(kb + 1) * 128]
                    qT = QT[po:po + 64, j, :]
                    kq_reads = [KTb[j][kt_], QTb[j], phase]
                    spb, b2 = spbs[hh]
                    MM(b2[0][:, :], trineg[:], spb[0][:, :], False, kbi == 0, [spb[1], cb], [b2[1]])
                    if kbi > 0:
                        MM(b2[0][:, :], onesneg[:], laccb[hh][0][:, :], False, True, [laccb[hh][1], cb], [b2[1]])
                    wT = p16.get()
                    ACT(wT[0][:, :], b2[0][:, :], AF.Exp, [b2[1], cb], [wT[1]], bias=BIAS[:, h:h + 1])
                    psum.put(b2)
                    MM(av[hh // 2][0][(hh % 2) * 64:(hh % 2) * 64 + 64, :], VB[:, kb, h * 64:(h + 1) * 64], wT[0][:, :],
                       kbi == 0, kbi == nkb - 1, [VBb[kb], wT[1], phase], [av[hh // 2][1]])
                    p16.put(wT)
                    if kbi < nkb - 1:
                        if kbi == 0:
                            CP("dve", lacc[hh][0][:, 0:512], spb[0][:, :], [spb[1]], [lacc[hh][1]])
                        else:
                            TT("dve", lacc[hh][0][:, 0:512], lacc[hh][0][:, 0:512], spb[0][:, :], ALU.add,
                               [lacc[hh][1], spb[1]], [lacc[hh][1]])
                        CP("dve", laccb[hh][0][:, :], lacc[hh][0][:, 0:512], [lacc[hh][1]], [laccb[hh][1]])
                    p16.put(spb)
            for k2 in range(2):
                CP("dve", MIX[:, 4 + hg * 2 + k2, :], av[k2][0][:, :], [av[k2][1]], [MIXb[4 + hg * 2 + k2]])
                psum.put(av[k2])
            for it in lacc:
                p32.put(it)
            for it in laccb:
                p16.put(it)
        if UPTO == 14:
            raise _Done()
        if DBG and t == NT - 1:
            DMA("sp", d_misc, dbg_mix0, MIX[:].rearrange("p c n -> p (c n)"), MIXb, ())
        outproj("out0", MIX, MIXb, X, Xb, N)
        if UPTO == 15:
            raise _Done()
        ffn(0, X, Xb, H, Hb, N, GF, GFb, 2)
        if UPTO == 16:
            raise _Done()
        rmsnorm(X, Xb, lambda c: GMIX[:, 1, c:c + 1], H, Hb, N)
        UG = [p32.get() for _ in range(4)]

        def epi_ug(m, bank):
            ACT(UG[m][0][:, 0:512], bank[0][:, :], AF.Gelu, [bank[1]], [UG[m][1]])
        dense_fm("in1", 0, 4, lambda c: H[:, c, :], Hb, N, epi_ug)
        VT = [p16.get() for _ in range(4)]
        view, sbuf = wload("in1", 512, 512)
        for tb in range(4):
            bank = psum.get()
            for c in range(8):
                MM(bank[0][:, :], H[:, c, tb * 128:(tb + 1) * 128], view[:, c, :], c == 0, c == 7, [sbuf, Hb[c]], [bank[1]])
            ACT(VT[tb][0][:, :], bank[0][:, :], AF.Gelu, [bank[1]], [VT[tb][1]])
            psum.put(bank)
        for g in range(4):
            bank = psum.get()
            for tb in range(4):
                MM(bank[0][:, tb * 128:(tb + 1) * 128], VT[tb][0][:, g * 128:(g + 1) * 128], WMT[:, g, :], tb == 0, False,
                   [VT[tb][1], cb], [bank[1]])
            for tb in range(4):
                MM(bank[0][:, tb * 128:(tb + 1) * 128], ones2[0:1, :], BSH[0:1, g, :], False, False, [cb], [bank[1]])
                MM(bank[0][:, tb * 128:(tb + 1) * 128], ones2[0:1, :], BSL[0:1, g, :], False, True, [cb], [bank[1]])
            TT("dve", MIX[:, g, :], bank[0][:, :], UG[g][0][:, 0:512], ALU.mult, [bank[1], UG[g][1]], [MIXb[g]])
            psum.put(bank)
        for it in VT:
            p16.put(it)
        for it in UG:
            p32.put(it)
        ZZ = [p32.get() for _ in range(4)]

        def epi_hh(m, bank):
            CP(evac_eng(), ZZ[m][0][:, 2:514], bank[0][:, :], [bank[1]], [ZZ[m][1]])
        dense_fm("in1", 1024, 4, lambda c: H[:, c, :], Hb, N, epi_hh)
        BG = [p32.get() for _ in range(4)]

        def epi_bg(m, bank):
            CP(evac_eng(), BG[m][0][:, 0:512], bank[0][:, :], [bank[1]], [BG[m][1]])
        dense_fm("in1", 1536, 4, lambda c: H[:, c, :], Hb, N, epi_bg)

        def epi_cg(m, bank):
            TT("dve", ZZ[m][0][:, 2:514], bank[0][:, :], ZZ[m][0][:, 2:514], ALU.mult, [bank[1], ZZ[m][1]], [ZZ[m][1]])
            CP("pool", ZZ[m][0][:, 0:2], ZHALO[:, m, :], [ZHALOb], [ZZ[m][1]])
            cv_ = p32.get()
            TS("pool", cv_[0][:, 0:512], ZZ[m][0][:, 0:512], CW[:, 0, m:m + 1], None, ALU.mult, None, [ZZ[m][1], cb], [cv_[1]])
            STT(cv_[0][:, 0:512], ZZ[m][0][:, 1:513], CW[:, 1, m:m + 1], cv_[0][:, 0:512], ALU.mult, ALU.add,
                [ZZ[m][1], cv_[1], cb], [cv_[1]])
            STT(cv_[0][:, 0:512], ZZ[m][0][:, 2:514], CW[:, 2, m:m + 1], cv_[0][:, 0:512], ALU.mult, ALU.add,
                [ZZ[m][1], cv_[1], cb], [cv_[1]])
            TT("pool", MIX[:, 4 + m, :], cv_[0][:, 0:512], BG[m][0][:, 0:512], ALU.mult, [cv_[1], BG[m][1]], [MIXb[4 + m]])
            p32.put(cv_)
            CP("pool", ZHALO[:, m, :], ZZ[m][0][:, 512:514], [ZZ[m][1]], [ZHALOb])
            if t == NT - 1:
                DMA("pool", dsof(ZZ[m][1]), convp[:, m * 128:(m + 1) * 128].rearrange("j c -> c j"), ZZ[m][0][:, 512:514], [ZZ[m][1]], (), slow=True)
        dense_fm("in1", 2048, 4, lambda c: H[:, c, :], Hb, N, epi_cg)
        for it in ZZ:
            p32.put(it)
        for it in BG:
            p32.put(it)
        if UPTO == 17:
            raise _Done()
        if DBG and t == NT - 1:
            DMA("sp", d_misc, dbg_mix, MIX[:].rearrange("p c n -> p (c n)"), MIXb, ())
            DMA("sp", d_misc, dbg_wmt, WMT[:].rearrange("p g t -> p (g t)"), [cb], ())
        outproj("out1", MIX, MIXb, X, Xb, N)
        ffn(1, X, Xb, H, Hb, N, GF, GFb, 2)
        if UPTO == 18:
            raise _Done()
        rstd = rmsnorm(X, Xb, None, None, None, N, out32=True)
        for hf in range(2):
            banks = [psum.get() for _ in range(4)]
            for c4 in range(4):
                c = hf * 4 + c4
                yn = p32.get()
                STT(yn[0][:, 0:512], X[:, c, :], GFIN[:, c:c + 1], rstd[0][:, 0:512], ALU.mult, ALU.mult,
                    [Xb[c], rstd[1], cb], [yn[1]])
                for tb in range(4):
                    TR(banks[tb][0][:, c4 * 128:(c4 + 1) * 128], yn[0][:, tb * 128:(tb + 1) * 128], [yn[1]], [banks[tb][1]])
                p32.put(yn)
            for tb in range(4):
                stg = p32.get()
                CP(evac_eng(), stg[0][:, 0:512], banks[tb][0][:, :], [banks[tb][1]], [stg[1]])
                psum.put(banks[tb])
                DMA("pool", dsof(stg[1]), yp[r0 + tb * 128:r0 + (tb + 1) * 128, hf * 512:(hf + 1) * 512], stg[0][:, 0:512], [stg[1]], ())
                p32.put(stg)
        p32.put(rstd)

    try:
        for t in range(NT):
            prompt_tile(t)
            if t == 0:
                issue_stage2()
    except _Done:
        return fin()
    if UPTO == 30:
        return fin()

    N = OWN
    for b in range(OWN):
        R = p32.get(); Tt = p32.get()
        p.dma("pool", dsof(R[1]), lambda e, R=R, b=b: e.indirect_dma_start(
            out=R[0][:NPG, 0:512], out_offset=None, in_=rec_o_all.rearrange("(q h) d -> q (h d)", h=8),
            in_offset=bass.IndirectOffsetOnAxis(ap=IDXO[:NPG, b:b + 1], axis=0)), [recb, cb, ib], [R[1]])
        p.dma("pool", dsof(Tt[1]), lambda e, Tt=Tt, b=b: e.indirect_dma_start(
            out=Tt[0][:NPG, 0:8], out_offset=None, in_=rec_t_all,
            in_offset=bass.IndirectOffsetOnAxis(ap=PTT[:NPG, b:b + 1], axis=0)), [recb, cb], [Tt[1]])
        sfx = psum.get()
        MM(sfx[0][:NPG, 0:8], tripos[:NPG, :NPG], Tt[0][:NPG, 0:8], True, True, [Tt[1], cb], [sfx[1]])
        cc_ = p32.get()
        ACT(cc_[0][:NPG, 0:8], sfx[0][:NPG, 0:8], AF.Exp, [sfx[1]], [cc_[1]])
        psum.put(sfx)
        TT("dve", R[0][:NPG, 0:512].rearrange("p (h d) -> p h d", h=8), R[0][:NPG, 0:512].rearrange("p (h d) -> p h d", h=8),
           cc_[0][:NPG, 0:8].unsqueeze(2).to_broadcast([NPG, 8, 64]), ALU.mult, [R[1], cc_[1]], [R[1]])
        p32.put(cc_); p32.put(Tt)
        att = psum.get()
        for j in range(4):
            MM(att[0][:, j:j + 1], R[0][:NPG, j * 128:(j + 1) * 128], ones32[:NPG, 0:1], True, True, [R[1], cb], [att[1]])
        for j in range(4):
            CP("dve", MIXs[:, 4 + j, b:b + 1], att[0][:, j:j + 1], [att[1]], [MIXsb[4 + j]])
        psum.put(att)
        p32.put(R)
    outproj("out0", MIXs, MIXsb, Xs, Xsb, N)
    ffn(0, Xs, Xsb, Hs, Hsb, N, GFs, GFsb, 1)
    rmsnorm(Xs, Xsb, lambda c: GMIX[:, 1, c:c + 1], Hs, Hsb, N)
    for pc in range(5):
        view, sbuf = wload("in1", pc * 512, 512)
        bank = psum.get()
        for c in range(8):
            MM(bank[0][0:N, :], Hs[:, c, 0:N], view[:, c, :], c == 0, c == 7, [sbuf, Hsb[c]], [bank[1]])
        if pc < 2:
            ACT(TOKS[0:N, pc * 512:(pc + 1) * 512], bank[0][0:N, :], AF.Gelu, [bank[1]], [TOKSb])
        else:
            CP("dve", TOKS[0:N, pc * 512:(pc + 1) * 512], bank[0][0:N, :], [bank[1]], [TOKSb])
        psum.put(bank)
    DMA("pool", d_toks, chv, TOKS[0:N, 512:1024], [TOKSb], ())
    TT("dve", TOKS[0:N, 1024:1536], TOKS[0:N, 1024:1536], TOKS[0:N, 2048:2560], ALU.mult, [TOKSb], [TOKSb])
    DMA("pool", d_toks, convs[:, 1, :], TOKS[0:N, 1024:1536], [TOKSb], ())
    DMA("pool", d_misc, convs[:, 0, :], scst[:, 1, :], (), ())
    FM = p32.get()
    for sec in range(5):
        view, sbuf = wload("in1", sec * 512, 512)
        bank = psum.get()
        for g in range(4):
            for c in range(8):
                MM(bank[0][:, g * 4:g * 4 + N], view[:, c, g * 128:(g + 1) * 128], Hs[:, c, 0:N], c == 0, c == 7,
                   [sbuf, Hsb[c]], [bank[1]])
        if sec < 2:
            ACT(FM[0][:, sec * 16:sec * 16 + 16], bank[0][:, 0:16], AF.Gelu, [bank[1]], [FM[1]])
        else:
            CP("dve", FM[0][:, sec * 16:sec * 16 + 16], bank[0][:, 0:16], [bank[1]], [FM[1]])
        psum.put(bank)
    WS0 = p32.get()
    for g in range(4):
        DMA("sp", dsof(WS0[1]), WS0[0][:, g:g + 1], w_s[g * 128:g * 128 + 1, 0:1].to_broadcast([128, 1]), (), [WS0[1]])
        DMA("sp", dsof(WS0[1]), WS0[0][:, 4 + g:5 + g], b_s[g:g + 1, 0:1].to_broadcast([128, 1]), (), [WS0[1]])
    for g in range(4):
        tmpg = p32.get()
        TS("dve", tmpg[0][:, 0:N], FM[0][:, 16 + g * 4:16 + g * 4 + N], WS0[0][:, g:g + 1], WS0[0][:, 4 + g:5 + g], ALU.mult, ALU.add,
           [FM[1], WS0[1]], [tmpg[1]])
        TT("dve", MIXs[:, g, 0:N], tmpg[0][:, 0:N], FM[0][:, g * 4:g * 4 + N], ALU.mult, [tmpg[1], FM[1]], [MIXsb[g]])
        p32.put(tmpg)
    p32.put(WS0)
    cst = p32.get()
    DMA("sp", dsof(cst[1]), cst[0][0:OWN * 2, 0:512], scst.rearrange("b j c -> (b j) c"), (), [cst[1]])
    bank = psum.get()
    for g in range(4):
        TR(bank[0][:, g * 8:g * 8 + OWN * 2], cst[0][0:OWN * 2, g * 128:(g + 1) * 128], [cst[1]], [bank[1]])
    CST = p32.get()
    CP("dve", CST[0][:, 0:32], bank[0][:, 0:32], [bank[1]], [CST[1]])
    psum.put(bank); p32.put(cst)
    for g in range(4):
        zf = p32.get(); cvv = p32.get()
        TT("dve", zf[0][:, 0:N], FM[0][:, 32 + g * 4:32 + g * 4 + N], FM[0][:, 64 + g * 4:64 + g * 4 + N], ALU.mult, [FM[1]], [zf[1]])
        st3 = CST[0][:, g * 8:g * 8 + 8].rearrange("p (b j) -> p b j", j=2)
        TS("dve", cvv[0][:, 0:N], st3[:, :, 0], CW[:, 0, g:g + 1], None, ALU.mult, None, [CST[1], cb], [cvv[1]])
        STT(cvv[0][:, 0:N], st3[:, :, 1], CW[:, 1, g:g + 1], cvv[0][:, 0:N], ALU.mult, ALU.add, [CST[1], cvv[1], cb], [cvv[1]])
        STT(cvv[0][:, 0:N], zf[0][:, 0:N], CW[:, 2, g:g + 1], cvv[0][:, 0:N], ALU.mult, ALU.add, [zf[1], cvv[1], cb], [cvv[1]])
        TT("dve", MIXs[:, 4 + g, 0:N], cvv[0][:, 0:N], FM[0][:, 48 + g * 4:48 + g * 4 + N], ALU.mult, [cvv[1], FM[1]], [MIXsb[4 + g]])
        p32.put(zf); p32.put(cvv)
    p32.put(CST); p32.put(FM)
    outproj("out1", MIXs, MIXsb, Xs, Xsb, N)
    ffn(1, Xs, Xsb, Hs, Hsb, N, GFs, GFsb, 1)
    rstd = rmsnorm(Xs, Xsb, None, None, None, N, out32=True)
    for hf in range(2):
        bank = psum.get()
        for c4 in range(4):
            c = hf * 4 + c4
            yn = p32.get()
            STT(yn[0][:, 0:N], Xs[:, c, 0:N], GFIN[:, c:c + 1], rstd[0][:, 0:N], ALU.mult, ALU.mult, [Xsb[c], rstd[1], cb], [yn[1]])
            TR(bank[0][0:N, c4 * 128:(c4 + 1) * 128], yn[0][:, 0:N], [yn[1]], [bank[1]])
            p32.put(yn)
        stg = p32.get()
        CP("dve", stg[0][0:N, 0:512], bank[0][0:N, :], [bank[1]], [stg[1]])
        psum.put(bank)
        DMA("pool", dsof(stg[1]), ys[:, hf * 512:(hf + 1) * 512], stg[0][0:N, 0:512], [stg[1]], ())
        p32.put(stg)
    p32.put(rstd)
    p.wait_all_dma("sp")
    p.emit()
    return nc


def make_in_maps(cfg, inputs):
    NCORES = cfg["NCORES"]; S = cfg["S"]; DB = cfg["DB"]; NPG = cfg["NPG"]; PPC = cfg["PPC"]
    OWN = DB // NCORES
    f = lambda a: np.ascontiguousarray(np.asarray(a))
    x_prompt = f(inputs["x_prompt"]); x_sample = f(inputs["x_sample"])[:, 0, :]
    cache_k = np.asarray(inputs["cache_k"])[0].reshape(-1, 128 * 512)
    cache_v = np.asarray(inputs["cache_v"])[0].reshape(-1, 128 * 512)
    state_pool = f(inputs["state_pool"])[0]; state_conv = f(inputs["state_conv"])[0]
    pt = f(inputs["page_table"]).astype(np.int32)
    common = dict(
        norm_mix=f(inputs["norm_mix"]), norm_ffn=f(inputs["norm_ffn"]), norm_final=f(inputs["norm_final"]).reshape(1, D),
        w_in0=f(inputs["ab_w_in"])[0], sb_bias=f(inputs["ab_sb_bias"]).reshape(1, 8), w_pool=f(inputs["ab_w_pool"])[0].reshape(512, 128),
        pool_scale=f(inputs["ab_pool_scale"]).reshape(1, 512), w_out0=f(inputs["ab_w_out"])[0], w_in1=f(inputs["cd_w_in"])[0],
        w_s=f(inputs["cd_w_s"])[0].reshape(512, 128), b_s=f(inputs["cd_b_s"])[0], conv_w=f(inputs["cd_conv_w"])[0],
        w_out1=f(inputs["cd_w_out"])[0])
    for l in range(2):
        common["w_gate%d" % l] = f(inputs["ffn_w_gate"])[l]
        common["w_up%d" % l] = f(inputs["ffn_w_up"])[l]
        common["w_down%d" % l] = f(inputs["ffn_w_down"])[l]
    maps = []
    for c in range(NCORES):
        order = np.roll(np.arange(DB), -c * OWN)
        m = dict(common)
        m["xp"] = x_prompt[c]
        m["xs"] = f(x_sample[order])
        NSPLIT = cfg.get("NSPLIT", 1); PPS = PPC // NSPLIT
        for i in range(NSPLIT):
            m["ck%d" % i] = f(cache_k[c * PPC + i * PPS:c * PPC + (i + 1) * PPS]).reshape(PPS * 128, 512)
            m["cv%d" % i] = f(cache_v[c * PPC + i * PPS:c * PPC + (i + 1) * PPS]).reshape(PPS * 128, 512)
        m["spst"] = f(state_pool[order[:OWN]]); m["scst"] = f(state_conv[order[:OWN]])
        m["ptT"] = f(pt[order].T)
        m["pbase"] = np.full((128, 1), float(c * PPC), np.float32)
        maps.append(m)
    return maps


def gather_outputs(cfg, res):
    NCORES = cfg["NCORES"]; S = cfg["S"]; DB = cfg["DB"]
    OWN = DB // NCORES
    R = res
    cat = lambda k: np.stack([r[k] for r in R])
    y_prompt = cat("yp")
    y_sample = np.concatenate([r["ys"] for r in R], 0).reshape(DB, 1, D)
    k_prompt = cat("kp").reshape(1, NCORES, S, 8, 64); v_prompt = cat("vp").reshape(1, NCORES, S, 8, 64)
    k_sample = np.concatenate([r["ksm"] for r in R], 0).reshape(1, DB, 1, 8, 64)
    v_sample = np.concatenate([r["vsm"] for r in R], 0).reshape(1, DB, 1, 8, 64)
    pool_prompt = cat("poolp")[None]; pool_sample = np.concatenate([r["pools"] for r in R], 0)[None]
    conv_prompt = cat("convp")[None]; conv_sample = np.concatenate([r["convs"] for r in R], 0)[None]
    chunk_v = np.concatenate([r["chv"] for r in R], 0).reshape(1, DB, 1, 512)
    return tuple(np.ascontiguousarray(a.astype(np.float32)) for a in
                 (y_prompt, y_sample, k_prompt, v_prompt, k_sample, v_sample, pool_prompt, pool_sample, conv_prompt,
                  conv_sample, chunk_v))


def kernel(**inputs):
    cfg = default_cfg()
    nc = build(cfg)
    maps = make_in_maps(cfg, inputs)
    res = run_bass_kernel_spmd(nc, maps, core_ids=list(range(cfg["NCORES"])))
    return gather_outputs(cfg, res.results)
```

```python
import numpy as np
from contextlib import ExitStack
import concourse.bass as bass
import concourse.mybir as mybir
from concourse.bass_utils import run_bass_kernel_spmd

F32 = mybir.dt.float32
BF16 = mybir.dt.bfloat16
I32 = mybir.dt.int32
AF = mybir.ActivationFunctionType
ALU = mybir.AluOpType
AX = mybir.AxisListType

CENG = ("pe", "act", "dve", "pool")
ALLENG = ("pe", "act", "dve", "pool", "sp")


class Buf:
    __slots__ = ("name", "w", "r")

    def __init__(self, name=""):
        self.name = name
        self.w = None
        self.r = {}


class DSem:
    __slots__ = ("sem", "cnt", "name", "unit")

    def __init__(self, sem, name, unit=16):
        self.sem = sem
        self.cnt = 0
        self.name = name
        self.unit = unit


class Op:
    __slots__ = ("eng", "fn", "waits", "need", "sig", "ds", "isdma")

    def __init__(self, eng, fn, isdma=False, ds=None):
        self.eng = eng
        self.fn = fn
        self.waits = []
        self.need = False
        self.sig = 0
        self.ds = ds
        self.isdma = isdma


class Prog:
    def __init__(self, nc):
        self.nc = nc
        self.stack = ExitStack()
        self.ops = {e: [] for e in ALLENG}
        self.esem = {}
        self.dsems = []
        self.nops = 0

    def sb(self, name, shape, dtype):
        return self.stack.enter_context(self.nc.sbuf_tensor(name, list(shape), dtype))

    def ps(self, name, shape, dtype=F32):
        return self.stack.enter_context(self.nc.psum_tensor(name, list(shape), dtype))

    def dsem(self, name, unit=16):
        s = self.stack.enter_context(self.nc.semaphore(name))
        d = DSem(s, name, unit)
        self.dsems.append(d)
        return d

    def _deps(self, op, reads, writes):
        def add(tok, kind):
            if tok is None:
                return
            if tok[0] == "c":
                prod = tok[1]
                if (not op.isdma) and prod.eng == op.eng and kind != "raw":
                    return
                prod.need = True
                op.waits.append(("c", prod))
            else:
                ds = tok[1]
                op.waits.append(("d", ds, ds.unit * ds.cnt))
        for b in reads:
            add(b.w, "raw")
        for b in writes:
            add(b.w, "waw")
            for t in b.r.values():
                add(t, "war")
        tok = ("d", op.ds) if op.isdma else ("c", op)
        for b in writes:
            b.w = tok
            b.r = {}
        rkey = id(op.ds) if op.isdma else op.eng
        for b in reads:
            b.r[rkey] = tok

    def op(self, eng, fn, reads=(), writes=()):
        o = Op(eng, fn)
        self._deps(o, reads, writes)
        self.ops[eng].append(o)
        self.nops += 1
        return o

    def dma(self, q, ds, fn, reads=(), writes=()):
        o = Op(q, fn, isdma=True, ds=ds)
        self._deps(o, reads, writes)
        ds.cnt += 1
        self.ops[q].append(o)
        self.nops += 1
        return o

    def wait_all_dma(self, q="sp"):
        o = Op(q, None)
        for d in self.dsems:
            if d.cnt:
                o.waits.append(("d", d, d.unit * d.cnt))
        self.ops[q].append(o)

    def emit(self):
        nc = self.nc
        for e in CENG:
            self.esem[e] = self.stack.enter_context(nc.semaphore("es_" + e))
        for e in ALLENG:
            n = 0
            for o in self.ops[e]:
                if o.need and not o.isdma:
                    n += 1
                    o.sig = n
        ops = self.ops
        esem = self.esem

        def stream(ename, eng):
            waited = {}
            for o in ops[ename]:
                for w in o.waits:
                    if w[0] == "c":
                        prod = w[1]
                        key = prod.eng
                        val = prod.sig
                        sem = esem[prod.eng]
                    else:
                        key = w[1]
                        val = w[2]
                        sem = w[1].sem
                    if waited.get(key, 0) >= val:
                        continue
                    waited[key] = val
                    eng.wait_ge(sem, val)
                if o.fn is None:
                    continue
                ins = o.fn(eng)
                if o.isdma:
                    ins.then_inc(o.ds.sem, o.ds.unit)
                elif o.need:
                    ins.then_inc(esem[ename], 1)

        with nc.Block() as block:
            @block.sync
            def _(e):
                stream("sp", e)

            @block.scalar
            def _(e):
                stream("act", e)

            @block.tensor
            def _(e):
                stream("pe", e)

            @block.vector
            def _(e):
                stream("dve", e)

            @block.gpsimd
            def _(e):
                stream("pool", e)
        self.stack.close()


class TPool:
    def __init__(self, items):
        self.free = list(items)

    def get(self):
        assert self.free, "tile pool exhausted"
        return self.free.pop(0)

    def put(self, it):
        self.free.append(it)


D = 1024
DFF = 2816
KFF = DFF // 128
EPS = 1e-6


def default_cfg():
    return dict(NCORES=8, S=4096, DB=32, NPG=128, PPC=640, NSPLIT=2)


def build(cfg):
    NCORES = cfg["NCORES"]; S = cfg["S"]; DB = cfg["DB"]; NPG = cfg["NPG"]; PPC = cfg["PPC"]
    NT = S // 512
    OWN = DB // NCORES
    assert OWN == 4 and DB <= 32
    G = 8
    assert PPC % G == 0
    NGRP = PPC // G

    nc = bass.Bass("TRN2", target_bir_lowering=False)

    def din(name, shape, dt=F32):
        return nc.dram_tensor(name, list(shape), dt, kind="ExternalInput").ap()

    def dout(name, shape, dt=F32):
        return nc.dram_tensor(name, list(shape), dt, kind="ExternalOutput").ap()

    def dint(name, shape, dt=F32):
        return nc.dram_tensor(name, list(shape), dt, kind="Internal").ap()

    xp = din("xp", [S, D]); xs = din("xs", [DB, D])
    NSPLIT = cfg.get("NSPLIT", 1)
    PPS = PPC // NSPLIT
    assert PPC % NSPLIT == 0 and PPS % G == 0
    cks = [din("ck%d" % i, [PPS * 128, 512]) for i in range(NSPLIT)]
    cvs = [din("cv%d" % i, [PPS * 128, 512]) for i in range(NSPLIT)]
    spst = din("spst", [OWN, 15, 512]); scst = din("scst", [OWN, 2, 512])
    ptT = din("ptT", [NPG, DB], I32); pbase = din("pbase", [128, 1])
    norm_mix = din("norm_mix", [2, D]); norm_ffn = din("norm_ffn", [2, D]); norm_final = din("norm_final", [1, D])
    w_in0 = din("w_in0", [D, 2048]); sb_bias = din("sb_bias", [1, 8]); w_pool = din("w_pool", [512, 128])
    pool_scale = din("pool_scale", [1, 512]); w_out0 = din("w_out0", [D, D])
    w_in1 = din("w_in1", [D, 2560]); w_s = din("w_s", [512, 128]); b_s = din("b_s", [4, 128])
    conv_w = din("conv_w", [3, 512]); w_out1 = din("w_out1", [D, D])
    w_gate = [din("w_gate%d" % l, [D, DFF]) for l in range(2)]
    w_up = [din("w_up%d" % l, [D, DFF]) for l in range(2)]
    w_down = [din("w_down%d" % l, [DFF, D]) for l in range(2)]

    yp = dout("yp", [S, D]); ys = dout("ys", [OWN, D])
    kp = dout("kp", [S, 512]); vp = dout("vp", [S, 512])
    ksm = dout("ksm", [OWN, 512]); vsm = dout("vsm", [OWN, 512])
    poolp = dout("poolp", [15, 512]); pools = dout("pools", [OWN, 15, 512])
    convp = dout("convp", [2, 512]); convs = dout("convs", [OWN, 2, 512])
    chv = dout("chv", [OWN, 512])
    DBG = cfg.get("dbg", False)
    if DBG:
        dbg_wmt = dout("dbg_wmt", [128, 512], BF16)
        dbg_mix = dout("dbg_mix", [128, 4096], BF16)
        dbg_mix0 = dout("dbg_mix0", [128, 4096], BF16)
        dbg_t32 = dout("dbg_t32", [128, 512], F32)
        dbg_t32b = dout("dbg_t32b", [128, 512], F32)
        dbg_wmt0 = dout("dbg_wmt0", [128, 512], BF16)

    WS = {}
    wsrc = {"in0": (w_in0, 8, 2048), "out0": (w_out0, 8, D), "in1": (w_in1, 8, 2560), "out1": (w_out1, 8, D)}
    for l in range(2):
        wsrc["gate%d" % l] = (w_gate[l], 8, DFF)
        wsrc["up%d" % l] = (w_up[l], 8, DFF)
        wsrc["down%d" % l] = (w_down[l], KFF, D)
    for k, (src, kc, m) in wsrc.items():
        WS[k] = dint("wbf_" + k, [128, kc, m], BF16)
    rec_o_loc = dint("rec_o_loc", [PPC * 8, 64]); rec_t_loc = dint("rec_t_loc", [PPC, 8])
    CPG = cfg.get("CPG", min(128, PPC))
    assert PPC % CPG == 0
    NCH = PPC // CPG
    rec_o_all = dint("rec_o_all", [NCH * NCORES * CPG * 8, 64]); rec_t_all = dint("rec_t_all", [NCORES * PPC, 8])

    p = Prog(nc)
    cfg["_sbuf0"] = nc.sbuf_bytes_remaining
    psum = TPool([(p.ps("bank%d" % i, [128, 512]), Buf("bank%d" % i)) for i in range(8)])
    P32W = 544
    NP32 = cfg.get("NP32", 12)
    NP16 = cfg.get("NP16", 20)
    p32 = TPool([(p.sb("p32_%d" % i, [128, P32W], F32), Buf("p32_%d" % i)) for i in range(NP32)])
    p16 = TPool([(p.sb("p16_%d" % i, [128, 512], BF16), Buf("p16_%d" % i)) for i in range(NP16)])
    X = p.sb("X", [128, 8, 512], F32); Xb = [Buf("X%d" % c) for c in range(8)]
    H = p.sb("H", [128, 8, 512], BF16); Hb = [Buf("H%d" % c) for c in range(8)]
    QT = p.sb("QT", [128, 4, 512], BF16); QTb = [Buf() for _ in range(4)]
    MIX = p.sb("MIX", [128, 8, 512], BF16); MIXb = [Buf() for _ in range(8)]
    GF = None; GFb = None
    ARENA = p.sb("ARENA", [128, 16384], F32)
    AR16 = ARENA[:].bitcast(BF16)
    KT = AR16[:, 0:16384].rearrange("p (j n) -> p j n", j=4)
    VB = AR16[:, 16384:32768].rearrange("p (b f) -> p b f", f=512)
    KTb = [[Buf() for _ in range(NT)] for _ in range(4)]
    VBb = [Buf() for _ in range(32)]
    phase = Buf("phase")
    NSLOT = 3
    WR = [(p.sb("wr%d" % i, [128, 4096], BF16), Buf("wr%d" % i)) for i in range(NSLOT)]
    wr_i = [0]
    Xs = p.sb("Xs", [128, 8, 32], F32); Xsb = [Buf() for _ in range(8)]
    Hs = p.sb("Hs", [128, 8, 32], BF16); Hsb = [Buf() for _ in range(8)]
    MIXs = p.sb("MIXs", [128, 8, 4], BF16); MIXsb = [Buf() for _ in range(8)]
    GFs = p.sb("GFs", [128, 22, 4], BF16); GFsb = [Buf() for _ in range(22)]
    TOKS = p.sb("TOKS", [32, 2560], F32); TOKSb = Buf()
    QSb16 = p.sb("QSb16", [DB, 512], BF16); QSbb = Buf()
    ident = p.sb("ident", [128, 128], F32); identb = p.sb("identb", [128, 128], BF16)
    trineg = p.sb("trineg", [128, 128], BF16); onesneg = p.sb("onesneg", [128, 128], BF16)
    onesmean = p.sb("onesmean", [128, 128], BF16)
    tripos = p.sb("tripos", [128, 128], F32)
    trimask = p.sb("trimask", [128, 128], F32)
    ones32 = p.sb("ones32", [128, 128], F32)
    negm = p.sb("negm", [128, 4, 512], BF16)
    cb = Buf("const")
    GMIX = p.sb("GMIX", [128, 2, 8], F32); GFFN = p.sb("GFFN", [128, 2, 8], F32); GFIN = p.sb("GFIN", [128, 8], F32)
    BIAS = p.sb("BIAS", [128, 8], F32); PSC = p.sb("PSC", [128, 4], F32); CW = p.sb("CW", [128, 3, 4], F32)
    WPb = p.sb("WPb", [128, 4, 128], BF16); WMT = p.sb("WMT", [128, 4, 128], BF16)
    BSH = p.sb("BSH", [2, 4, 128], BF16); BSL = p.sb("BSL", [2, 4, 128], BF16)
    ones2 = p.sb("ones2", [2, 128], BF16)
    RC0 = p.sb("RC0", [128, 16], F32)
    HALO = p.sb("HALO", [128, 4, 16], F32); HALOb = Buf()
    ZHALO = p.sb("ZHALO", [128, 4, 2], F32); ZHALOb = Buf()
    PTT = p.sb("PTT", [128, DB], I32); PTF = p.sb("PTF", [128, DB], F32); PBASE = p.sb("PBASE", [128, 1], F32)
    OH = p.sb("OH", [DB, PPC], BF16); OHb = Buf()
    IOTAP = p.sb("IOTAP", [128, PPC], F32)
    ESEL = p.sb("ESEL", [128, DB, DB], BF16)
    BIASG = p.sb("BIASG", [128, G, 8], F32)
    BD128 = p.sb("BD128", [128, 512], F32); SEL4 = p.sb("SEL4", [128, 4], F32)

    d_const = p.dsem("d_const"); d_misc = p.dsem("d_misc"); d_toks = p.dsem("d_toks")
    _bds = {}

    def dsof(buf):
        if id(buf) not in _bds:
            _bds[id(buf)] = p.dsem("db%d" % len(_bds))
        return _bds[id(buf)]
    d_wr = [p.dsem("d_wr%d" % i) for i in range(NSLOT)]
    d_cc = p.dsem("d_cc", unit=1)
    d_pc = p.dsem("d_pc")

    def MM(out, lhsT, rhs, start, stop, reads, writes):
        return p.op("pe", lambda e: e.matmul(out, lhsT=lhsT, rhs=rhs, start=start, stop=stop), reads, writes)

    def TR(out, in_, reads, writes):
        k = in_.shape[0]
        return p.op("pe", lambda e: e.transpose(out, in_, ident[:k, :k]), list(reads) + [cb], writes)

    def ACT(out, in_, func, reads, writes, bias=None, scale=None):
        kw = {}
        if bias is not None:
            kw["bias"] = bias
        if scale is not None:
            kw["scale"] = scale
        return p.op("act", lambda e: e.activation(out=out, in_=in_, func=func, **kw), reads, writes)

    def TT(eng, out, in0, in1, op, reads, writes):
        return p.op(eng, lambda e: e.tensor_tensor(out=out, in0=in0, in1=in1, op=op), reads, writes)

    def TS(eng, out, in0, s1, s2, op0, op1, reads, writes):
        if s2 is None:
            return p.op(eng, lambda e: e.tensor_scalar(out=out, in0=in0, scalar1=s1, scalar2=None, op0=op0), reads, writes)
        return p.op(eng, lambda e: e.tensor_scalar(out=out, in0=in0, scalar1=s1, scalar2=s2, op0=op0, op1=op1), reads, writes)

    def STT(out, in0, scalar, in1, op0, op1, reads, writes):
        return p.op("dve", lambda e: e.scalar_tensor_tensor(out=out, in0=in0, scalar=scalar, in1=in1, op0=op0, op1=op1),
                    reads, writes)

    def CP(eng, out, in_, reads, writes):
        if eng == "act":
            return ACT(out, in_, AF.Copy, reads, writes)
        return p.op(eng, lambda e: e.tensor_copy(out=out, in_=in_), reads, writes)

    def MS(eng, ap, val, writes):
        return p.op(eng, lambda e: e.memset(ap, val), (), writes)

    def DMA(q, ds, out, in_, reads, writes, slow=False):
        if slow:
            return p.dma(q, ds, lambda e: e.dma_start(out=out, in_=in_, allow_slow_non_contiguous=True), reads, writes)
        return p.dma(q, ds, lambda e: e.dma_start(out=out, in_=in_), reads, writes)

    ev_i = [0]

    def evac_eng():
        ev_i[0] += 1
        return "act" if ev_i[0] % 2 else "dve"

    UPTO = cfg.get("upto", 99)
    print("sbuf bytes remaining", cfg["_sbuf0"], "->", nc.sbuf_bytes_remaining)

    class _Done(Exception):
        pass

    def fin():
        p.wait_all_dma("sp")
        p.emit()
        return nc

    MS("pool", ident[:], 1.0, [cb])
    p.op("pool", lambda e: e.affine_select(out=ident[:], in_=ident[:], pattern=[[-1, 128]], compare_op=ALU.is_equal,
                                           fill=0.0, base=0, channel_multiplier=1), [cb], [cb])
    CP("pool", identb[:], ident[:], [cb], [cb])
    MS("pool", onesneg[:], -1.0, [cb])
    MS("pool", ones32[:], 1.0, [cb])
    MS("pool", onesmean[:], 1.0 / 1024.0, [cb])
    MS("pool", ones2[:], 1.0, [cb])
    p.op("pool", lambda e: e.affine_select(out=trineg[:], in_=onesneg[:], pattern=[[-1, 128]], compare_op=ALU.is_ge,
                                           fill=0.0, base=0, channel_multiplier=1), [cb], [cb])
    p.op("pool", lambda e: e.affine_select(out=tripos[:], in_=ones32[:], pattern=[[-1, 128]], compare_op=ALU.is_gt,
                                           fill=0.0, base=0, channel_multiplier=1), [cb], [cb])
    p.op("pool", lambda e: e.affine_select(out=trimask[:], in_=ones32[:], pattern=[[1, 128]], compare_op=ALU.is_ge,
                                           fill=0.0, base=0, channel_multiplier=-1), [cb], [cb])
    big = p32.get()
    MS("pool", big[0][:, 0:512], -30000.0, [big[1]])
    for r in range(4):
        p.op("pool", lambda e, r=r: e.affine_select(out=negm[:, r, :], in_=big[0][:, 0:512], pattern=[[-1, 512]],
                                                    compare_op=ALU.is_ge, fill=0.0, base=r * 128, channel_multiplier=1),
             [big[1]], [cb])
    p32.put(big)
    p.op("pool", lambda e: e.iota(RC0[:], pattern=[[1, 16]], base=1, channel_multiplier=0,
                                  allow_small_or_imprecise_dtypes=True), (), [cb])
    p.op("dve", lambda e: e.reciprocal(out=RC0[:], in_=RC0[:]), [cb], [cb])
    p.op("pool", lambda e: e.iota(IOTAP[:], pattern=[[1, PPC]], base=0, channel_multiplier=0,
                                  allow_small_or_imprecise_dtypes=True), (), [cb])
    MS("pool", ESEL[:], 1.0, [cb])
    p.op("pool", lambda e: e.affine_select(out=ESEL[:], in_=ESEL[:], pattern=[[1, DB], [-1, DB]], compare_op=ALU.is_equal,
                                           fill=0.0, base=0, channel_multiplier=0), [cb], [cb])
    MS("pool", BD128[:], 1.0, [cb])
    MS("pool", SEL4[:], 0.0, [cb])
    for q in range(4):
        p.op("pool", lambda e, q=q: e.affine_select(out=BD128[q * 32:(q + 1) * 32, :].rearrange("p (h d) -> p h d", h=8),
                                                    in_=BD128[q * 32:(q + 1) * 32, :].rearrange("p (h d) -> p h d", h=8),
                                                    pattern=[[-1, 8], [0, 64]], compare_op=ALU.is_equal, fill=0.0, base=0,
                                                    channel_multiplier=1), [cb], [cb])
        MS("pool", SEL4[q * 32:(q + 1) * 32, q:q + 1], 1.0, [cb])
    MS("pool", HALO[:], 0.0, [HALOb])
    MS("pool", ZHALO[:], 0.0, [ZHALOb])
    for l in range(2):
        DMA("sp", d_const, GMIX[:, l, :], norm_mix[l].rearrange("(c q) -> q c", q=128), (), [cb], slow=True)
        DMA("sp", d_const, GFFN[:, l, :], norm_ffn[l].rearrange("(c q) -> q c", q=128), (), [cb], slow=True)
    DMA("sp", d_const, GFIN[:], norm_final[0].rearrange("(c q) -> q c", q=128), (), [cb], slow=True)
    DMA("sp", d_const, BIAS[:], sb_bias.to_broadcast([128, 8]), (), [cb])
    DMA("sp", d_const, PSC[:], pool_scale[0].rearrange("(c q) -> q c", q=128), (), [cb], slow=True)
    for j in range(3):
        DMA("sp", d_const, CW[:, j, :], conv_w[j].rearrange("(c q) -> q c", q=128), (), [cb], slow=True)
    DMA("sp", d_const, PTT[:NPG, :], ptT, (), [cb])
    DMA("sp", d_const, PBASE[:], pbase, (), [cb])
    t32 = p32.get()
    DMA("sp", dsof(t32[1]), t32[0][:, 0:512].rearrange("p (g d) -> p g d", g=4), w_pool.rearrange("(g c) d -> c g d", g=4), (), [t32[1]])
    CP("dve", WPb[:], t32[0][:, 0:512].rearrange("p (g d) -> p g d", g=4), [t32[1]], [cb])
    p32.put(t32)
    t32 = p32.get()
    DMA("sp", dsof(t32[1]), t32[0][:, 0:512].rearrange("p (g d) -> p g d", g=4), w_s.rearrange("(g t) s -> t g s", g=4), (), [t32[1]])
    tb_ = psum.get()
    for g in range(4):
        TR(tb_[0][:, g * 128:(g + 1) * 128], t32[0][:, g * 128:(g + 1) * 128], [t32[1]], [tb_[1]])
    t32b = p32.get()
    CP("dve", t32b[0][:, 0:512], tb_[0][:, :], [tb_[1]], [t32b[1]])
    psum.put(tb_)
    for g in range(4):
        TT("dve", WMT[:, g, :], t32b[0][:, g * 128:(g + 1) * 128], trimask[:], ALU.mult, [t32b[1], cb], [cb])
    if DBG:
        DMA("sp", d_misc, dbg_wmt0, WMT[:].rearrange("p g t -> p (g t)"), [cb], ())
        DMA("sp", d_misc, dbg_t32, t32[0][:, 0:512], [t32[1]], ())
        DMA("sp", d_misc, dbg_t32b, t32b[0][:, 0:512], [t32b[1]], ())
    p32.put(t32); p32.put(t32b)
    t32 = p32.get(); t32b = p32.get()
    DMA("sp", dsof(t32[1]), t32[0][0:1, 0:512], b_s.rearrange("(o g) t -> o (g t)", o=1), (), [t32[1]])
    CP("dve", BSH[0:1, :, :].rearrange("p g t -> p (g t)"), t32[0][0:1, 0:512], [t32[1]], [cb])
    CP("dve", t32b[0][0:1, 0:512], BSH[0:1, :, :].rearrange("p g t -> p (g t)"), [cb], [t32b[1]])
    TT("dve", t32b[0][0:1, 0:512], t32[0][0:1, 0:512], t32b[0][0:1, 0:512], ALU.subtract, [t32[1], t32b[1]], [t32b[1]])
    CP("dve", BSL[0:1, :, :].rearrange("p g t -> p (g t)"), t32b[0][0:1, 0:512], [t32b[1]], [cb])
    p32.put(t32); p32.put(t32b)
    for g in range(G):
        CP("pool", BIASG[:, g, :], BIAS[:], [cb], [cb])

    wsb = {k: [Buf("ws_%s_%d" % (k, c)) for c in range(wsrc[k][1])] for k in WS}
    for k, (src, kc, m) in wsrc.items():
        for c in range(kc):
            DMA("pool", d_pc, WS[k][:, c, :], src[c * 128:(c + 1) * 128, :], (), [wsb[k][c]])

    def wload(k, c0, mc):
        kc = wsrc[k][1]
        i = wr_i[0] % NSLOT
        wr_i[0] += 1
        slot, sbuf = WR[i]
        view = slot[:, 0:kc * mc].rearrange("p (k m) -> p k m", k=kc)
        DMA("sp", d_wr[i], view, WS[k][:, :, c0:c0 + mc], wsb[k], [sbuf])
        return view, sbuf

    def dense_fm(k, c0, nm, rhs, rhs_bufs, N, epi, mc=512):
        kc = wsrc[k][1]
        mper = mc // 128
        m = 0
        while m < nm:
            nmm = min(mper, nm - m)
            view, sbuf = wload(k, c0 + m * 128, nmm * 128)
            for mi in range(nmm):
                bank = psum.get()
                for c in range(kc):
                    MM(bank[0][:, 0:N], view[:, c, mi * 128:(mi + 1) * 128], rhs(c), c == 0, c == kc - 1,
                       [sbuf, rhs_bufs[c]], [bank[1]])
                epi(m + mi, bank)
                psum.put(bank)
            m += nmm

    def rmsnorm(Xt, Xbufs, gcol, Ht, Hbufs, N, out32=None):
        ms = psum.get()
        for c in range(8):
            sq = p16.get()
            ACT(sq[0][:, 0:N], Xt[:, c, 0:N], AF.Square, [Xbufs[c]], [sq[1]])
            MM(ms[0][:, 0:N], onesmean[:], sq[0][:, 0:N], c == 0, c == 7, [sq[1], cb], [ms[1]])
            p16.put(sq)
        rstd = p32.get()
        ACT(rstd[0][:, 0:N], ms[0][:, 0:N], AF.Ln, [ms[1]], [rstd[1]], bias=EPS)
        psum.put(ms)
        ACT(rstd[0][:, 0:N], rstd[0][:, 0:N], AF.Exp, [rstd[1]], [rstd[1]], scale=-0.5)
        if out32 is not None:
            return rstd
        for c in range(8):
            STT(Ht[:, c, 0:N], Xt[:, c, 0:N], gcol(c), rstd[0][:, 0:N], ALU.mult, ALU.mult,
                [Xbufs[c], rstd[1], cb], [Hbufs[c]])
        p32.put(rstd)
        return None

    def ffn(l, Xt, Xbufs, Ht, Hbufs, N, GFt, GFbufs, nhalf):
        rmsnorm(Xt, Xbufs, lambda c: GFFN[:, l, c:c + 1], Ht, Hbufs, N)
        per = KFF // nhalf
        for hf in range(nhalf):
            m0 = hf * per
            mm = 0
            gfl = None
            if GFt is None:
                gfl = [p16.get() for _ in range(per)]
            while mm < per:
                nmm = min(4, per - mm)
                gview, gsb = wload("gate%d" % l, (m0 + mm) * 128, nmm * 128)
                uview, usb = wload("up%d" % l, (m0 + mm) * 128, nmm * 128)
                for mi in range(nmm):
                    bg_ = psum.get(); bu_ = psum.get()
                    for c in range(8):
                        MM(bg_[0][:, 0:N], gview[:, c, mi * 128:(mi + 1) * 128], Ht[:, c, 0:N], c == 0, c == 7,
                           [gsb, Hbufs[c]], [bg_[1]])
                    for c in range(8):
                        MM(bu_[0][:, 0:N], uview[:, c, mi * 128:(mi + 1) * 128], Ht[:, c, 0:N], c == 0, c == 7,
                           [usb, Hbufs[c]], [bu_[1]])
                    st = p32.get()
                    ACT(st[0][:, 0:N], bg_[0][:, 0:N], AF.Silu, [bg_[1]], [st[1]])
                    psum.put(bg_)
                    if gfl is not None:
                        TT("dve", gfl[mm + mi][0][:, 0:N], st[0][:, 0:N], bu_[0][:, 0:N], ALU.mult, [st[1], bu_[1]],
                           [gfl[mm + mi][1]])
                    else:
                        TT("dve", GFt[:, mm + mi, 0:N], st[0][:, 0:N], bu_[0][:, 0:N], ALU.mult, [st[1], bu_[1]],
                           [GFbufs[mm + mi]])
                    psum.put(bu_)
                    p32.put(st)
                mm += nmm
            dstep = 2 if per <= 16 else 1
            for mo in range(0, 8, dstep):
                i = wr_i[0] % NSLOT
                wr_i[0] += 1
                slot, sbuf = WR[i]
                view = slot[:, 0:per * 128 * dstep].rearrange("p (k m) -> p k m", k=per)
                DMA("sp", d_wr[i], view, WS["down%d" % l][:, m0:m0 + per, mo * 128:(mo + dstep) * 128], wsb["down%d" % l], [sbuf])
                for mi in range(dstep):
                    bank = psum.get()
                    for c in range(per):
                        if gfl is not None:
                            MM(bank[0][:, 0:N], view[:, c, mi * 128:(mi + 1) * 128], gfl[c][0][:, 0:N], c == 0, c == per - 1,
                               [sbuf, gfl[c][1]], [bank[1]])
                        else:
                            MM(bank[0][:, 0:N], view[:, c, mi * 128:(mi + 1) * 128], GFt[:, c, 0:N], c == 0, c == per - 1,
                               [sbuf, GFbufs[c]], [bank[1]])
                    TT("dve", Xt[:, mo + mi, 0:N], bank[0][:, 0:N], Xt[:, mo + mi, 0:N], ALU.add,
                       [bank[1], Xbufs[mo + mi]], [Xbufs[mo + mi]])
                    psum.put(bank)
            if gfl is not None:
                for it in gfl:
                    p16.put(it)

    def outproj(k, MIXt, MIXbufs, Xt, Xbufs, N):
        def epi(m, bank):
            TT("dve", Xt[:, m, 0:N], bank[0][:, 0:N], Xt[:, m, 0:N], ALU.add, [bank[1], Xbufs[m]], [Xbufs[m]])
        dense_fm(k, 0, 8, lambda c: MIXt[:, c, 0:N], MIXbufs, N, epi)

    if UPTO == 1:
        return fin()
    for hf in range(2):
        xt = p32.get()
        DMA("sp", dsof(xt[1]), xt[0][0:DB, 0:512], xs[:, hf * 512:(hf + 1) * 512], (), [xt[1]])
        bank = psum.get()
        for c4 in range(4):
            TR(bank[0][:, c4 * 32:c4 * 32 + DB], xt[0][0:DB, c4 * 128:(c4 + 1) * 128], [xt[1]], [bank[1]])
        p32.put(xt)
        for c4 in range(4):
            CP("dve", Xs[:, hf * 4 + c4, 0:DB], bank[0][:, c4 * 32:c4 * 32 + DB], [bank[1]], [Xsb[hf * 4 + c4]])
        psum.put(bank)
    rmsnorm(Xs, Xsb, lambda c: GMIX[:, 0, c:c + 1], Hs, Hsb, DB)
    for pc in range(4):
        view, sbuf = wload("in0", pc * 512, 512)
        bank = psum.get()
        for c in range(8):
            MM(bank[0][0:DB, :], Hs[:, c, 0:DB], view[:, c, :], c == 0, c == 7, [sbuf, Hsb[c]], [bank[1]])
        CP("dve", TOKS[0:DB, pc * 512:(pc + 1) * 512], bank[0][0:DB, :], [bank[1]], [TOKSb])
        psum.put(bank)
    DMA("pool", d_toks, ksm, TOKS[0:OWN, 1024:1536], [TOKSb], ())
    DMA("pool", d_toks, vsm, TOKS[0:OWN, 1536:2048], [TOKSb], ())
    DMA("pool", d_toks, pools[:, 14, :], TOKS[0:OWN, 0:512], [TOKSb], ())
    DMA("pool", d_misc, pools[:, 0:14, :], spst[:, 1:15, :], (), ())
    TS("dve", QSb16[0:DB, :], TOKS[0:DB, 512:1024], 0.125, None, ALU.mult, None, [TOKSb], [QSbb])
    stt_ = p32.get()
    DMA("sp", dsof(stt_[1]), stt_[0][0:OWN * 15, 0:512], spst.rearrange("b j c -> (b j) c"), (), [stt_[1]])
    ST = p32.get()
    bank = psum.get()
    for g in range(4):
        TR(bank[0][:, g * 64:g * 64 + OWN * 15], stt_[0][0:OWN * 15, g * 128:(g + 1) * 128], [stt_[1]], [bank[1]])
    p32.put(stt_)
    CP("dve", ST[0][:, 0:256], bank[0][:, 0:256], [bank[1]], [ST[1]])
    psum.put(bank)
    US = p32.get()
    view, sbuf = wload("in0", 0, 512)
    bank = psum.get()
    for g in range(4):
        for c in range(8):
            MM(bank[0][:, g * 4:g * 4 + OWN], view[:, c, g * 128:(g + 1) * 128], Hs[:, c, 0:OWN], c == 0, c == 7,
               [sbuf, Hsb[c]], [bank[1]])
    CP("dve", US[0][:, 0:16], bank[0][:, 0:16], [bank[1]], [US[1]])
    psum.put(bank)
    DS = p16.get()
    for g in range(4):
        w = 2 << g
        ssum = p32.get()
        p.op("dve", lambda e, g=g, w=w, ssum=ssum: e.tensor_reduce(
            out=ssum[0][:, 0:OWN], in_=ST[0][:, g * 64:g * 64 + OWN * 15].rearrange("p (b j) -> p b j", j=15)[:, :, 16 - w:15],
            axis=AX.X, op=ALU.add), [ST[1]], [ssum[1]])
        TT("dve", ssum[0][:, 0:OWN], ssum[0][:, 0:OWN], US[0][:, g * 4:g * 4 + OWN], ALU.add, [ssum[1], US[1]], [ssum[1]])
        STT(DS[0][:, g * 4:g * 4 + OWN], ssum[0][:, 0:OWN], 1.0 / w, US[0][:, g * 4:g * 4 + OWN], ALU.mult, ALU.subtract,
            [ssum[1], US[1]], [DS[1]])
        p32.put(ssum)
    bank = psum.get()
    for g in range(4):
        MM(bank[0][:, g * 4:g * 4 + OWN], WPb[:, g, :], DS[0][:, g * 4:g * 4 + OWN], True, True, [DS[1], cb], [bank[1]])
    for g in range(4):
        TS("dve", MIXs[:, g, 0:OWN], bank[0][:, g * 4:g * 4 + OWN], PSC[:, g:g + 1], None, ALU.mult, None,
           [bank[1], cb], [MIXsb[g]])
    psum.put(bank)
    p16.put(DS); p32.put(ST); p32.put(US)

    if UPTO == 2:
        return fin()
    CP("dve", PTF[:NPG, :], PTT[:NPG, :], [cb], [cb])
    TS("dve", PTF[:NPG, :], PTF[:NPG, :], PBASE[:NPG, 0:1], None, ALU.subtract, None, [cb], [cb])
    CH = 128
    for c0 in range(0, PPC, CH):
        cw = min(CH, PPC - c0)
        cmpb = Buf()
        CMP = AR16[:, 0:DB * cw].rearrange("p (b q) -> p b q", b=DB)
        p.op("dve", lambda e, CMP=CMP, c0=c0, cw=cw: e.tensor_tensor(
            out=CMP[:NPG], in0=IOTAP[:NPG, c0:c0 + cw].unsqueeze(1).to_broadcast([NPG, DB, cw]),
            in1=PTF[:NPG, :].unsqueeze(2).to_broadcast([NPG, DB, cw]), op=ALU.is_equal), [cb, phase], [cmpb])
        bank = psum.get()
        for b in range(DB):
            MM(bank[0][0:DB, 0:cw], ESEL[:NPG, b, :], CMP[:NPG, b, :], b == 0, b == DB - 1, [cmpb, cb, phase], [bank[1]])
        CP("dve", OH[:, c0:c0 + cw], bank[0][0:DB, 0:cw], [bank[1]], [OHb])
        psum.put(bank)
        p.op("pool", lambda e: e.memset(AR16[0:1, 0:2], 0.0), [cmpb, phase], [cmpb])

    p.op("pool", lambda e: e.memset(AR16[0:1, 0:2], 0.0), [], [phase])
    NKS = 3
    VBOFF = NKS * G * 512
    kslots = [(ARENA[:, s * G * 512:(s + 1) * G * 512].rearrange("p (g f) -> p g f", g=G), Buf()) for s in range(NKS)]
    vbslots = [(AR16[:, 2 * VBOFF + s * G * 512:2 * VBOFF + (s + 1) * G * 512].rearrange("p (g f) -> p g f", g=G), Buf()) for s in range(2)]
    OHR = [p.sb("OHR%d" % s, [DB, G * 128], BF16) for s in range(2)]
    ohrslots = [(OHR[s][:, :].rearrange("p (g t) -> p g t", g=G), Buf()) for s in range(2)]
    def sampA(gi):
        s = gi % 2
        kt, kb_ = kslots[gi % NKS]; vbt, vbb_ = vbslots[s]; oht, ohb_ = ohrslots[s]
        pg0 = gi * G
        ck = cks[pg0 // PPS]; cv = cvs[pg0 // PPS]; pl0 = pg0 % PPS
        DMA("sp", dsof(kb_), kt, ck[pl0 * 128:(pl0 + G) * 128, :].rearrange("(g t) f -> t g f", g=G), [phase], [kb_])
        DMA("pool", dsof(vbb_), vbt, cv[pl0 * 128:(pl0 + G) * 128, :].rearrange("(g t) f -> t g f", g=G), [phase], [vbb_])
        CP("pool", oht, OH[:, pg0:pg0 + G].unsqueeze(2).to_broadcast([DB, G, 128]), [OHb, phase], [ohb_])
        zt = p32.get()
        for g in range(G):
            qb = psum.get()
            MM(qb[0][:, :], oht[:, g, :], QSb16[0:DB, :], True, True, [ohb_, QSbb, phase], [qb[1]])
            pr = p32.get()
            TT("dve", pr[0][:, 0:512], kt[:, g, :], qb[0][:, :], ALU.mult, [kb_, qb[1], phase], [pr[1]])
            psum.put(qb)
            p.op("dve", lambda e, pr=pr, zt=zt, g=g: e.tensor_reduce(
                out=zt[0][:, g * 8:(g + 1) * 8], in_=pr[0][:, 0:512].rearrange("p (h d) -> p h d", h=8), axis=AX.X,
                op=ALU.add), [pr[1]], [zt[1]])
            p32.put(pr)
        return (zt, vbt, vbb_, pg0)

    def sampB(ctx):
        zt, vbt, vbb_, pg0 = ctx
        NC8 = G * 8
        TT("dve", zt[0][:, 0:NC8], zt[0][:, 0:NC8], BIASG[:].rearrange("p g h -> p (g h)"), ALU.add, [zt[1], cb], [zt[1]])
        et = p32.get()
        ACT(et[0][:, 0:NC8], zt[0][:, 0:NC8], AF.Exp, [zt[1]], [et[1]])
        spb = p16.get()
        ACT(spb[0][:, 0:NC8], et[0][:, 0:NC8], AF.Ln, [et[1]], [spb[1]], bias=1.0)
        p32.put(et)
        lat = psum.get()
        MM(lat[0][:, 0:NC8], trineg[:], spb[0][:, 0:NC8], True, True, [spb[1], cb], [lat[1]])
        tot = psum.get()
        MM(tot[0][0:1, 0:NC8], onesneg[:, 0:1], spb[0][:, 0:NC8], True, True, [spb[1], cb], [tot[1]])
        p16.put(spb)
        TT("dve", zt[0][:, 0:NC8], lat[0][:, 0:NC8], zt[0][:, 0:NC8], ALU.add, [lat[1], zt[1]], [zt[1]])
        psum.put(lat)
        wt = p16.get()
        ACT(wt[0][:, 0:NC8], zt[0][:, 0:NC8], AF.Exp, [zt[1]], [wt[1]])
        p32.put(zt)
        tots = p32.get()
        CP("act", tots[0][0:1, 0:NC8], tot[0][0:1, 0:NC8], [tot[1]], [tots[1]])
        psum.put(tot)
        DMA("pool", dsof(tots[1]), rec_t_loc[pg0:pg0 + G, :].rearrange("(o g) h -> o (g h)", o=1), tots[0][0:1, 0:NC8], [tots[1]], [phase] if False else ())
        p32.put(tots)
        for sg in range(G // 4):
            ot = psum.get()
            for q in range(4):
                pg = sg * 4 + q
                p.op("pe", lambda e, ot=ot, q=q, pg=pg, vbt=vbt, wt=wt: e.matmul(
                    ot[0][q * 32:q * 32 + 8, :], lhsT=wt[0][:, pg * 8:(pg + 1) * 8], rhs=vbt[:, pg, :], start=True, stop=True,
                    tile_position=(0, q * 32)), [vbb_, wt[1], phase], [ot[1]])
            msk = p32.get()
            TT("dve", msk[0][:, 0:512], ot[0][:, :], BD128[:], ALU.mult, [ot[1], cb], [msk[1]])
            psum.put(ot)
            o2 = psum.get()
            MM(o2[0][0:4, :], SEL4[:], msk[0][:, 0:512], True, True, [msk[1], cb], [o2[1]])
            p32.put(msk)
            orow = p32.get()
            CP("act", orow[0][0:4, 0:512], o2[0][0:4, :], [o2[1]], [orow[1]])
            psum.put(o2)
            DMA("pool", dsof(orow[1]), rec_o_loc.rearrange("(q h) d -> q (h d)", h=8)[pg0 + sg * 4:pg0 + sg * 4 + 4, :],
                orow[0][0:4, 0:512], [orow[1]], ())
            p32.put(orow)
        p16.put(wt)
    ctx_ = sampA(0)
    for gi in range(NGRP):
        nxt_ = sampA(gi + 1) if gi + 1 < NGRP else None
        sampB(ctx_)
        ctx_ = nxt_
    recb = Buf("rec")

    def allgather(groups, src, dst, first, rb, wb):
        p.dma("pool", d_cc, lambda e: e.collective_compute("AllGather", ALU.bypass, replica_groups=groups, ins=[src],
                                                           outs=[dst]), rb, wb)
        if first:
            for _it in p32.free:
                _d = dsof(_it[1])
                p.ops["pool"][-1].waits.append(("d", _d, _d.unit * _d.cnt))

    CR = CPG * 8
    stage2 = []
    if NCORES == 1:
        DMA("pool", d_misc, rec_o_all, rec_o_loc, [], [recb])
        for _it in p32.free:
            _d = dsof(_it[1])
            p.ops["pool"][-1].waits.append(("d", _d, _d.unit * _d.cnt))
        DMA("pool", d_misc, rec_t_all, rec_t_loc, [], [recb])
    elif NCORES == 8:
        g1 = [[0, 1, 2, 3], [4, 5, 6, 7]]; g2 = [[0, 4], [1, 5], [2, 6], [3, 7]]
        rec_t_half = dint("rec_t_half", [4 * PPC, 8])
        hb = Buf()
        allgather(g1, rec_t_loc, rec_t_half, True, [], [hb])
        stage2.append((g2, rec_t_half, rec_t_all, hb))
        for k in range(NCH):
            half = dint("rec_o_half%d" % k, [4 * CR, 64])
            hb = Buf()
            allgather(g1, rec_o_loc[k * CR:(k + 1) * CR, :], half, False, [], [hb])
            stage2.append((g2, half, rec_o_all[k * NCORES * CR:(k + 1) * NCORES * CR, :], hb))
    else:
        rg = [list(range(NCORES))]
        allgather(rg, rec_t_loc, rec_t_all, True, [], [recb])
        for k in range(NCH):
            allgather(rg, rec_o_loc[k * CR:(k + 1) * CR, :], rec_o_all[k * NCORES * CR:(k + 1) * NCORES * CR, :], False, [], [recb])

    def issue_stage2():
        for (g, src, dst, hb) in stage2:
            allgather(g, src, dst, False, [hb], [recb])
        stage2.clear()

    PGF = p.sb("PGF", [128, DB], F32); RNK = p.sb("RNK", [128, DB], F32); CHK = p.sb("CHK", [128, DB], F32)
    IDXO = p.sb("IDXO", [128, DB], I32)
    ib = Buf("idx")
    CP("dve", PGF[:NPG, :], PTT[:NPG, :], [cb], [ib])
    MS("dve", RNK[:NPG, :], 0.0, [ib])
    MS("dve", CHK[:NPG, :], 0.0, [ib])
    for r in range(1, NCORES):
        STT(RNK[:NPG, :], PGF[:NPG, :], float(r * PPC), RNK[:NPG, :], ALU.is_ge, ALU.add, [ib], [ib])
    STT(PGF[:NPG, :], RNK[:NPG, :], float(-PPC), PGF[:NPG, :], ALU.mult, ALU.add, [ib], [ib])
    for k in range(1, NCH):
        STT(CHK[:NPG, :], PGF[:NPG, :], float(k * CPG), CHK[:NPG, :], ALU.is_ge, ALU.add, [ib], [ib])
    STT(PGF[:NPG, :], CHK[:NPG, :], float((NCORES - 1) * CPG), PGF[:NPG, :], ALU.mult, ALU.add, [ib], [ib])
    STT(PGF[:NPG, :], RNK[:NPG, :], float(CPG), PGF[:NPG, :], ALU.mult, ALU.add, [ib], [ib])
    CP("dve", IDXO[:NPG, :], PGF[:NPG, :], [ib], [ib])
    p.op("pool", lambda e: e.memset(AR16[0:1, 0:2], 0.0), [], [phase])

    if UPTO == 3:
        return fin()
    def prompt_tile(t):
        N = 512
        r0 = t * 512
        for hf in range(2):
            banks = [psum.get() for _ in range(4)]
            for tb in range(4):
                xt = p32.get()
                DMA("sp", dsof(xt[1]), xt[0][:, 0:512], xp[r0 + tb * 128:r0 + (tb + 1) * 128, hf * 512:(hf + 1) * 512], (), [xt[1]])
                for c4 in range(4):
                    TR(banks[c4][0][:, tb * 128:(tb + 1) * 128], xt[0][:, c4 * 128:(c4 + 1) * 128], [xt[1]], [banks[c4][1]])
                p32.put(xt)
            for c4 in range(4):
                CP(evac_eng(), X[:, hf * 4 + c4, :], banks[c4][0][:, :], [banks[c4][1]], [Xb[hf * 4 + c4]])
                psum.put(banks[c4])
        if UPTO == 10:
            raise _Done()
        rmsnorm(X, Xb, lambda c: GMIX[:, 0, c:c + 1], H, Hb, N)
        if UPTO == 11:
            raise _Done()
        U = [p32.get() for _ in range(4)]

        def epi_u(m, bank):
            CP("dve", U[m][0][:, 0:16], HALO[:, m, :], [HALOb], [U[m][1]])
            CP(evac_eng(), U[m][0][:, 16:528], bank[0][:, :], [bank[1]], [U[m][1]])
        dense_fm("in0", 0, 4, lambda c: H[:, c, :], Hb, N, epi_u)
        if UPTO == 111:
            raise _Done()

        def epi_q(m, bank):
            ACT(QT[:, m, :], bank[0][:, :], AF.Copy, [bank[1]], [QTb[m]], scale=0.125)
        dense_fm("in0", 512, 4, lambda c: H[:, c, :], Hb, N, epi_q)
        if UPTO == 112:
            raise _Done()
        view, sbuf = wload("in0", 1024, 512)
        for m in range(4):
            bank = psum.get()
            for c in range(8):
                MM(bank[0][:, :], view[:, c, m * 128:(m + 1) * 128], H[:, c, :], c == 0, c == 7, [sbuf, Hb[c]], [bank[1]])
            CP(evac_eng(), KT[:, m, r0:r0 + 512], bank[0][:, :], [bank[1], phase], [KTb[m][t]])
            psum.put(bank)
        if UPTO == 113:
            raise _Done()
        for tb in range(4):
            bank = psum.get()
            for c in range(8):
                MM(bank[0][:, :], H[:, c, tb * 128:(tb + 1) * 128], view[:, c, :], c == 0, c == 7, [sbuf, Hb[c]], [bank[1]])
            stg = p32.get()
            CP(evac_eng(), stg[0][:, 0:512], bank[0][:, :], [bank[1]], [stg[1]])
            psum.put(bank)
            DMA("pool", dsof(stg[1]), kp[r0 + tb * 128:r0 + (tb + 1) * 128, :], stg[0][:, 0:512], [stg[1]], ())
            p32.put(stg)
        if UPTO == 114:
            raise _Done()
        view, sbuf = wload("in0", 1536, 512)
        for tb in range(4):
            bank = psum.get()
            for c in range(8):
                MM(bank[0][:, :], H[:, c, tb * 128:(tb + 1) * 128], view[:, c, :], c == 0, c == 7, [sbuf, Hb[c]], [bank[1]])
            stg = p32.get()
            CP("act", stg[0][:, 0:512], bank[0][:, :], [bank[1]], [stg[1]])
            psum.put(bank)
            CP("pool", VB[:, t * 4 + tb, :], stg[0][:, 0:512], [stg[1], phase], [VBb[t * 4 + tb]])
            DMA("pool", dsof(stg[1]), vp[r0 + tb * 128:r0 + (tb + 1) * 128, :], stg[0][:, 0:512], [stg[1]], ())
            p32.put(stg)
        if UPTO == 12:
            raise _Done()
        for g in range(4):
            w = 2 << g
            ub = U[g]
            cur = ub
            tmps = []
            for k in range(g):
                sh = 1 << k
                lo = 1 + (2 << k) - 1
                nxt = p32.get()
                tmps.append(nxt)
                TT("pool", nxt[0][:, lo:528], cur[0][:, lo:528], cur[0][:, lo - sh:528 - sh], ALU.add, [cur[1]], [nxt[1]])
                cur = nxt
            sh = 1 << g
            ssum = p32.get()
            TT("pool", ssum[0][:, 0:512], cur[0][:, 16:528], cur[0][:, 16 - sh:528 - sh], ALU.add, [cur[1]], [ssum[1]])
            for tm_ in tmps:
                p32.put(tm_)
            dd = p16.get()
            STT(dd[0][:, :], ssum[0][:, 0:512], 1.0 / w, ub[0][:, 16:528], ALU.mult, ALU.subtract, [ssum[1], ub[1]], [dd[1]])
            if t == 0:
                fx = p32.get()
                TT("dve", fx[0][:, 0:w - 1], ssum[0][:, 0:w - 1], RC0[:, 0:w - 1], ALU.mult, [ssum[1], cb], [fx[1]])
                TT("dve", dd[0][:, 0:w - 1], fx[0][:, 0:w - 1], ub[0][:, 16:16 + w - 1], ALU.subtract, [fx[1], ub[1]], [dd[1]])
                p32.put(fx)
            p32.put(ssum)
            CP("pool", HALO[:, g, 1:16], ub[0][:, 513:528], [ub[1]], [HALOb])
            if t == NT - 1:
                DMA("pool", dsof(ub[1]), poolp[:, g * 128:(g + 1) * 128].rearrange("j c -> c j"), ub[0][:, 513:528], [ub[1]], (), slow=True)
            bank = psum.get()
            MM(bank[0][:, :], WPb[:, g, :], dd[0][:, :], True, True, [dd[1], cb], [bank[1]])
            p16.put(dd)
            TS("dve", MIX[:, g, :], bank[0][:, :], PSC[:, g:g + 1], None, ALU.mult, None, [bank[1], cb], [MIXb[g]])
            psum.put(bank)
            p32.put(ub)
        if UPTO == 13:
            raise _Done()
        nkb = 4 * t + 4
        for hg in range(2):
            av = [psum.get(), psum.get()]
            lacc = [p32.get() for _ in range(4)]
            laccb = [p16.get() for _ in range(4)]
            units = [(kbi, pr_) for kbi in range(nkb) for pr_ in range(2)]

            def P1(u):
                kbi, pr_ = u
                kb = nkb - 1 - kbi
                diag = kb >= 4 * t
                r = kb - 4 * t
                kt_ = kb // 4
                st_ = []
                for hh in (2 * pr_, 2 * pr_ + 1):
                    h = hg * 4 + hh
                    j = h // 2
                    po = (h % 2) * 64
                    kT = KT[po:po + 64, j, kb * 128:(kb + 1) * 128]
                    qT = QT[po:po + 64, j, :]
                    zb = psum.get()
                    MM(zb[0][:, :], kT, qT, True, not diag, [KTb[j][kt_], QTb[j], phase], [zb[1]])
                    if diag:
                        MM(zb[0][:, :], identb[:], negm[:, r, :], False, True, [cb], [zb[1]])
                    et = p32.get()
                    ACT(et[0][:, 0:512], zb[0][:, :], AF.Exp, [zb[1], cb], [et[1]], bias=BIAS[:, h:h + 1])
                    spb = p16.get()
                    ACT(spb[0][:, :], et[0][:, 0:512], AF.Ln, [et[1]], [spb[1]], bias=1.0)
                    p32.put(et)
                    st_.append((hh, h, spb, zb))
                return (kbi, kb, st_)

            def P2a(state):
                kbi, kb, st_ = state
                wts = []
                for (hh, h, spb, b2) in st_:
                    MM(b2[0][:, :], trineg[:], spb[0][:, :], False, kbi == 0, [spb[1], cb], [b2[1]])
                    if kbi > 0:
                        MM(b2[0][:, :], onesneg[:], laccb[hh][0][:, :], False, True, [laccb[hh][1], cb], [b2[1]])
                    wT = p16.get()
                    ACT(wT[0][:, :], b2[0][:, :], AF.Exp, [b2[1], cb], [wT[1]], bias=BIAS[:, h:h + 1])
                    psum.put(b2)
                    wts.append(wT)
                return wts

            def P2b(state, wts):
                kbi, kb, st_ = state
                for (hh, h, spb, b2), wT in zip(st_, wts):
                    MM(av[hh // 2][0][(hh % 2) * 64:(hh % 2) * 64 + 64, :], VB[:, kb, h * 64:(h + 1) * 64], wT[0][:, :],
                       kbi == 0, kbi == nkb - 1, [VBb[kb], wT[1], phase], [av[hh // 2][1]])
                    p16.put(wT)
                    if kbi < nkb - 1:
                        if kbi == 0:
                            CP("dve", lacc[hh][0][:, 0:512], spb[0][:, :], [spb[1]], [lacc[hh][1]])
                        else:
                            TT("dve", lacc[hh][0][:, 0:512], lacc[hh][0][:, 0:512], spb[0][:, :], ALU.add,
                               [lacc[hh][1], spb[1]], [lacc[hh][1]])
                        CP("dve", laccb[hh][0][:, :], lacc[hh][0][:, 0:512], [lacc[hh][1]], [laccb[hh][1]])
                    p16.put(spb)

            states = {}
            for i in range(min(2, len(units))):
                states[i] = P1(units[i])
            for i in range(len(units)):
                wts = P2a(states[i])
                if i + 2 < len(units):
                    states[i + 2] = P1(units[i + 2])
                P2b(states[i], wts)
                del states[i]
            for k2 in range(2):
                CP("dve", MIX[:, 4 + hg * 2 + k2, :], av[k2][0][:, :], [av[k2][1]], [MIXb[4 + hg * 2 + k2]])
                psum.put(av[k2])
            for it in lacc:
                p32.put(it)
            for it in laccb:
                p16.put(it)
        if UPTO == 14:
            raise _Done()
        if DBG and t == NT - 1:
            DMA("sp", d_misc, dbg_mix0, MIX[:].rearrange("p c n -> p (c n)"), MIXb, ())
        outproj("out0", MIX, MIXb, X, Xb, N)
        if UPTO == 15:
            raise _Done()
        ffn(0, X, Xb, H, Hb, N, GF, GFb, 2)
        if UPTO == 16:
            raise _Done()
        rmsnorm(X, Xb, lambda c: GMIX[:, 1, c:c + 1], H, Hb, N)
        UG = [p32.get() for _ in range(4)]

        def epi_ug(m, bank):
            ACT(UG[m][0][:, 0:512], bank[0][:, :], AF.Gelu, [bank[1]], [UG[m][1]])
        dense_fm("in1", 0, 4, lambda c: H[:, c, :], Hb, N, epi_ug)
        VT = [p16.get() for _ in range(4)]
        view, sbuf = wload("in1", 512, 512)
        for tb in range(4):
            bank = psum.get()
            for c in range(8):
                MM(bank[0][:, :], H[:, c, tb * 128:(tb + 1) * 128], view[:, c, :], c == 0, c == 7, [sbuf, Hb[c]], [bank[1]])
            ACT(VT[tb][0][:, :], bank[0][:, :], AF.Gelu, [bank[1]], [VT[tb][1]])
            psum.put(bank)
        for g in range(4):
            bank = psum.get()
            for tb in range(4):
                MM(bank[0][:, tb * 128:(tb + 1) * 128], VT[tb][0][:, g * 128:(g + 1) * 128], WMT[:, g, :], tb == 0, False,
                   [VT[tb][1], cb], [bank[1]])
            for tb in range(4):
                MM(bank[0][:, tb * 128:(tb + 1) * 128], ones2[0:1, :], BSH[0:1, g, :], False, False, [cb], [bank[1]])
                MM(bank[0][:, tb * 128:(tb + 1) * 128], ones2[0:1, :], BSL[0:1, g, :], False, True, [cb], [bank[1]])
            TT("dve", MIX[:, g, :], bank[0][:, :], UG[g][0][:, 0:512], ALU.mult, [bank[1], UG[g][1]], [MIXb[g]])
            psum.put(bank)
        for it in VT:
            p16.put(it)
        for it in UG:
            p32.put(it)
        ZZ = [p32.get() for _ in range(4)]

        def epi_hh(m, bank):
            CP(evac_eng(), ZZ[m][0][:, 2:514], bank[0][:, :], [bank[1]], [ZZ[m][1]])
        dense_fm("in1", 1024, 4, lambda c: H[:, c, :], Hb, N, epi_hh)
        BG = [p32.get() for _ in range(4)]

        def epi_bg(m, bank):
            CP(evac_eng(), BG[m][0][:, 0:512], bank[0][:, :], [bank[1]], [BG[m][1]])
        dense_fm("in1", 1536, 4, lambda c: H[:, c, :], Hb, N, epi_bg)

        def epi_cg(m, bank):
            TT("dve", ZZ[m][0][:, 2:514], bank[0][:, :], ZZ[m][0][:, 2:514], ALU.mult, [bank[1], ZZ[m][1]], [ZZ[m][1]])
            CP("pool", ZZ[m][0][:, 0:2], ZHALO[:, m, :], [ZHALOb], [ZZ[m][1]])
            cv_ = p32.get()
            TS("pool", cv_[0][:, 0:512], ZZ[m][0][:, 0:512], CW[:, 0, m:m + 1], None, ALU.mult, None, [ZZ[m][1], cb], [cv_[1]])
            STT(cv_[0][:, 0:512], ZZ[m][0][:, 1:513], CW[:, 1, m:m + 1], cv_[0][:, 0:512], ALU.mult, ALU.add,
                [ZZ[m][1], cv_[1], cb], [cv_[1]])
            STT(cv_[0][:, 0:512], ZZ[m][0][:, 2:514], CW[:, 2, m:m + 1], cv_[0][:, 0:512], ALU.mult, ALU.add,
                [ZZ[m][1], cv_[1], cb], [cv_[1]])
            TT("pool", MIX[:, 4 + m, :], cv_[0][:, 0:512], BG[m][0][:, 0:512], ALU.mult, [cv_[1], BG[m][1]], [MIXb[4 + m]])
            p32.put(cv_)
            CP("pool", ZHALO[:, m, :], ZZ[m][0][:, 512:514], [ZZ[m][1]], [ZHALOb])
            if t == NT - 1:
                DMA("pool", dsof(ZZ[m][1]), convp[:, m * 128:(m + 1) * 128].rearrange("j c -> c j"), ZZ[m][0][:, 512:514], [ZZ[m][1]], (), slow=True)
        dense_fm("in1", 2048, 4, lambda c: H[:, c, :], Hb, N, epi_cg)
        for it in ZZ:
            p32.put(it)
        for it in BG:
            p32.put(it)
        if UPTO == 17:
            raise _Done()
        if DBG and t == NT - 1:
            DMA("sp", d_misc, dbg_mix, MIX[:].rearrange("p c n -> p (c n)"), MIXb, ())
            DMA("sp", d_misc, dbg_wmt, WMT[:].rearrange("p g t -> p (g t)"), [cb], ())
        outproj("out1", MIX, MIXb, X, Xb, N)
        ffn(1, X, Xb, H, Hb, N, GF, GFb, 2)
        if UPTO == 18:
            raise _Done()
        rstd = rmsnorm(X, Xb, None, None, None, N, out32=True)
        for hf in range(2):
            banks = [psum.get() for _ in range(4)]
            for c4 in range(4):
                c = hf * 4 + c4
                yn = p32.get()
                STT(yn[0][:, 0:512], X[:, c, :], GFIN[:, c:c + 1], rstd[0][:, 0:512], ALU.mult, ALU.mult,
                    [Xb[c], rstd[1], cb], [yn[1]])
                for tb in range(4):
                    TR(banks[tb][0][:, c4 * 128:(c4 + 1) * 128], yn[0][:, tb * 128:(tb + 1) * 128], [yn[1]], [banks[tb][1]])
                p32.put(yn)
            for tb in range(4):
                stg = p32.get()
                CP(evac_eng(), stg[0][:, 0:512], banks[tb][0][:, :], [banks[tb][1]], [stg[1]])
                psum.put(banks[tb])
                DMA("pool", dsof(stg[1]), yp[r0 + tb * 128:r0 + (tb + 1) * 128, hf * 512:(hf + 1) * 512], stg[0][:, 0:512], [stg[1]], ())
                p32.put(stg)
        p32.put(rstd)

    try:
        for t in range(NT):
            prompt_tile(t)
            if t == 0:
                issue_stage2()
    except _Done:
        return fin()
    if UPTO == 30:
        return fin()

    N = OWN
    for b in range(OWN):
        R = p32.get(); Tt = p32.get()
        p.dma("pool", dsof(R[1]), lambda e, R=R, b=b: e.indirect_dma_start(
            out=R[0][:NPG, 0:512], out_offset=None, in_=rec_o_all.rearrange("(q h) d -> q (h d)", h=8),
            in_offset=bass.IndirectOffsetOnAxis(ap=IDXO[:NPG, b:b + 1], axis=0)), [recb, cb, ib], [R[1]])
        p.dma("pool", dsof(Tt[1]), lambda e, Tt=Tt, b=b: e.indirect_dma_start(
            out=Tt[0][:NPG, 0:8], out_offset=None, in_=rec_t_all,
            in_offset=bass.IndirectOffsetOnAxis(ap=PTT[:NPG, b:b + 1], axis=0)), [recb, cb], [Tt[1]])
        sfx = psum.get()
        MM(sfx[0][:NPG, 0:8], tripos[:NPG, :NPG], Tt[0][:NPG, 0:8], True, True, [Tt[1], cb], [sfx[1]])
        cc_ = p32.get()
        ACT(cc_[0][:NPG, 0:8], sfx[0][:NPG, 0:8], AF.Exp, [sfx[1]], [cc_[1]])
        psum.put(sfx)
        TT("dve", R[0][:NPG, 0:512].rearrange("p (h d) -> p h d", h=8), R[0][:NPG, 0:512].rearrange("p (h d) -> p h d", h=8),
           cc_[0][:NPG, 0:8].unsqueeze(2).to_broadcast([NPG, 8, 64]), ALU.mult, [R[1], cc_[1]], [R[1]])
        p32.put(cc_); p32.put(Tt)
        att = psum.get()
        for j in range(4):
            MM(att[0][:, j:j + 1], R[0][:NPG, j * 128:(j + 1) * 128], ones32[:NPG, 0:1], True, True, [R[1], cb], [att[1]])
        for j in range(4):
            CP("dve", MIXs[:, 4 + j, b:b + 1], att[0][:, j:j + 1], [att[1]], [MIXsb[4 + j]])
        psum.put(att)
        p32.put(R)
    outproj("out0", MIXs, MIXsb, Xs, Xsb, N)
    ffn(0, Xs, Xsb, Hs, Hsb, N, GFs, GFsb, 1)
    rmsnorm(Xs, Xsb, lambda c: GMIX[:, 1, c:c + 1], Hs, Hsb, N)
    for pc in range(5):
        view, sbuf = wload("in1", pc * 512, 512)
        bank = psum.get()
        for c in range(8):
            MM(bank[0][0:N, :], Hs[:, c, 0:N], view[:, c, :], c == 0, c == 7, [sbuf, Hsb[c]], [bank[1]])
        if pc < 2:
            ACT(TOKS[0:N, pc * 512:(pc + 1) * 512], bank[0][0:N, :], AF.Gelu, [bank[1]], [TOKSb])
        else:
            CP("dve", TOKS[0:N, pc * 512:(pc + 1) * 512], bank[0][0:N, :], [bank[1]], [TOKSb])
        psum.put(bank)
    DMA("pool", d_toks, chv, TOKS[0:N, 512:1024], [TOKSb], ())
    TT("dve", TOKS[0:N, 1024:1536], TOKS[0:N, 1024:1536], TOKS[0:N, 2048:2560], ALU.mult, [TOKSb], [TOKSb])
    DMA("pool", d_toks, convs[:, 1, :], TOKS[0:N, 1024:1536], [TOKSb], ())
    DMA("pool", d_misc, convs[:, 0, :], scst[:, 1, :], (), ())
    FM = p32.get()
    for sec in range(5):
        view, sbuf = wload("in1", sec * 512, 512)
        bank = psum.get()
        for g in range(4):
            for c in range(8):
                MM(bank[0][:, g * 4:g * 4 + N], view[:, c, g * 128:(g + 1) * 128], Hs[:, c, 0:N], c == 0, c == 7,
                   [sbuf, Hsb[c]], [bank[1]])
        if sec < 2:
            ACT(FM[0][:, sec * 16:sec * 16 + 16], bank[0][:, 0:16], AF.Gelu, [bank[1]], [FM[1]])
        else:
            CP("dve", FM[0][:, sec * 16:sec * 16 + 16], bank[0][:, 0:16], [bank[1]], [FM[1]])
        psum.put(bank)
    WS0 = p32.get()
    for g in range(4):
        DMA("sp", dsof(WS0[1]), WS0[0][:, g:g + 1], w_s[g * 128:g * 128 + 1, 0:1].to_broadcast([128, 1]), (), [WS0[1]])
        DMA("sp", dsof(WS0[1]), WS0[0][:, 4 + g:5 + g], b_s[g:g + 1, 0:1].to_broadcast([128, 1]), (), [WS0[1]])
    for g in range(4):
        tmpg = p32.get()
        TS("dve", tmpg[0][:, 0:N], FM[0][:, 16 + g * 4:16 + g * 4 + N], WS0[0][:, g:g + 1], WS0[0][:, 4 + g:5 + g], ALU.mult, ALU.add,
           [FM[1], WS0[1]], [tmpg[1]])
        TT("dve", MIXs[:, g, 0:N], tmpg[0][:, 0:N], FM[0][:, g * 4:g * 4 + N], ALU.mult, [tmpg[1], FM[1]], [MIXsb[g]])
        p32.put(tmpg)
    p32.put(WS0)
    cst = p32.get()
    DMA("sp", dsof(cst[1]), cst[0][0:OWN * 2, 0:512], scst.rearrange("b j c -> (b j) c"), (), [cst[1]])
    bank = psum.get()
    for g in range(4):
        TR(bank[0][:, g * 8:g * 8 + OWN * 2], cst[0][0:OWN * 2, g * 128:(g + 1) * 128], [cst[1]], [bank[1]])
    CST = p32.get()
    CP("dve", CST[0][:, 0:32], bank[0][:, 0:32], [bank[1]], [CST[1]])
    psum.put(bank); p32.put(cst)
    for g in range(4):
        zf = p32.get(); cvv = p32.get()
        TT("dve", zf[0][:, 0:N], FM[0][:, 32 + g * 4:32 + g * 4 + N], FM[0][:, 64 + g * 4:64 + g * 4 + N], ALU.mult, [FM[1]], [zf[1]])
        st3 = CST[0][:, g * 8:g * 8 + 8].rearrange("p (b j) -> p b j", j=2)
        TS("dve", cvv[0][:, 0:N], st3[:, :, 0], CW[:, 0, g:g + 1], None, ALU.mult, None, [CST[1], cb], [cvv[1]])
        STT(cvv[0][:, 0:N], st3[:, :, 1], CW[:, 1, g:g + 1], cvv[0][:, 0:N], ALU.mult, ALU.add, [CST[1], cvv[1], cb], [cvv[1]])
        STT(cvv[0][:, 0:N], zf[0][:, 0:N], CW[:, 2, g:g + 1], cvv[0][:, 0:N], ALU.mult, ALU.add, [zf[1], cvv[1], cb], [cvv[1]])
        TT("dve", MIXs[:, 4 + g, 0:N], cvv[0][:, 0:N], FM[0][:, 48 + g * 4:48 + g * 4 + N], ALU.mult, [cvv[1], FM[1]], [MIXsb[4 + g]])
        p32.put(zf); p32.put(cvv)
    p32.put(CST); p32.put(FM)
    outproj("out1", MIXs, MIXsb, Xs, Xsb, N)
    ffn(1, Xs, Xsb, Hs, Hsb, N, GFs, GFsb, 1)
    rstd = rmsnorm(Xs, Xsb, None, None, None, N, out32=True)
    for hf in range(2):
        bank = psum.get()
        for c4 in range(4):
            c = hf * 4 + c4
            yn = p32.get()
            STT(yn[0][:, 0:N], Xs[:, c, 0:N], GFIN[:, c:c + 1], rstd[0][:, 0:N], ALU.mult, ALU.mult, [Xsb[c], rstd[1], cb], [yn[1]])
            TR(bank[0][0:N, c4 * 128:(c4 + 1) * 128], yn[0][:, 0:N], [yn[1]], [bank[1]])
            p32.put(yn)
        stg = p32.get()
        CP("dve", stg[0][0:N, 0:512], bank[0][0:N, :], [bank[1]], [stg[1]])
        psum.put(bank)
        DMA("pool", dsof(stg[1]), ys[:, hf * 512:(hf + 1) * 512], stg[0][0:N, 0:512], [stg[1]], ())
        p32.put(stg)
    p32.put(rstd)
    p.wait_all_dma("sp")
    p.emit()
    return nc


def make_in_maps(cfg, inputs):
    NCORES = cfg["NCORES"]; S = cfg["S"]; DB = cfg["DB"]; NPG = cfg["NPG"]; PPC = cfg["PPC"]
    OWN = DB // NCORES
    f = lambda a: np.ascontiguousarray(np.asarray(a))
    x_prompt = f(inputs["x_prompt"]); x_sample = f(inputs["x_sample"])[:, 0, :]
    cache_k = np.asarray(inputs["cache_k"])[0].reshape(-1, 128 * 512)
    cache_v = np.asarray(inputs["cache_v"])[0].reshape(-1, 128 * 512)
    state_pool = f(inputs["state_pool"])[0]; state_conv = f(inputs["state_conv"])[0]
    pt = f(inputs["page_table"]).astype(np.int32)
    common = dict(
        norm_mix=f(inputs["norm_mix"]), norm_ffn=f(inputs["norm_ffn"]), norm_final=f(inputs["norm_final"]).reshape(1, D),
        w_in0=f(inputs["ab_w_in"])[0], sb_bias=f(inputs["ab_sb_bias"]).reshape(1, 8), w_pool=f(inputs["ab_w_pool"])[0].reshape(512, 128),
        pool_scale=f(inputs["ab_pool_scale"]).reshape(1, 512), w_out0=f(inputs["ab_w_out"])[0], w_in1=f(inputs["cd_w_in"])[0],
        w_s=f(inputs["cd_w_s"])[0].reshape(512, 128), b_s=f(inputs["cd_b_s"])[0], conv_w=f(inputs["cd_conv_w"])[0],
        w_out1=f(inputs["cd_w_out"])[0])
    for l in range(2):
        common["w_gate%d" % l] = f(inputs["ffn_w_gate"])[l]
        common["w_up%d" % l] = f(inputs["ffn_w_up"])[l]
        common["w_down%d" % l] = f(inputs["ffn_w_down"])[l]
    maps = []
    for c in range(NCORES):
        order = np.roll(np.arange(DB), -c * OWN)
        m = dict(common)
        m["xp"] = x_prompt[c]
        m["xs"] = f(x_sample[order])
        NSPLIT = cfg.get("NSPLIT", 1); PPS = PPC // NSPLIT
        for i in range(NSPLIT):
            m["ck%d" % i] = f(cache_k[c * PPC + i * PPS:c * PPC + (i + 1) * PPS]).reshape(PPS * 128, 512)
            m["cv%d" % i] = f(cache_v[c * PPC + i * PPS:c * PPC + (i + 1) * PPS]).reshape(PPS * 128, 512)
        m["spst"] = f(state_pool[order[:OWN]]); m["scst"] = f(state_conv[order[:OWN]])
        m["ptT"] = f(pt[order].T)
        m["pbase"] = np.full((128, 1), float(c * PPC), np.float32)
        maps.append(m)
    return maps


def gather_outputs(cfg, res):
    NCORES = cfg["NCORES"]; S = cfg["S"]; DB = cfg["DB"]
    OWN = DB // NCORES
    R = res
    cat = lambda k: np.stack([r[k] for r in R])
    y_prompt = cat("yp")
    y_sample = np.concatenate([r["ys"] for r in R], 0).reshape(DB, 1, D)
    k_prompt = cat("kp").reshape(1, NCORES, S, 8, 64); v_prompt = cat("vp").reshape(1, NCORES, S, 8, 64)
    k_sample = np.concatenate([r["ksm"] for r in R], 0).reshape(1, DB, 1, 8, 64)
    v_sample = np.concatenate([r["vsm"] for r in R], 0).reshape(1, DB, 1, 8, 64)
    pool_prompt = cat("poolp")[None]; pool_sample = np.concatenate([r["pools"] for r in R], 0)[None]
    conv_prompt = cat("convp")[None]; conv_sample = np.concatenate([r["convs"] for r in R], 0)[None]
    chunk_v = np.concatenate([r["chv"] for r in R], 0).reshape(1, DB, 1, 512)
    return tuple(np.ascontiguousarray(a.astype(np.float32)) for a in
                 (y_prompt, y_sample, k_prompt, v_prompt, k_sample, v_sample, pool_prompt, pool_sample, conv_prompt,
                  conv_sample, chunk_v))


def kernel(**inputs):
    cfg = default_cfg()
    nc = build(cfg)
    maps = make_in_maps(cfg, inputs)
    res = run_bass_kernel_spmd(nc, maps, core_ids=list(range(cfg["NCORES"])))
    return gather_outputs(cfg, res.results)
```

```python
import numpy as np
from contextlib import ExitStack
import concourse.bass as bass
import concourse.mybir as mybir
from concourse.bass_utils import run_bass_kernel_spmd

F32 = mybir.dt.float32
BF16 = mybir.dt.bfloat16
I32 = mybir.dt.int32
AF = mybir.ActivationFunctionType
ALU = mybir.AluOpType
AX = mybir.AxisListType

CENG = ("pe", "act", "dve", "pool")
ALLENG = ("pe", "act", "dve", "pool", "sp")


class Buf:
    __slots__ = ("name", "w", "r")

    def __init__(self, name=""):
        self.name = name
        self.w = None
        self.r = {}


class DSem:
    __slots__ = ("sem", "cnt", "name", "unit")

    def __init__(self, sem, name, unit=16):
        self.sem = sem
        self.cnt = 0
        self.name = name
        self.unit = unit


class Op:
    __slots__ = ("eng", "fn", "waits", "need", "sig", "ds", "isdma")

    def __init__(self, eng, fn, isdma=False, ds=None):
        self.eng = eng
        self.fn = fn
        self.waits = []
        self.need = False
        self.sig = 0
        self.ds = ds
        self.isdma = isdma


class Prog:
    def __init__(self, nc):
        self.nc = nc
        self.stack = ExitStack()
        self.ops = {e: [] for e in ALLENG}
        self.esem = {}
        self.dsems = []
        self.nops = 0

    def sb(self, name, shape, dtype):
        return self.stack.enter_context(self.nc.sbuf_tensor(name, list(shape), dtype))

    def ps(self, name, shape, dtype=F32):
        return self.stack.enter_context(self.nc.psum_tensor(name, list(shape), dtype))

    def dsem(self, name, unit=16):
        s = self.stack.enter_context(self.nc.semaphore(name))
        d = DSem(s, name, unit)
        self.dsems.append(d)
        return d

    def _deps(self, op, reads, writes):
        def add(tok, kind):
            if tok is None:
                return
            if tok[0] == "c":
                prod = tok[1]
                if (not op.isdma) and prod.eng == op.eng and kind != "raw":
                    return
                prod.need = True
                op.waits.append(("c", prod))
            else:
                ds = tok[1]
                op.waits.append(("d", ds, ds.unit * ds.cnt))
        for b in reads:
            add(b.w, "raw")
        for b in writes:
            add(b.w, "waw")
            for t in b.r.values():
                add(t, "war")
        tok = ("d", op.ds) if op.isdma else ("c", op)
        for b in writes:
            b.w = tok
            b.r = {}
        rkey = id(op.ds) if op.isdma else op.eng
        for b in reads:
            b.r[rkey] = tok

    def op(self, eng, fn, reads=(), writes=()):
        o = Op(eng, fn)
        self._deps(o, reads, writes)
        self.ops[eng].append(o)
        self.nops += 1
        return o

    def dma(self, q, ds, fn, reads=(), writes=()):
        o = Op(q, fn, isdma=True, ds=ds)
        self._deps(o, reads, writes)
        ds.cnt += 1
        self.ops[q].append(o)
        self.nops += 1
        return o

    def wait_all_dma(self, q="sp"):
        o = Op(q, None)
        for d in self.dsems:
            if d.cnt:
                o.waits.append(("d", d, d.unit * d.cnt))
        self.ops[q].append(o)

    def emit(self):
        nc = self.nc
        for e in CENG:
            self.esem[e] = self.stack.enter_context(nc.semaphore("es_" + e))
        for e in ALLENG:
            n = 0
            for o in self.ops[e]:
                if o.need and not o.isdma:
                    n += 1
                    o.sig = n
        ops = self.ops
        esem = self.esem

        def stream(ename, eng):
            waited = {}
            for o in ops[ename]:
                for w in o.waits:
                    if w[0] == "c":
                        prod = w[1]
                        key = prod.eng
                        val = prod.sig
                        sem = esem[prod.eng]
                    else:
                        key = w[1]
                        val = w[2]
                        sem = w[1].sem
                    if waited.get(key, 0) >= val:
                        continue
                    waited[key] = val
                    eng.wait_ge(sem, val)
                if o.fn is None:
                    continue
                ins = o.fn(eng)
                if o.isdma:
                    ins.then_inc(o.ds.sem, o.ds.unit)
                elif o.need:
                    ins.then_inc(esem[ename], 1)

        with nc.Block() as block:
            @block.sync
            def _(e):
                stream("sp", e)

            @block.scalar
            def _(e):
                stream("act", e)

            @block.tensor
            def _(e):
                stream("pe", e)

            @block.vector
            def _(e):
                stream("dve", e)

            @block.gpsimd
            def _(e):
                stream("pool", e)
        self.stack.close()


class TPool:
    def __init__(self, items):
        self.free = list(items)

    def get(self):
        assert self.free, "tile pool exhausted"
        return self.free.pop(0)

    def put(self, it):
        self.free.append(it)


D = 1024
DFF = 2816
KFF = DFF // 128
EPS = 1e-6


def default_cfg():
    return dict(NCORES=8, S=4096, DB=32, NPG=128, PPC=640, NSPLIT=2)


def build(cfg):
    NCORES = cfg["NCORES"]; S = cfg["S"]; DB = cfg["DB"]; NPG = cfg["NPG"]; PPC = cfg["PPC"]
    NT = S // 512
    OWN = DB // NCORES
    assert OWN == 4 and DB <= 32
    G = 8
    assert PPC % G == 0
    NGRP = PPC // G

    nc = bass.Bass("TRN2", target_bir_lowering=False)

    def din(name, shape, dt=F32):
        return nc.dram_tensor(name, list(shape), dt, kind="ExternalInput").ap()

    def dout(name, shape, dt=F32):
        return nc.dram_tensor(name, list(shape), dt, kind="ExternalOutput").ap()

    def dint(name, shape, dt=F32):
        return nc.dram_tensor(name, list(shape), dt, kind="Internal").ap()

    xp = din("xp", [S, D]); xs = din("xs", [DB, D])
    NSPLIT = cfg.get("NSPLIT", 1)
    PPS = PPC // NSPLIT
    assert PPC % NSPLIT == 0 and PPS % G == 0
    cks = [din("ck%d" % i, [PPS * 128, 512]) for i in range(NSPLIT)]
    cvs = [din("cv%d" % i, [PPS * 128, 512]) for i in range(NSPLIT)]
    spst = din("spst", [OWN, 15, 512]); scst = din("scst", [OWN, 2, 512])
    ptT = din("ptT", [NPG, DB], I32); pbase = din("pbase", [128, 1])
    norm_mix = din("norm_mix", [2, D]); norm_ffn = din("norm_ffn", [2, D]); norm_final = din("norm_final", [1, D])
    w_in0 = din("w_in0", [D, 2048]); sb_bias = din("sb_bias", [1, 8]); w_pool = din("w_pool", [512, 128])
    pool_scale = din("pool_scale", [1, 512]); w_out0 = din("w_out0", [D, D])
    w_in1 = din("w_in1", [D, 2560]); w_s = din("w_s", [512, 128]); b_s = din("b_s", [4, 128])
    conv_w = din("conv_w", [3, 512]); w_out1 = din("w_out1", [D, D])
    w_gate = [din("w_gate%d" % l, [D, DFF]) for l in range(2)]
    w_up = [din("w_up%d" % l, [D, DFF]) for l in range(2)]
    w_down = [din("w_down%d" % l, [DFF, D]) for l in range(2)]

    yp = dout("yp", [S, D]); ys = dout("ys", [OWN, D])
    kp = dout("kp", [S, 512]); vp = dout("vp", [S, 512])
    ksm = dout("ksm", [OWN, 512]); vsm = dout("vsm", [OWN, 512])
    poolp = dout("poolp", [15, 512]); pools = dout("pools", [OWN, 15, 512])
    convp = dout("convp", [2, 512]); convs = dout("convs", [OWN, 2, 512])
    chv = dout("chv", [OWN, 512])
    DBG = cfg.get("dbg", False)
    if DBG:
        dbg_wmt = dout("dbg_wmt", [128, 512], BF16)
        dbg_mix = dout("dbg_mix", [128, 4096], BF16)
        dbg_mix0 = dout("dbg_mix0", [128, 4096], BF16)
        dbg_t32 = dout("dbg_t32", [128, 512], F32)
        dbg_t32b = dout("dbg_t32b", [128, 512], F32)
        dbg_wmt0 = dout("dbg_wmt0", [128, 512], BF16)

    WS = {}
    wsrc = {"in0": (w_in0, 8, 2048), "out0": (w_out0, 8, D), "in1": (w_in1, 8, 2560), "out1": (w_out1, 8, D)}
    for l in range(2):
        wsrc["gate%d" % l] = (w_gate[l], 8, DFF)
        wsrc["up%d" % l] = (w_up[l], 8, DFF)
        wsrc["down%d" % l] = (w_down[l], KFF, D)
    for k, (src, kc, m) in wsrc.items():
        WS[k] = dint("wbf_" + k, [128, kc, m], BF16)
    rec_o_loc = dint("rec_o_loc", [PPC * 8, 64]); rec_t_loc = dint("rec_t_loc", [PPC, 8])
    CPG = cfg.get("CPG", min(128, PPC))
    assert PPC % CPG == 0
    NCH = PPC // CPG
    rec_o_all = dint("rec_o_all", [NCH * NCORES * CPG * 8, 64]); rec_t_all = dint("rec_t_all", [NCORES * PPC, 8])

    p = Prog(nc)
    cfg["_sbuf0"] = nc.sbuf_bytes_remaining
    psum = TPool([(p.ps("bank%d" % i, [128, 512]), Buf("bank%d" % i)) for i in range(8)])
    P32W = 544
    NP32 = cfg.get("NP32", 12)
    NP16 = cfg.get("NP16", 20)
    p32 = TPool([(p.sb("p32_%d" % i, [128, P32W], F32), Buf("p32_%d" % i)) for i in range(NP32)])
    p16 = TPool([(p.sb("p16_%d" % i, [128, 512], BF16), Buf("p16_%d" % i)) for i in range(NP16)])
    X = p.sb("X", [128, 8, 512], F32); Xb = [Buf("X%d" % c) for c in range(8)]
    H = p.sb("H", [128, 8, 512], BF16); Hb = [Buf("H%d" % c) for c in range(8)]
    QT = p.sb("QT", [128, 4, 512], BF16); QTb = [Buf() for _ in range(4)]
    MIX = p.sb("MIX", [128, 8, 512], BF16); MIXb = [Buf() for _ in range(8)]
    GF = None; GFb = None
    ARENA = p.sb("ARENA", [128, 16384], F32)
    AR16 = ARENA[:].bitcast(BF16)
    KT = AR16[:, 0:16384].rearrange("p (j n) -> p j n", j=4)
    VB = AR16[:, 16384:32768].rearrange("p (b f) -> p b f", f=512)
    KTb = [[Buf() for _ in range(NT)] for _ in range(4)]
    VBb = [Buf() for _ in range(32)]
    phase = Buf("phase")
    NSLOT = 3
    WR = [(p.sb("wr%d" % i, [128, 4096], BF16), Buf("wr%d" % i)) for i in range(NSLOT)]
    wr_i = [0]
    Xs = p.sb("Xs", [128, 8, 32], F32); Xsb = [Buf() for _ in range(8)]
    Hs = p.sb("Hs", [128, 8, 32], BF16); Hsb = [Buf() for _ in range(8)]
    MIXs = p.sb("MIXs", [128, 8, 4], BF16); MIXsb = [Buf() for _ in range(8)]
    GFs = p.sb("GFs", [128, 22, 4], BF16); GFsb = [Buf() for _ in range(22)]
    TOKS = p.sb("TOKS", [32, 2560], F32); TOKSb = Buf()
    QSb16 = p.sb("QSb16", [DB, 512], BF16); QSbb = Buf()
    ident = p.sb("ident", [128, 128], F32); identb = p.sb("identb", [128, 128], BF16)
    trineg = p.sb("trineg", [128, 128], BF16); onesneg = p.sb("onesneg", [128, 128], BF16)
    onesmean = p.sb("onesmean", [128, 128], BF16)
    tripos = p.sb("tripos", [128, 128], F32)
    trimask = p.sb("trimask", [128, 128], F32)
    ones32 = p.sb("ones32", [128, 128], F32)
    negm = p.sb("negm", [128, 4, 512], BF16)
    cb = Buf("const")
    GMIX = p.sb("GMIX", [128, 2, 8], F32); GFFN = p.sb("GFFN", [128, 2, 8], F32); GFIN = p.sb("GFIN", [128, 8], F32)
    BIAS = p.sb("BIAS", [128, 8], F32); PSC = p.sb("PSC", [128, 4], F32); CW = p.sb("CW", [128, 3, 4], F32)
    WPb = p.sb("WPb", [128, 4, 128], BF16); WMT = p.sb("WMT", [128, 4, 128], BF16)
    BSH = p.sb("BSH", [2, 4, 128], BF16); BSL = p.sb("BSL", [2, 4, 128], BF16)
    ones2 = p.sb("ones2", [2, 128], BF16)
    RC0 = p.sb("RC0", [128, 16], F32)
    HALO = p.sb("HALO", [128, 4, 16], F32); HALOb = Buf()
    ZHALO = p.sb("ZHALO", [128, 4, 2], F32); ZHALOb = Buf()
    PTT = p.sb("PTT", [128, DB], I32); PTF = p.sb("PTF", [128, DB], F32); PBASE = p.sb("PBASE", [128, 1], F32)
    OH = p.sb("OH", [DB, PPC], BF16); OHb = Buf()
    IOTAP = p.sb("IOTAP", [128, PPC], F32)
    ESEL = p.sb("ESEL", [128, DB, DB], BF16)
    BIASG = p.sb("BIASG", [128, G, 8], F32)
    BD128 = p.sb("BD128", [128, 512], F32); SEL4 = p.sb("SEL4", [128, 4], F32)

    d_const = p.dsem("d_const"); d_misc = p.dsem("d_misc"); d_toks = p.dsem("d_toks")
    _bds = {}

    def dsof(buf):
        if id(buf) not in _bds:
            _bds[id(buf)] = p.dsem("db%d" % len(_bds))
        return _bds[id(buf)]
    d_wr = [p.dsem("d_wr%d" % i) for i in range(NSLOT)]
    d_cc = p.dsem("d_cc", unit=1)
    d_pc = p.dsem("d_pc")

    def MM(out, lhsT, rhs, start, stop, reads, writes):
        return p.op("pe", lambda e: e.matmul(out, lhsT=lhsT, rhs=rhs, start=start, stop=stop), reads, writes)

    def TR(out, in_, reads, writes):
        k = in_.shape[0]
        return p.op("pe", lambda e: e.transpose(out, in_, ident[:k, :k]), list(reads) + [cb], writes)

    def ACT(out, in_, func, reads, writes, bias=None, scale=None):
        kw = {}
        if bias is not None:
            kw["bias"] = bias
        if scale is not None:
            kw["scale"] = scale
        return p.op("act", lambda e: e.activation(out=out, in_=in_, func=func, **kw), reads, writes)

    def TT(eng, out, in0, in1, op, reads, writes):
        return p.op(eng, lambda e: e.tensor_tensor(out=out, in0=in0, in1=in1, op=op), reads, writes)

    def TS(eng, out, in0, s1, s2, op0, op1, reads, writes):
        if s2 is None:
            return p.op(eng, lambda e: e.tensor_scalar(out=out, in0=in0, scalar1=s1, scalar2=None, op0=op0), reads, writes)
        return p.op(eng, lambda e: e.tensor_scalar(out=out, in0=in0, scalar1=s1, scalar2=s2, op0=op0, op1=op1), reads, writes)

    def STT(out, in0, scalar, in1, op0, op1, reads, writes):
        return p.op("dve", lambda e: e.scalar_tensor_tensor(out=out, in0=in0, scalar=scalar, in1=in1, op0=op0, op1=op1),
                    reads, writes)

    def CP(eng, out, in_, reads, writes):
        if eng == "act":
            return ACT(out, in_, AF.Copy, reads, writes)
        return p.op(eng, lambda e: e.tensor_copy(out=out, in_=in_), reads, writes)

    def MS(eng, ap, val, writes):
        return p.op(eng, lambda e: e.memset(ap, val), (), writes)

    def DMA(q, ds, out, in_, reads, writes, slow=False):
        if slow:
            return p.dma(q, ds, lambda e: e.dma_start(out=out, in_=in_, allow_slow_non_contiguous=True), reads, writes)
        return p.dma(q, ds, lambda e: e.dma_start(out=out, in_=in_), reads, writes)

    ev_i = [0]

    def evac_eng():
        ev_i[0] += 1
        return "act" if ev_i[0] % 2 else "dve"

    UPTO = cfg.get("upto", 99)
    print("sbuf bytes remaining", cfg["_sbuf0"], "->", nc.sbuf_bytes_remaining)

    class _Done(Exception):
        pass

    def fin():
        p.wait_all_dma("sp")
        p.emit()
        return nc

    MS("pool", ident[:], 1.0, [cb])
    p.op("pool", lambda e: e.affine_select(out=ident[:], in_=ident[:], pattern=[[-1, 128]], compare_op=ALU.is_equal,
                                           fill=0.0, base=0, channel_multiplier=1), [cb], [cb])
    CP("pool", identb[:], ident[:], [cb], [cb])
    MS("pool", onesneg[:], -1.0, [cb])
    MS("pool", ones32[:], 1.0, [cb])
    MS("pool", onesmean[:], 1.0 / 1024.0, [cb])
    MS("pool", ones2[:], 1.0, [cb])
    p.op("pool", lambda e: e.affine_select(out=trineg[:], in_=onesneg[:], pattern=[[-1, 128]], compare_op=ALU.is_ge,
                                           fill=0.0, base=0, channel_multiplier=1), [cb], [cb])
    p.op("pool", lambda e: e.affine_select(out=tripos[:], in_=ones32[:], pattern=[[-1, 128]], compare_op=ALU.is_gt,
                                           fill=0.0, base=0, channel_multiplier=1), [cb], [cb])
    p.op("pool", lambda e: e.affine_select(out=trimask[:], in_=ones32[:], pattern=[[1, 128]], compare_op=ALU.is_ge,
                                           fill=0.0, base=0, channel_multiplier=-1), [cb], [cb])
    big = p32.get()
    MS("pool", big[0][:, 0:512], -30000.0, [big[1]])
    for r in range(4):
        p.op("pool", lambda e, r=r: e.affine_select(out=negm[:, r, :], in_=big[0][:, 0:512], pattern=[[-1, 512]],
                                                    compare_op=ALU.is_ge, fill=0.0, base=r * 128, channel_multiplier=1),
             [big[1]], [cb])
    p32.put(big)
    p.op("pool", lambda e: e.iota(RC0[:], pattern=[[1, 16]], base=1, channel_multiplier=0,
                                  allow_small_or_imprecise_dtypes=True), (), [cb])
    p.op("dve", lambda e: e.reciprocal(out=RC0[:], in_=RC0[:]), [cb], [cb])
    p.op("pool", lambda e: e.iota(IOTAP[:], pattern=[[1, PPC]], base=0, channel_multiplier=0,
                                  allow_small_or_imprecise_dtypes=True), (), [cb])
    MS("pool", ESEL[:], 1.0, [cb])
    p.op("pool", lambda e: e.affine_select(out=ESEL[:], in_=ESEL[:], pattern=[[1, DB], [-1, DB]], compare_op=ALU.is_equal,
                                           fill=0.0, base=0, channel_multiplier=0), [cb], [cb])
    MS("pool", BD128[:], 1.0, [cb])
    MS("pool", SEL4[:], 0.0, [cb])
    for q in range(4):
        p.op("pool", lambda e, q=q: e.affine_select(out=BD128[q * 32:(q + 1) * 32, :].rearrange("p (h d) -> p h d", h=8),
                                                    in_=BD128[q * 32:(q + 1) * 32, :].rearrange("p (h d) -> p h d", h=8),
                                                    pattern=[[-1, 8], [0, 64]], compare_op=ALU.is_equal, fill=0.0, base=0,
                                                    channel_multiplier=1), [cb], [cb])
        MS("pool", SEL4[q * 32:(q + 1) * 32, q:q + 1], 1.0, [cb])
    MS("pool", HALO[:], 0.0, [HALOb])
    MS("pool", ZHALO[:], 0.0, [ZHALOb])
    for l in range(2):
        DMA("sp", d_const, GMIX[:, l, :], norm_mix[l].rearrange("(c q) -> q c", q=128), (), [cb], slow=True)
        DMA("sp", d_const, GFFN[:, l, :], norm_ffn[l].rearrange("(c q) -> q c", q=128), (), [cb], slow=True)
    DMA("sp", d_const, GFIN[:], norm_final[0].rearrange("(c q) -> q c", q=128), (), [cb], slow=True)
    DMA("sp", d_const, BIAS[:], sb_bias.to_broadcast([128, 8]), (), [cb])
    DMA("sp", d_const, PSC[:], pool_scale[0].rearrange("(c q) -> q c", q=128), (), [cb], slow=True)
    for j in range(3):
        DMA("sp", d_const, CW[:, j, :], conv_w[j].rearrange("(c q) -> q c", q=128), (), [cb], slow=True)
    DMA("sp", d_const, PTT[:NPG, :], ptT, (), [cb])
    DMA("sp", d_const, PBASE[:], pbase, (), [cb])
    t32 = p32.get()
    DMA("sp", dsof(t32[1]), t32[0][:, 0:512].rearrange("p (g d) -> p g d", g=4), w_pool.rearrange("(g c) d -> c g d", g=4), (), [t32[1]])
    CP("dve", WPb[:], t32[0][:, 0:512].rearrange("p (g d) -> p g d", g=4), [t32[1]], [cb])
    p32.put(t32)
    t32 = p32.get()
    DMA("sp", dsof(t32[1]), t32[0][:, 0:512].rearrange("p (g d) -> p g d", g=4), w_s.rearrange("(g t) s -> t g s", g=4), (), [t32[1]])
    tb_ = psum.get()
    for g in range(4):
        TR(tb_[0][:, g * 128:(g + 1) * 128], t32[0][:, g * 128:(g + 1) * 128], [t32[1]], [tb_[1]])
    t32b = p32.get()
    CP("dve", t32b[0][:, 0:512], tb_[0][:, :], [tb_[1]], [t32b[1]])
    psum.put(tb_)
    for g in range(4):
        TT("dve", WMT[:, g, :], t32b[0][:, g * 128:(g + 1) * 128], trimask[:], ALU.mult, [t32b[1], cb], [cb])
    if DBG:
        DMA("sp", d_misc, dbg_wmt0, WMT[:].rearrange("p g t -> p (g t)"), [cb], ())
        DMA("sp", d_misc, dbg_t32, t32[0][:, 0:512], [t32[1]], ())
        DMA("sp", d_misc, dbg_t32b, t32b[0][:, 0:512], [t32b[1]], ())
    p32.put(t32); p32.put(t32b)
    t32 = p32.get(); t32b = p32.get()
    DMA("sp", dsof(t32[1]), t32[0][0:1, 0:512], b_s.rearrange("(o g) t -> o (g t)", o=1), (), [t32[1]])
    CP("dve", BSH[0:1, :, :].rearrange("p g t -> p (g t)"), t32[0][0:1, 0:512], [t32[1]], [cb])
    CP("dve", t32b[0][0:1, 0:512], BSH[0:1, :, :].rearrange("p g t -> p (g t)"), [cb], [t32b[1]])
    TT("dve", t32b[0][0:1, 0:512], t32[0][0:1, 0:512], t32b[0][0:1, 0:512], ALU.subtract, [t32[1], t32b[1]], [t32b[1]])
    CP("dve", BSL[0:1, :, :].rearrange("p g t -> p (g t)"), t32b[0][0:1, 0:512], [t32b[1]], [cb])
    p32.put(t32); p32.put(t32b)
    for g in range(G):
        CP("pool", BIASG[:, g, :], BIAS[:], [cb], [cb])

    wsb = {k: [Buf("ws_%s_%d" % (k, c)) for c in range(wsrc[k][1])] for k in WS}
    d_pck = {k: p.dsem("d_pc_" + k) for k in WS}
    for k, (src, kc, m) in wsrc.items():
        for c in range(kc):
            DMA("pool", d_pck[k], WS[k][:, c, :], src[c * 128:(c + 1) * 128, :], (), [wsb[k][c]])

    def wload(k, c0, mc):
        kc = wsrc[k][1]
        i = wr_i[0] % NSLOT
        wr_i[0] += 1
        slot, sbuf = WR[i]
        view = slot[:, 0:kc * mc].rearrange("p (k m) -> p k m", k=kc)
        DMA("sp", d_wr[i], view, WS[k][:, :, c0:c0 + mc], wsb[k], [sbuf])
        return view, sbuf

    def dense_fm(k, c0, nm, rhs, rhs_bufs, N, epi, mc=512):
        kc = wsrc[k][1]
        mper = mc // 128
        m = 0
        while m < nm:
            nmm = min(mper, nm - m)
            view, sbuf = wload(k, c0 + m * 128, nmm * 128)
            for mi in range(nmm):
                bank = psum.get()
                for c in range(kc):
                    MM(bank[0][:, 0:N], view[:, c, mi * 128:(mi + 1) * 128], rhs(c), c == 0, c == kc - 1,
                       [sbuf, rhs_bufs[c]], [bank[1]])
                epi(m + mi, bank)
                psum.put(bank)
            m += nmm

    def rmsnorm(Xt, Xbufs, gcol, Ht, Hbufs, N, out32=None):
        ms = psum.get()
        for c in range(8):
            sq = p16.get()
            ACT(sq[0][:, 0:N], Xt[:, c, 0:N], AF.Square, [Xbufs[c]], [sq[1]])
            MM(ms[0][:, 0:N], onesmean[:], sq[0][:, 0:N], c == 0, c == 7, [sq[1], cb], [ms[1]])
            p16.put(sq)
        rstd = p32.get()
        ACT(rstd[0][:, 0:N], ms[0][:, 0:N], AF.Ln, [ms[1]], [rstd[1]], bias=EPS)
        psum.put(ms)
        ACT(rstd[0][:, 0:N], rstd[0][:, 0:N], AF.Exp, [rstd[1]], [rstd[1]], scale=-0.5)
        if out32 is not None:
            return rstd
        for c in range(8):
            STT(Ht[:, c, 0:N], Xt[:, c, 0:N], gcol(c), rstd[0][:, 0:N], ALU.mult, ALU.mult,
                [Xbufs[c], rstd[1], cb], [Hbufs[c]])
        p32.put(rstd)
        return None

    def ffn(l, Xt, Xbufs, Ht, Hbufs, N, GFt, GFbufs, nhalf):
        rmsnorm(Xt, Xbufs, lambda c: GFFN[:, l, c:c + 1], Ht, Hbufs, N)
        per = KFF // nhalf
        for hf in range(nhalf):
            m0 = hf * per
            mm = 0
            gfl = None
            if GFt is None:
                gfl = [p16.get() for _ in range(per)]
            while mm < per:
                nmm = min(4, per - mm)
                gview, gsb = wload("gate%d" % l, (m0 + mm) * 128, nmm * 128)
                uview, usb = wload("up%d" % l, (m0 + mm) * 128, nmm * 128)
                for mi in range(nmm):
                    bg_ = psum.get(); bu_ = psum.get()
                    for c in range(8):
                        MM(bg_[0][:, 0:N], gview[:, c, mi * 128:(mi + 1) * 128], Ht[:, c, 0:N], c == 0, c == 7,
                           [gsb, Hbufs[c]], [bg_[1]])
                    for c in range(8):
                        MM(bu_[0][:, 0:N], uview[:, c, mi * 128:(mi + 1) * 128], Ht[:, c, 0:N], c == 0, c == 7,
                           [usb, Hbufs[c]], [bu_[1]])
                    st = p32.get()
                    ACT(st[0][:, 0:N], bg_[0][:, 0:N], AF.Silu, [bg_[1]], [st[1]])
                    psum.put(bg_)
                    if gfl is not None:
                        TT("dve", gfl[mm + mi][0][:, 0:N], st[0][:, 0:N], bu_[0][:, 0:N], ALU.mult, [st[1], bu_[1]],
                           [gfl[mm + mi][1]])
                    else:
                        TT("dve", GFt[:, mm + mi, 0:N], st[0][:, 0:N], bu_[0][:, 0:N], ALU.mult, [st[1], bu_[1]],
                           [GFbufs[mm + mi]])
                    psum.put(bu_)
                    p32.put(st)
                mm += nmm
            dstep = 2 if per <= 16 else 1
            for mo in range(0, 8, dstep):
                i = wr_i[0] % NSLOT
                wr_i[0] += 1
                slot, sbuf = WR[i]
                view = slot[:, 0:per * 128 * dstep].rearrange("p (k m) -> p k m", k=per)
                DMA("sp", d_wr[i], view, WS["down%d" % l][:, m0:m0 + per, mo * 128:(mo + dstep) * 128], wsb["down%d" % l], [sbuf])
                for mi in range(dstep):
                    bank = psum.get()
                    for c in range(per):
                        if gfl is not None:
                            MM(bank[0][:, 0:N], view[:, c, mi * 128:(mi + 1) * 128], gfl[c][0][:, 0:N], c == 0, c == per - 1,
                               [sbuf, gfl[c][1]], [bank[1]])
                        else:
                            MM(bank[0][:, 0:N], view[:, c, mi * 128:(mi + 1) * 128], GFt[:, c, 0:N], c == 0, c == per - 1,
                               [sbuf, GFbufs[c]], [bank[1]])
                    TT("dve", Xt[:, mo + mi, 0:N], bank[0][:, 0:N], Xt[:, mo + mi, 0:N], ALU.add,
                       [bank[1], Xbufs[mo + mi]], [Xbufs[mo + mi]])
                    psum.put(bank)
            if gfl is not None:
                for it in gfl:
                    p16.put(it)

    def outproj(k, MIXt, MIXbufs, Xt, Xbufs, N):
        def epi(m, bank):
            TT("dve", Xt[:, m, 0:N], bank[0][:, 0:N], Xt[:, m, 0:N], ALU.add, [bank[1], Xbufs[m]], [Xbufs[m]])
        dense_fm(k, 0, 8, lambda c: MIXt[:, c, 0:N], MIXbufs, N, epi)

    if UPTO == 1:
        return fin()
    for hf in range(2):
        xt = p32.get()
        DMA("sp", dsof(xt[1]), xt[0][0:DB, 0:512], xs[:, hf * 512:(hf + 1) * 512], (), [xt[1]])
        bank = psum.get()
        for c4 in range(4):
            TR(bank[0][:, c4 * 32:c4 * 32 + DB], xt[0][0:DB, c4 * 128:(c4 + 1) * 128], [xt[1]], [bank[1]])
        p32.put(xt)
        for c4 in range(4):
            CP("dve", Xs[:, hf * 4 + c4, 0:DB], bank[0][:, c4 * 32:c4 * 32 + DB], [bank[1]], [Xsb[hf * 4 + c4]])
        psum.put(bank)
    rmsnorm(Xs, Xsb, lambda c: GMIX[:, 0, c:c + 1], Hs, Hsb, DB)
    for pc in range(4):
        view, sbuf = wload("in0", pc * 512, 512)
        bank = psum.get()
        for c in range(8):
            MM(bank[0][0:DB, :], Hs[:, c, 0:DB], view[:, c, :], c == 0, c == 7, [sbuf, Hsb[c]], [bank[1]])
        CP("dve", TOKS[0:DB, pc * 512:(pc + 1) * 512], bank[0][0:DB, :], [bank[1]], [TOKSb])
        psum.put(bank)
    DMA("pool", d_toks, ksm, TOKS[0:OWN, 1024:1536], [TOKSb], ())
    DMA("pool", d_toks, vsm, TOKS[0:OWN, 1536:2048], [TOKSb], ())
    DMA("pool", d_toks, pools[:, 14, :], TOKS[0:OWN, 0:512], [TOKSb], ())
    DMA("pool", d_misc, pools[:, 0:14, :], spst[:, 1:15, :], (), ())
    TS("dve", QSb16[0:DB, :], TOKS[0:DB, 512:1024], 0.125, None, ALU.mult, None, [TOKSb], [QSbb])
    stt_ = p32.get()
    DMA("sp", dsof(stt_[1]), stt_[0][0:OWN * 15, 0:512], spst.rearrange("b j c -> (b j) c"), (), [stt_[1]])
    ST = p32.get()
    bank = psum.get()
    for g in range(4):
        TR(bank[0][:, g * 64:g * 64 + OWN * 15], stt_[0][0:OWN * 15, g * 128:(g + 1) * 128], [stt_[1]], [bank[1]])
    p32.put(stt_)
    CP("dve", ST[0][:, 0:256], bank[0][:, 0:256], [bank[1]], [ST[1]])
    psum.put(bank)
    US = p32.get()
    view, sbuf = wload("in0", 0, 512)
    bank = psum.get()
    for g in range(4):
        for c in range(8):
            MM(bank[0][:, g * 4:g * 4 + OWN], view[:, c, g * 128:(g + 1) * 128], Hs[:, c, 0:OWN], c == 0, c == 7,
               [sbuf, Hsb[c]], [bank[1]])
    CP("dve", US[0][:, 0:16], bank[0][:, 0:16], [bank[1]], [US[1]])
    psum.put(bank)
    DS = p16.get()
    for g in range(4):
        w = 2 << g
        ssum = p32.get()
        p.op("dve", lambda e, g=g, w=w, ssum=ssum: e.tensor_reduce(
            out=ssum[0][:, 0:OWN], in_=ST[0][:, g * 64:g * 64 + OWN * 15].rearrange("p (b j) -> p b j", j=15)[:, :, 16 - w:15],
            axis=AX.X, op=ALU.add), [ST[1]], [ssum[1]])
        TT("dve", ssum[0][:, 0:OWN], ssum[0][:, 0:OWN], US[0][:, g * 4:g * 4 + OWN], ALU.add, [ssum[1], US[1]], [ssum[1]])
        STT(DS[0][:, g * 4:g * 4 + OWN], ssum[0][:, 0:OWN], 1.0 / w, US[0][:, g * 4:g * 4 + OWN], ALU.mult, ALU.subtract,
            [ssum[1], US[1]], [DS[1]])
        p32.put(ssum)
    bank = psum.get()
    for g in range(4):
        MM(bank[0][:, g * 4:g * 4 + OWN], WPb[:, g, :], DS[0][:, g * 4:g * 4 + OWN], True, True, [DS[1], cb], [bank[1]])
    for g in range(4):
        TS("dve", MIXs[:, g, 0:OWN], bank[0][:, g * 4:g * 4 + OWN], PSC[:, g:g + 1], None, ALU.mult, None,
           [bank[1], cb], [MIXsb[g]])
    psum.put(bank)
    p16.put(DS); p32.put(ST); p32.put(US)

    if UPTO == 2:
        return fin()
    CP("dve", PTF[:NPG, :], PTT[:NPG, :], [cb], [cb])
    TS("dve", PTF[:NPG, :], PTF[:NPG, :], PBASE[:NPG, 0:1], None, ALU.subtract, None, [cb], [cb])
    CH = 128
    for c0 in range(0, PPC, CH):
        cw = min(CH, PPC - c0)
        cmpb = Buf()
        CMP = AR16[:, 0:DB * cw].rearrange("p (b q) -> p b q", b=DB)
        p.op("dve", lambda e, CMP=CMP, c0=c0, cw=cw: e.tensor_tensor(
            out=CMP[:NPG], in0=IOTAP[:NPG, c0:c0 + cw].unsqueeze(1).to_broadcast([NPG, DB, cw]),
            in1=PTF[:NPG, :].unsqueeze(2).to_broadcast([NPG, DB, cw]), op=ALU.is_equal), [cb, phase], [cmpb])
        bank = psum.get()
        for b in range(DB):
            MM(bank[0][0:DB, 0:cw], ESEL[:NPG, b, :], CMP[:NPG, b, :], b == 0, b == DB - 1, [cmpb, cb, phase], [bank[1]])
        CP("dve", OH[:, c0:c0 + cw], bank[0][0:DB, 0:cw], [bank[1]], [OHb])
        psum.put(bank)
        p.op("pool", lambda e: e.memset(AR16[0:1, 0:2], 0.0), [cmpb, phase], [cmpb])

    p.op("pool", lambda e: e.memset(AR16[0:1, 0:2], 0.0), [], [phase])
    NKS = 3
    VBOFF = NKS * G * 512
    kslots = [(ARENA[:, s * G * 512:(s + 1) * G * 512].rearrange("p (g f) -> p g f", g=G), Buf()) for s in range(NKS)]
    vbslots = [(AR16[:, 2 * VBOFF + s * G * 512:2 * VBOFF + (s + 1) * G * 512].rearrange("p (g f) -> p g f", g=G), Buf()) for s in range(2)]
    OHR = [p.sb("OHR%d" % s, [DB, G * 128], BF16) for s in range(2)]
    ohrslots = [(OHR[s][:, :].rearrange("p (g t) -> p g t", g=G), Buf()) for s in range(2)]
    def sampA(gi):
        s = gi % 2
        kt, kb_ = kslots[gi % NKS]; vbt, vbb_ = vbslots[s]; oht, ohb_ = ohrslots[s]
        pg0 = gi * G
        ck = cks[pg0 // PPS]; cv = cvs[pg0 // PPS]; pl0 = pg0 % PPS
        DMA("sp", dsof(kb_), kt, ck[pl0 * 128:(pl0 + G) * 128, :].rearrange("(g t) f -> t g f", g=G), [phase], [kb_])
        DMA("pool", dsof(vbb_), vbt, cv[pl0 * 128:(pl0 + G) * 128, :].rearrange("(g t) f -> t g f", g=G), [phase], [vbb_])
        CP("pool", oht, OH[:, pg0:pg0 + G].unsqueeze(2).to_broadcast([DB, G, 128]), [OHb, phase], [ohb_])
        zt = p32.get()
        for g in range(G):
            qb = psum.get()
            MM(qb[0][:, :], oht[:, g, :], QSb16[0:DB, :], True, True, [ohb_, QSbb, phase], [qb[1]])
            pr = p32.get()
            TT("dve", pr[0][:, 0:512], kt[:, g, :], qb[0][:, :], ALU.mult, [kb_, qb[1], phase], [pr[1]])
            psum.put(qb)
            p.op("dve", lambda e, pr=pr, zt=zt, g=g: e.tensor_reduce(
                out=zt[0][:, g * 8:(g + 1) * 8], in_=pr[0][:, 0:512].rearrange("p (h d) -> p h d", h=8), axis=AX.X,
                op=ALU.add), [pr[1]], [zt[1]])
            p32.put(pr)
        return (zt, vbt, vbb_, pg0)

    def sampB(ctx):
        zt, vbt, vbb_, pg0 = ctx
        NC8 = G * 8
        TT("dve", zt[0][:, 0:NC8], zt[0][:, 0:NC8], BIASG[:].rearrange("p g h -> p (g h)"), ALU.add, [zt[1], cb], [zt[1]])
        et = p32.get()
        ACT(et[0][:, 0:NC8], zt[0][:, 0:NC8], AF.Exp, [zt[1]], [et[1]])
        spb = p16.get()
        ACT(spb[0][:, 0:NC8], et[0][:, 0:NC8], AF.Ln, [et[1]], [spb[1]], bias=1.0)
        p32.put(et)
        lat = psum.get()
        MM(lat[0][:, 0:NC8], trineg[:], spb[0][:, 0:NC8], True, True, [spb[1], cb], [lat[1]])
        tot = psum.get()
        MM(tot[0][0:1, 0:NC8], onesneg[:, 0:1], spb[0][:, 0:NC8], True, True, [spb[1], cb], [tot[1]])
        p16.put(spb)
        TT("dve", zt[0][:, 0:NC8], lat[0][:, 0:NC8], zt[0][:, 0:NC8], ALU.add, [lat[1], zt[1]], [zt[1]])
        psum.put(lat)
        wt = p16.get()
        ACT(wt[0][:, 0:NC8], zt[0][:, 0:NC8], AF.Exp, [zt[1]], [wt[1]])
        p32.put(zt)
        tots = p32.get()
        CP("act", tots[0][0:1, 0:NC8], tot[0][0:1, 0:NC8], [tot[1]], [tots[1]])
        psum.put(tot)
        DMA("pool", dsof(tots[1]), rec_t_loc[pg0:pg0 + G, :].rearrange("(o g) h -> o (g h)", o=1), tots[0][0:1, 0:NC8], [tots[1]], [phase] if False else ())
        p32.put(tots)
        for sg in range(G // 4):
            ot = psum.get()
            for q in range(4):
                pg = sg * 4 + q
                p.op("pe", lambda e, ot=ot, q=q, pg=pg, vbt=vbt, wt=wt: e.matmul(
                    ot[0][q * 32:q * 32 + 8, :], lhsT=wt[0][:, pg * 8:(pg + 1) * 8], rhs=vbt[:, pg, :], start=True, stop=True,
                    tile_position=(0, q * 32)), [vbb_, wt[1], phase], [ot[1]])
            msk = p32.get()
            TT("dve", msk[0][:, 0:512], ot[0][:, :], BD128[:], ALU.mult, [ot[1], cb], [msk[1]])
            psum.put(ot)
            o2 = psum.get()
            MM(o2[0][0:4, :], SEL4[:], msk[0][:, 0:512], True, True, [msk[1], cb], [o2[1]])
            p32.put(msk)
            orow = p32.get()
            CP("act", orow[0][0:4, 0:512], o2[0][0:4, :], [o2[1]], [orow[1]])
            psum.put(o2)
            DMA("pool", dsof(orow[1]), rec_o_loc.rearrange("(q h) d -> q (h d)", h=8)[pg0 + sg * 4:pg0 + sg * 4 + 4, :],
                orow[0][0:4, 0:512], [orow[1]], ())
            p32.put(orow)
        p16.put(wt)
    ctx_ = sampA(0)
    for gi in range(NGRP):
        nxt_ = sampA(gi + 1) if gi + 1 < NGRP else None
        sampB(ctx_)
        ctx_ = nxt_
    recb = Buf("rec")

    def allgather(groups, src, dst, first, rb, wb):
        p.dma("pool", d_cc, lambda e: e.collective_compute("AllGather", ALU.bypass, replica_groups=groups, ins=[src],
                                                           outs=[dst]), rb, wb)
        if first:
            for _it in p32.free:
                _d = dsof(_it[1])
                p.ops["pool"][-1].waits.append(("d", _d, _d.unit * _d.cnt))

    CR = CPG * 8
    stage2 = []
    if NCORES == 1:
        DMA("pool", d_misc, rec_o_all, rec_o_loc, [], [recb])
        for _it in p32.free:
            _d = dsof(_it[1])
            p.ops["pool"][-1].waits.append(("d", _d, _d.unit * _d.cnt))
        DMA("pool", d_misc, rec_t_all, rec_t_loc, [], [recb])
    elif NCORES == 8:
        g1 = [[0, 1, 2, 3], [4, 5, 6, 7]]; g2 = [[0, 4], [1, 5], [2, 6], [3, 7]]
        rec_t_half = dint("rec_t_half", [4 * PPC, 8])
        hb = Buf()
        allgather(g1, rec_t_loc, rec_t_half, True, [], [hb])
        stage2.append((g2, rec_t_half, rec_t_all, hb))
        for k in range(NCH):
            half = dint("rec_o_half%d" % k, [4 * CR, 64])
            hb = Buf()
            allgather(g1, rec_o_loc[k * CR:(k + 1) * CR, :], half, False, [], [hb])
            stage2.append((g2, half, rec_o_all[k * NCORES * CR:(k + 1) * NCORES * CR, :], hb))
    else:
        rg = [list(range(NCORES))]
        allgather(rg, rec_t_loc, rec_t_all, True, [], [recb])
        for k in range(NCH):
            allgather(rg, rec_o_loc[k * CR:(k + 1) * CR, :], rec_o_all[k * NCORES * CR:(k + 1) * NCORES * CR, :], False, [], [recb])

    def issue_stage2():
        for (g, src, dst, hb) in stage2:
            allgather(g, src, dst, False, [hb], [recb])
        stage2.clear()

    PGF = p.sb("PGF", [128, DB], F32); RNK = p.sb("RNK", [128, DB], F32); CHK = p.sb("CHK", [128, DB], F32)
    IDXO = p.sb("IDXO", [128, DB], I32)
    ib = Buf("idx")
    CP("dve", PGF[:NPG, :], PTT[:NPG, :], [cb], [ib])
    MS("dve", RNK[:NPG, :], 0.0, [ib])
    MS("dve", CHK[:NPG, :], 0.0, [ib])
    for r in range(1, NCORES):
        STT(RNK[:NPG, :], PGF[:NPG, :], float(r * PPC), RNK[:NPG, :], ALU.is_ge, ALU.add, [ib], [ib])
    STT(PGF[:NPG, :], RNK[:NPG, :], float(-PPC), PGF[:NPG, :], ALU.mult, ALU.add, [ib], [ib])
    for k in range(1, NCH):
        STT(CHK[:NPG, :], PGF[:NPG, :], float(k * CPG), CHK[:NPG, :], ALU.is_ge, ALU.add, [ib], [ib])
    STT(PGF[:NPG, :], CHK[:NPG, :], float((NCORES - 1) * CPG), PGF[:NPG, :], ALU.mult, ALU.add, [ib], [ib])
    STT(PGF[:NPG, :], RNK[:NPG, :], float(CPG), PGF[:NPG, :], ALU.mult, ALU.add, [ib], [ib])
    CP("dve", IDXO[:NPG, :], PGF[:NPG, :], [ib], [ib])
    p.op("pool", lambda e: e.memset(AR16[0:1, 0:2], 0.0), [], [phase])

    if UPTO == 3:
        return fin()
    def prompt_tile(t):
        N = 512
        r0 = t * 512
        for hf in range(2):
            banks = [psum.get() for _ in range(4)]
            for tb in range(4):
                xt = p32.get()
                DMA("sp", dsof(xt[1]), xt[0][:, 0:512], xp[r0 + tb * 128:r0 + (tb + 1) * 128, hf * 512:(hf + 1) * 512], (), [xt[1]])
                for c4 in range(4):
                    TR(banks[c4][0][:, tb * 128:(tb + 1) * 128], xt[0][:, c4 * 128:(c4 + 1) * 128], [xt[1]], [banks[c4][1]])
                p32.put(xt)
            for c4 in range(4):
                CP(evac_eng(), X[:, hf * 4 + c4, :], banks[c4][0][:, :], [banks[c4][1]], [Xb[hf * 4 + c4]])
                psum.put(banks[c4])
        if UPTO == 10:
            raise _Done()
        rmsnorm(X, Xb, lambda c: GMIX[:, 0, c:c + 1], H, Hb, N)
        if UPTO == 11:
            raise _Done()
        U = [p32.get() for _ in range(4)]

        def epi_u(m, bank):
            CP("dve", U[m][0][:, 0:16], HALO[:, m, :], [HALOb], [U[m][1]])
            CP(evac_eng(), U[m][0][:, 16:528], bank[0][:, :], [bank[1]], [U[m][1]])
        dense_fm("in0", 0, 4, lambda c: H[:, c, :], Hb, N, epi_u)
        if UPTO == 111:
            raise _Done()

        def epi_q(m, bank):
            ACT(QT[:, m, :], bank[0][:, :], AF.Copy, [bank[1]], [QTb[m]], scale=0.125)
        dense_fm("in0", 512, 4, lambda c: H[:, c, :], Hb, N, epi_q)
        if UPTO == 112:
            raise _Done()
        view, sbuf = wload("in0", 1024, 512)
        for m in range(4):
            bank = psum.get()
            for c in range(8):
                MM(bank[0][:, :], view[:, c, m * 128:(m + 1) * 128], H[:, c, :], c == 0, c == 7, [sbuf, Hb[c]], [bank[1]])
            CP(evac_eng(), KT[:, m, r0:r0 + 512], bank[0][:, :], [bank[1], phase], [KTb[m][t]])
            psum.put(bank)
        if UPTO == 113:
            raise _Done()
        for tb in range(4):
            bank = psum.get()
            for c in range(8):
                MM(bank[0][:, :], H[:, c, tb * 128:(tb + 1) * 128], view[:, c, :], c == 0, c == 7, [sbuf, Hb[c]], [bank[1]])
            stg = p32.get()
            CP(evac_eng(), stg[0][:, 0:512], bank[0][:, :], [bank[1]], [stg[1]])
            psum.put(bank)
            DMA("pool", dsof(stg[1]), kp[r0 + tb * 128:r0 + (tb + 1) * 128, :], stg[0][:, 0:512], [stg[1]], ())
            p32.put(stg)
        if UPTO == 114:
            raise _Done()
        view, sbuf = wload("in0", 1536, 512)
        for tb in range(4):
            bank = psum.get()
            for c in range(8):
                MM(bank[0][:, :], H[:, c, tb * 128:(tb + 1) * 128], view[:, c, :], c == 0, c == 7, [sbuf, Hb[c]], [bank[1]])
            stg = p32.get()
            CP("act", stg[0][:, 0:512], bank[0][:, :], [bank[1]], [stg[1]])
            psum.put(bank)
            CP("pool", VB[:, t * 4 + tb, :], stg[0][:, 0:512], [stg[1], phase], [VBb[t * 4 + tb]])
            DMA("pool", dsof(stg[1]), vp[r0 + tb * 128:r0 + (tb + 1) * 128, :], stg[0][:, 0:512], [stg[1]], ())
            p32.put(stg)
        if UPTO == 12:
            raise _Done()
        for g in range(4):
            w = 2 << g
            ub = U[g]
            cur = ub
            tmps = []
            for k in range(g):
                sh = 1 << k
                lo = 1 + (2 << k) - 1
                nxt = p32.get()
                tmps.append(nxt)
                TT("pool", nxt[0][:, lo:528], cur[0][:, lo:528], cur[0][:, lo - sh:528 - sh], ALU.add, [cur[1]], [nxt[1]])
                cur = nxt
            sh = 1 << g
            ssum = p32.get()
            TT("pool", ssum[0][:, 0:512], cur[0][:, 16:528], cur[0][:, 16 - sh:528 - sh], ALU.add, [cur[1]], [ssum[1]])
            for tm_ in tmps:
                p32.put(tm_)
            dd = p16.get()
            STT(dd[0][:, :], ssum[0][:, 0:512], 1.0 / w, ub[0][:, 16:528], ALU.mult, ALU.subtract, [ssum[1], ub[1]], [dd[1]])
            if t == 0:
                fx = p32.get()
                TT("dve", fx[0][:, 0:w - 1], ssum[0][:, 0:w - 1], RC0[:, 0:w - 1], ALU.mult, [ssum[1], cb], [fx[1]])
                TT("dve", dd[0][:, 0:w - 1], fx[0][:, 0:w - 1], ub[0][:, 16:16 + w - 1], ALU.subtract, [fx[1], ub[1]], [dd[1]])
                p32.put(fx)
            p32.put(ssum)
            CP("pool", HALO[:, g, 1:16], ub[0][:, 513:528], [ub[1]], [HALOb])
            if t == NT - 1:
                DMA("pool", dsof(ub[1]), poolp[:, g * 128:(g + 1) * 128].rearrange("j c -> c j"), ub[0][:, 513:528], [ub[1]], (), slow=True)
            bank = psum.get()
            MM(bank[0][:, :], WPb[:, g, :], dd[0][:, :], True, True, [dd[1], cb], [bank[1]])
            p16.put(dd)
            TS("dve", MIX[:, g, :], bank[0][:, :], PSC[:, g:g + 1], None, ALU.mult, None, [bank[1], cb], [MIXb[g]])
            psum.put(bank)
            p32.put(ub)
        if UPTO == 13:
            raise _Done()
        nkb = 4 * t + 4
        for hg in range(2):
            av = [psum.get(), psum.get()]
            lacc = [p32.get() for _ in range(4)]
            laccb = [p16.get() for _ in range(4)]
            units = [(kbi, pr_) for kbi in range(nkb) for pr_ in range(2)]

            def P1(u):
                kbi, pr_ = u
                kb = nkb - 1 - kbi
                diag = kb >= 4 * t
                r = kb - 4 * t
                kt_ = kb // 4
                st_ = []
                for hh in (2 * pr_, 2 * pr_ + 1):
                    h = hg * 4 + hh
                    j = h // 2
                    po = (h % 2) * 64
                    kT = KT[po:po + 64, j, kb * 128:(kb + 1) * 128]
                    qT = QT[po:po + 64, j, :]
                    zb = psum.get()
                    MM(zb[0][:, :], kT, qT, True, not diag, [KTb[j][kt_], QTb[j], phase], [zb[1]])
                    if diag:
                        MM(zb[0][:, :], identb[:], negm[:, r, :], False, True, [cb], [zb[1]])
                    et = p32.get()
                    ACT(et[0][:, 0:512], zb[0][:, :], AF.Exp, [zb[1], cb], [et[1]], bias=BIAS[:, h:h + 1])
                    spb = p16.get()
                    ACT(spb[0][:, :], et[0][:, 0:512], AF.Ln, [et[1]], [spb[1]], bias=1.0)
                    p32.put(et)
                    st_.append((hh, h, spb, zb))
                return (kbi, kb, st_)

            def P2a(state):
                kbi, kb, st_ = state
                wts = []
                for (hh, h, spb, b2) in st_:
                    MM(b2[0][:, :], trineg[:], spb[0][:, :], False, kbi == 0, [spb[1], cb], [b2[1]])
                    if kbi > 0:
                        MM(b2[0][:, :], onesneg[:], laccb[hh][0][:, :], False, True, [laccb[hh][1], cb], [b2[1]])
                    wT = p16.get()
                    ACT(wT[0][:, :], b2[0][:, :], AF.Exp, [b2[1], cb], [wT[1]], bias=BIAS[:, h:h + 1])
                    psum.put(b2)
                    wts.append(wT)
                return wts

            def P2b(state, wts):
                kbi, kb, st_ = state
                for (hh, h, spb, b2), wT in zip(st_, wts):
                    MM(av[hh // 2][0][(hh % 2) * 64:(hh % 2) * 64 + 64, :], VB[:, kb, h * 64:(h + 1) * 64], wT[0][:, :],
                       kbi == 0, kbi == nkb - 1, [VBb[kb], wT[1], phase], [av[hh // 2][1]])
                    p16.put(wT)
                    if kbi < nkb - 1:
                        if kbi == 0:
                            CP("dve", lacc[hh][0][:, 0:512], spb[0][:, :], [spb[1]], [lacc[hh][1]])
                        else:
                            TT("dve", lacc[hh][0][:, 0:512], lacc[hh][0][:, 0:512], spb[0][:, :], ALU.add,
                               [lacc[hh][1], spb[1]], [lacc[hh][1]])
                        CP("dve", laccb[hh][0][:, :], lacc[hh][0][:, 0:512], [lacc[hh][1]], [laccb[hh][1]])
                    p16.put(spb)

            states = {}
            for i in range(min(2, len(units))):
                states[i] = P1(units[i])
            for i in range(len(units)):
                wts = P2a(states[i])
                if i + 2 < len(units):
                    states[i + 2] = P1(units[i + 2])
                P2b(states[i], wts)
                del states[i]
            for k2 in range(2):
                CP("dve", MIX[:, 4 + hg * 2 + k2, :], av[k2][0][:, :], [av[k2][1]], [MIXb[4 + hg * 2 + k2]])
                psum.put(av[k2])
            for it in lacc:
                p32.put(it)
            for it in laccb:
                p16.put(it)
        if UPTO == 14:
            raise _Done()
        if DBG and t == NT - 1:
            DMA("sp", d_misc, dbg_mix0, MIX[:].rearrange("p c n -> p (c n)"), MIXb, ())
        outproj("out0", MIX, MIXb, X, Xb, N)
        if UPTO == 15:
            raise _Done()
        ffn(0, X, Xb, H, Hb, N, GF, GFb, 2)
        if UPTO == 16:
            raise _Done()
        rmsnorm(X, Xb, lambda c: GMIX[:, 1, c:c + 1], H, Hb, N)
        UG = [p32.get() for _ in range(4)]

        def epi_ug(m, bank):
            ACT(UG[m][0][:, 0:512], bank[0][:, :], AF.Gelu, [bank[1]], [UG[m][1]])
        dense_fm("in1", 0, 4, lambda c: H[:, c, :], Hb, N, epi_ug)
        VT = [p16.get() for _ in range(4)]
        view, sbuf = wload("in1", 512, 512)
        for tb in range(4):
            bank = psum.get()
            for c in range(8):
                MM(bank[0][:, :], H[:, c, tb * 128:(tb + 1) * 128], view[:, c, :], c == 0, c == 7, [sbuf, Hb[c]], [bank[1]])
            ACT(VT[tb][0][:, :], bank[0][:, :], AF.Gelu, [bank[1]], [VT[tb][1]])
            psum.put(bank)
        for g in range(4):
            bank = psum.get()
            for tb in range(4):
                MM(bank[0][:, tb * 128:(tb + 1) * 128], VT[tb][0][:, g * 128:(g + 1) * 128], WMT[:, g, :], tb == 0, False,
                   [VT[tb][1], cb], [bank[1]])
            for tb in range(4):
                MM(bank[0][:, tb * 128:(tb + 1) * 128], ones2[0:1, :], BSH[0:1, g, :], False, False, [cb], [bank[1]])
                MM(bank[0][:, tb * 128:(tb + 1) * 128], ones2[0:1, :], BSL[0:1, g, :], False, True, [cb], [bank[1]])
            TT("dve", MIX[:, g, :], bank[0][:, :], UG[g][0][:, 0:512], ALU.mult, [bank[1], UG[g][1]], [MIXb[g]])
            psum.put(bank)
        for it in VT:
            p16.put(it)
        for it in UG:
            p32.put(it)
        ZZ = [p32.get() for _ in range(4)]

        def epi_hh(m, bank):
            CP(evac_eng(), ZZ[m][0][:, 2:514], bank[0][:, :], [bank[1]], [ZZ[m][1]])
        dense_fm("in1", 1024, 4, lambda c: H[:, c, :], Hb, N, epi_hh)
        BG = [p32.get() for _ in range(4)]

        def epi_bg(m, bank):
            CP(evac_eng(), BG[m][0][:, 0:512], bank[0][:, :], [bank[1]], [BG[m][1]])
        dense_fm("in1", 1536, 4, lambda c: H[:, c, :], Hb, N, epi_bg)

        def epi_cg(m, bank):
            TT("dve", ZZ[m][0][:, 2:514], bank[0][:, :], ZZ[m][0][:, 2:514], ALU.mult, [bank[1], ZZ[m][1]], [ZZ[m][1]])
            CP("pool", ZZ[m][0][:, 0:2], ZHALO[:, m, :], [ZHALOb], [ZZ[m][1]])
            cv_ = p32.get()
            TS("pool", cv_[0][:, 0:512], ZZ[m][0][:, 0:512], CW[:, 0, m:m + 1], None, ALU.mult, None, [ZZ[m][1], cb], [cv_[1]])
            STT(cv_[0][:, 0:512], ZZ[m][0][:, 1:513], CW[:, 1, m:m + 1], cv_[0][:, 0:512], ALU.mult, ALU.add,
                [ZZ[m][1], cv_[1], cb], [cv_[1]])
            STT(cv_[0][:, 0:512], ZZ[m][0][:, 2:514], CW[:, 2, m:m + 1], cv_[0][:, 0:512], ALU.mult, ALU.add,
                [ZZ[m][1], cv_[1], cb], [cv_[1]])
            TT("pool", MIX[:, 4 + m, :], cv_[0][:, 0:512], BG[m][0][:, 0:512], ALU.mult, [cv_[1], BG[m][1]], [MIXb[4 + m]])
            p32.put(cv_)
            CP("pool", ZHALO[:, m, :], ZZ[m][0][:, 512:514], [ZZ[m][1]], [ZHALOb])
            if t == NT - 1:
                DMA("pool", dsof(ZZ[m][1]), convp[:, m * 128:(m + 1) * 128].rearrange("j c -> c j"), ZZ[m][0][:, 512:514], [ZZ[m][1]], (), slow=True)
        dense_fm("in1", 2048, 4, lambda c: H[:, c, :], Hb, N, epi_cg)
        for it in ZZ:
            p32.put(it)
        for it in BG:
            p32.put(it)
        if UPTO == 17:
            raise _Done()
        if DBG and t == NT - 1:
            DMA("sp", d_misc, dbg_mix, MIX[:].rearrange("p c n -> p (c n)"), MIXb, ())
            DMA("sp", d_misc, dbg_wmt, WMT[:].rearrange("p g t -> p (g t)"), [cb], ())
        outproj("out1", MIX, MIXb, X, Xb, N)
        ffn(1, X, Xb, H, Hb, N, GF, GFb, 2)
        if UPTO == 18:
            raise _Done()
        rstd = rmsnorm(X, Xb, None, None, None, N, out32=True)
        for hf in range(2):
            banks = [psum.get() for _ in range(4)]
            for c4 in range(4):
                c = hf * 4 + c4
                yn = p32.get()
                STT(yn[0][:, 0:512], X[:, c, :], GFIN[:, c:c + 1], rstd[0][:, 0:512], ALU.mult, ALU.mult,
                    [Xb[c], rstd[1], cb], [yn[1]])
                for tb in range(4):
                    TR(banks[tb][0][:, c4 * 128:(c4 + 1) * 128], yn[0][:, tb * 128:(tb + 1) * 128], [yn[1]], [banks[tb][1]])
                p32.put(yn)
            for tb in range(4):
                stg = p32.get()
                CP(evac_eng(), stg[0][:, 0:512], banks[tb][0][:, :], [banks[tb][1]], [stg[1]])
                psum.put(banks[tb])
                DMA("pool", dsof(stg[1]), yp[r0 + tb * 128:r0 + (tb + 1) * 128, hf * 512:(hf + 1) * 512], stg[0][:, 0:512], [stg[1]], ())
                p32.put(stg)
        p32.put(rstd)

    try:
        for t in range(NT):
            prompt_tile(t)
            if t == 0:
                issue_stage2()
    except _Done:
        return fin()
    if UPTO == 30:
        return fin()

    N = OWN
    for b in range(OWN):
        R = p32.get(); Tt = p32.get()
        p.dma("pool", dsof(R[1]), lambda e, R=R, b=b: e.indirect_dma_start(
            out=R[0][:NPG, 0:512], out_offset=None, in_=rec_o_all.rearrange("(q h) d -> q (h d)", h=8),
            in_offset=bass.IndirectOffsetOnAxis(ap=IDXO[:NPG, b:b + 1], axis=0)), [recb, cb, ib], [R[1]])
        p.dma("pool", dsof(Tt[1]), lambda e, Tt=Tt, b=b: e.indirect_dma_start(
            out=Tt[0][:NPG, 0:8], out_offset=None, in_=rec_t_all,
            in_offset=bass.IndirectOffsetOnAxis(ap=PTT[:NPG, b:b + 1], axis=0)), [recb, cb], [Tt[1]])
        sfx = psum.get()
        MM(sfx[0][:NPG, 0:8], tripos[:NPG, :NPG], Tt[0][:NPG, 0:8], True, True, [Tt[1], cb], [sfx[1]])
        cc_ = p32.get()
        ACT(cc_[0][:NPG, 0:8], sfx[0][:NPG, 0:8], AF.Exp, [sfx[1]], [cc_[1]])
        psum.put(sfx)
        TT("dve", R[0][:NPG, 0:512].rearrange("p (h d) -> p h d", h=8), R[0][:NPG, 0:512].rearrange("p (h d) -> p h d", h=8),
           cc_[0][:NPG, 0:8].unsqueeze(2).to_broadcast([NPG, 8, 64]), ALU.mult, [R[1], cc_[1]], [R[1]])
        p32.put(cc_); p32.put(Tt)
        att = psum.get()
        for j in range(4):
            MM(att[0][:, j:j + 1], R[0][:NPG, j * 128:(j + 1) * 128], ones32[:NPG, 0:1], True, True, [R[1], cb], [att[1]])
        for j in range(4):
            CP("dve", MIXs[:, 4 + j, b:b + 1], att[0][:, j:j + 1], [att[1]], [MIXsb[4 + j]])
        psum.put(att)
        p32.put(R)
    outproj("out0", MIXs, MIXsb, Xs, Xsb, N)
    ffn(0, Xs, Xsb, Hs, Hsb, N, GFs, GFsb, 1)
    rmsnorm(Xs, Xsb, lambda c: GMIX[:, 1, c:c + 1], Hs, Hsb, N)
    for pc in range(5):
        view, sbuf = wload("in1", pc * 512, 512)
        bank = psum.get()
        for c in range(8):
            MM(bank[0][0:N, :], Hs[:, c, 0:N], view[:, c, :], c == 0, c == 7, [sbuf, Hsb[c]], [bank[1]])
        if pc < 2:
            ACT(TOKS[0:N, pc * 512:(pc + 1) * 512], bank[0][0:N, :], AF.Gelu, [bank[1]], [TOKSb])
        else:
            CP("dve", TOKS[0:N, pc * 512:(pc + 1) * 512], bank[0][0:N, :], [bank[1]], [TOKSb])
        psum.put(bank)
    DMA("pool", d_toks, chv, TOKS[0:N, 512:1024], [TOKSb], ())
    TT("dve", TOKS[0:N, 1024:1536], TOKS[0:N, 1024:1536], TOKS[0:N, 2048:2560], ALU.mult, [TOKSb], [TOKSb])
    DMA("pool", d_toks, convs[:, 1, :], TOKS[0:N, 1024:1536], [TOKSb], ())
    DMA("pool", d_misc, convs[:, 0, :], scst[:, 1, :], (), ())
    FM = p32.get()
    for sec in range(5):
        view, sbuf = wload("in1", sec * 512, 512)
        bank = psum.get()
        for g in range(4):
            for c in range(8):
                MM(bank[0][:, g * 4:g * 4 + N], view[:, c, g * 128:(g + 1) * 128], Hs[:, c, 0:N], c == 0, c == 7,
                   [sbuf, Hsb[c]], [bank[1]])
        if sec < 2:
            ACT(FM[0][:, sec * 16:sec * 16 + 16], bank[0][:, 0:16], AF.Gelu, [bank[1]], [FM[1]])
        else:
            CP("dve", FM[0][:, sec * 16:sec * 16 + 16], bank[0][:, 0:16], [bank[1]], [FM[1]])
        psum.put(bank)
    WS0 = p32.get()
    for g in range(4):
        DMA("sp", dsof(WS0[1]), WS0[0][:, g:g + 1], w_s[g * 128:g * 128 + 1, 0:1].to_broadcast([128, 1]), (), [WS0[1]])
        DMA("sp", dsof(WS0[1]), WS0[0][:, 4 + g:5 + g], b_s[g:g + 1, 0:1].to_broadcast([128, 1]), (), [WS0[1]])
    for g in range(4):
        tmpg = p32.get()
        TS("dve", tmpg[0][:, 0:N], FM[0][:, 16 + g * 4:16 + g * 4 + N], WS0[0][:, g:g + 1], WS0[0][:, 4 + g:5 + g], ALU.mult, ALU.add,
           [FM[1], WS0[1]], [tmpg[1]])
        TT("dve", MIXs[:, g, 0:N], tmpg[0][:, 0:N], FM[0][:, g * 4:g * 4 + N], ALU.mult, [tmpg[1], FM[1]], [MIXsb[g]])
        p32.put(tmpg)
    p32.put(WS0)
    cst = p32.get()
    DMA("sp", dsof(cst[1]), cst[0][0:OWN * 2, 0:512], scst.rearrange("b j c -> (b j) c"), (), [cst[1]])
    bank = psum.get()
    for g in range(4):
        TR(bank[0][:, g * 8:g * 8 + OWN * 2], cst[0][0:OWN * 2, g * 128:(g + 1) * 128], [cst[1]], [bank[1]])
    CST = p32.get()
    CP("dve", CST[0][:, 0:32], bank[0][:, 0:32], [bank[1]], [CST[1]])
    psum.put(bank); p32.put(cst)
    for g in range(4):
        zf = p32.get(); cvv = p32.get()
        TT("dve", zf[0][:, 0:N], FM[0][:, 32 + g * 4:32 + g * 4 + N], FM[0][:, 64 + g * 4:64 + g * 4 + N], ALU.mult, [FM[1]], [zf[1]])
        st3 = CST[0][:, g * 8:g * 8 + 8].rearrange("p (b j) -> p b j", j=2)
        TS("dve", cvv[0][:, 0:N], st3[:, :, 0], CW[:, 0, g:g + 1], None, ALU.mult, None, [CST[1], cb], [cvv[1]])
        STT(cvv[0][:, 0:N], st3[:, :, 1], CW[:, 1, g:g + 1], cvv[0][:, 0:N], ALU.mult, ALU.add, [CST[1], cvv[1], cb], [cvv[1]])
        STT(cvv[0][:, 0:N], zf[0][:, 0:N], CW[:, 2, g:g + 1], cvv[0][:, 0:N], ALU.mult, ALU.add, [zf[1], cvv[1], cb], [cvv[1]])
        TT("dve", MIXs[:, 4 + g, 0:N], cvv[0][:, 0:N], FM[0][:, 48 + g * 4:48 + g * 4 + N], ALU.mult, [cvv[1], FM[1]], [MIXsb[4 + g]])
        p32.put(zf); p32.put(cvv)
    p32.put(CST); p32.put(FM)
    outproj("out1", MIXs, MIXsb, Xs, Xsb, N)
    ffn(1, Xs, Xsb, Hs, Hsb, N, GFs, GFsb, 1)
    rstd = rmsnorm(Xs, Xsb, None, None, None, N, out32=True)
    for hf in range(2):
        bank = psum.get()
        for c4 in range(4):
            c = hf * 4 + c4
            yn = p32.get()
            STT(yn[0][:, 0:N], Xs[:, c, 0:N], GFIN[:, c:c + 1], rstd[0][:, 0:N], ALU.mult, ALU.mult, [Xsb[c], rstd[1], cb], [yn[1]])
            TR(bank[0][0:N, c4 * 128:(c4 + 1) * 128], yn[0][:, 0:N], [yn[1]], [bank[1]])
            p32.put(yn)
        stg = p32.get()
        CP("dve", stg[0][0:N, 0:512], bank[0][0:N, :], [bank[1]], [stg[1]])
        psum.put(bank)
        DMA("pool", dsof(stg[1]), ys[:, hf * 512:(hf + 1) * 512], stg[0][0:N, 0:512], [stg[1]], ())
        p32.put(stg)
    p32.put(rstd)
    p.wait_all_dma("sp")
    p.emit()
    return nc


def make_in_maps(cfg, inputs):
    NCORES = cfg["NCORES"]; S = cfg["S"]; DB = cfg["DB"]; NPG = cfg["NPG"]; PPC = cfg["PPC"]
    OWN = DB // NCORES
    f = lambda a: np.ascontiguousarray(np.asarray(a))
    x_prompt = f(inputs["x_prompt"]); x_sample = f(inputs["x_sample"])[:, 0, :]
    cache_k = np.asarray(inputs["cache_k"])[0].reshape(-1, 128 * 512)
    cache_v = np.asarray(inputs["cache_v"])[0].reshape(-1, 128 * 512)
    state_pool = f(inputs["state_pool"])[0]; state_conv = f(inputs["state_conv"])[0]
    pt = f(inputs["page_table"]).astype(np.int32)
    common = dict(
        norm_mix=f(inputs["norm_mix"]), norm_ffn=f(inputs["norm_ffn"]), norm_final=f(inputs["norm_final"]).reshape(1, D),
        w_in0=f(inputs["ab_w_in"])[0], sb_bias=f(inputs["ab_sb_bias"]).reshape(1, 8), w_pool=f(inputs["ab_w_pool"])[0].reshape(512, 128),
        pool_scale=f(inputs["ab_pool_scale"]).reshape(1, 512), w_out0=f(inputs["ab_w_out"])[0], w_in1=f(inputs["cd_w_in"])[0],
        w_s=f(inputs["cd_w_s"])[0].reshape(512, 128), b_s=f(inputs["cd_b_s"])[0], conv_w=f(inputs["cd_conv_w"])[0],
        w_out1=f(inputs["cd_w_out"])[0])
    for l in range(2):
        common["w_gate%d" % l] = f(inputs["ffn_w_gate"])[l]
        common["w_up%d" % l] = f(inputs["ffn_w_up"])[l]
        common["w_down%d" % l] = f(inputs["ffn_w_down"])[l]
    maps = []
    for c in range(NCORES):
        order = np.roll(np.arange(DB), -c * OWN)
        m = dict(common)
        m["xp"] = x_prompt[c]
        m["xs"] = f(x_sample[order])
        NSPLIT = cfg.get("NSPLIT", 1); PPS = PPC // NSPLIT
        for i in range(NSPLIT):
            m["ck%d" % i] = f(cache_k[c * PPC + i * PPS:c * PPC + (i + 1) * PPS]).reshape(PPS * 128, 512)
            m["cv%d" % i] = f(cache_v[c * PPC + i * PPS:c * PPC + (i + 1) * PPS]).reshape(PPS * 128, 512)
        m["spst"] = f(state_pool[order[:OWN]]); m["scst"] = f(state_conv[order[:OWN]])
        m["ptT"] = f(pt[order].T)
        m["pbase"] = np.full((128, 1), float(c * PPC), np.float32)
        maps.append(m)
    return maps


def gather_outputs(cfg, res):
    NCORES = cfg["NCORES"]; S = cfg["S"]; DB = cfg["DB"]
    OWN = DB // NCORES
    R = res
    cat = lambda k: np.stack([r[k] for r in R])
    y_prompt = cat("yp")
    y_sample = np.concatenate([r["ys"] for r in R], 0).reshape(DB, 1, D)
    k_prompt = cat("kp").reshape(1, NCORES, S, 8, 64); v_prompt = cat("vp").reshape(1, NCORES, S, 8, 64)
    k_sample = np.concatenate([r["ksm"] for r in R], 0).reshape(1, DB, 1, 8, 64)
    v_sample = np.concatenate([r["vsm"] for r in R], 0).reshape(1, DB, 1, 8, 64)
    pool_prompt = cat("poolp")[None]; pool_sample = np.concatenate([r["pools"] for r in R], 0)[None]
    conv_prompt = cat("convp")[None]; conv_sample = np.concatenate([r["convs"] for r in R], 0)[None]
    chunk_v = np.concatenate([r["chv"] for r in R], 0).reshape(1, DB, 1, 512)
    return tuple(np.ascontiguousarray(a.astype(np.float32)) for a in
                 (y_prompt, y_sample, k_prompt, v_prompt, k_sample, v_sample, pool_prompt, pool_sample, conv_prompt,
                  conv_sample, chunk_v))


def kernel(**inputs):
    cfg = default_cfg()
    nc = build(cfg)
    maps = make_in_maps(cfg, inputs)
    res = run_bass_kernel_spmd(nc, maps, core_ids=list(range(cfg["NCORES"])))
    return gather_outputs(cfg, res.results)
```
